# Optimizing a Trainium2 kernel written in Bass

```python
import math
import jax, jax.numpy as jnp
from jax import lax
import numpy as np

D_MODEL = 2048
BATCH = 4
SEQ = 2048
DEPTH = 4

N_A_LAYERS = DEPTH // 2
N_B_LAYERS = DEPTH - N_A_LAYERS
CHUNK = 128
A_GROUPS = 8
A_HALF = D_MODEL
HEAD_DIM = 64
N_HEADS = D_MODEL // HEAD_DIM
N_KV_HEADS = N_HEADS // 8
WINDOW = 128
BLOCK = WINDOW
N_BUCKETS = 32
MAX_DISTANCE = 128
D_FF = ((8 * D_MODEL // 3 + 255) // 256) * 256
RMS_EPS = 1e-5
NEG_INF = -1e30

kernel_name = "yoco_gmlp_swa_sink_hybrid"


def rmsnorm(x, g):
    x32 = x.astype(jnp.float32)
    y = x32 * lax.rsqrt(jnp.mean(x32 * x32, axis=-1, keepdims=True) + RMS_EPS)
    return (y * g.astype(jnp.float32)).astype(x.dtype)


def swiglu(x, w_gate, w_up, w_down):
    return (jax.nn.silu(x @ w_gate) * (x @ w_up)) @ w_down


def gmlp_mixer(xn, w_in, norm_v, w_s, b_s, w_out):
    B, S, _ = xn.shape
    n_chunks = S // CHUNK
    z = jax.nn.gelu(xn @ w_in)
    u, v = jnp.split(z, 2, axis=-1)
    v = rmsnorm(v, norm_v)
    v = v.reshape(B, n_chunks, CHUNK, A_GROUPS, A_HALF // A_GROUPS)
    w_causal = jnp.tril(w_s)
    s = jnp.einsum('gts,bcsgd->bctgd', w_causal, v) + b_s.T[None, None, :, :, None]
    gated = u * s.reshape(B, S, A_HALF)
    return gated @ w_out


def t5_bucket(dist):
    max_exact = N_BUCKETS // 2
    is_small = dist < max_exact
    d = jnp.maximum(dist, 1).astype(jnp.float32)
    large = max_exact + (jnp.log(d / max_exact) / math.log(MAX_DISTANCE / max_exact)
                         * (N_BUCKETS - max_exact)).astype(jnp.int32)
    large = jnp.minimum(large, N_BUCKETS - 1)
    return jnp.where(is_small, dist, large)


def banded_sink_attention(q, k, v, sinks, rel_bias):
    B, S = q.shape[0], q.shape[1]
    nb = S // BLOCK
    grp = N_HEADS // N_KV_HEADS
    qb = q.reshape(B, nb, BLOCK, N_KV_HEADS, grp, HEAD_DIM).astype(jnp.float32)

    def band(t):
        tp = jnp.pad(t, ((0, 0), (BLOCK, 0), (0, 0), (0, 0)))
        prev = tp[:, :S].reshape(B, nb, BLOCK, N_KV_HEADS, HEAD_DIM)
        cur = t.reshape(B, nb, BLOCK, N_KV_HEADS, HEAD_DIM)
        return jnp.concatenate([prev, cur], axis=2)

    kb = band(k).astype(jnp.float32)
    vb = band(v)
    scores = jnp.einsum('bcqhgd,bckhd->bchgqk', qb, kb) / math.sqrt(HEAD_DIM)

    dist = np.arange(BLOCK)[:, None] + BLOCK - np.arange(2 * BLOCK)[None, :]
    in_window = (dist >= 0) & (dist < WINDOW)
    bucket = t5_bucket(jnp.asarray(np.clip(dist, 0, None), dtype=jnp.int32))
    bias = rel_bias[bucket].astype(jnp.float32)
    bias = bias.transpose(2, 0, 1).reshape(N_KV_HEADS, grp, BLOCK, 2 * BLOCK)
    key_exists = (np.arange(nb)[:, None] * BLOCK - BLOCK + np.arange(2 * BLOCK)[None, :]) >= 0
    mask = in_window[None] & key_exists[:, None, :]
    mask = jnp.asarray(mask)[None, :, None, None]
    scores = jnp.where(mask, scores + bias, NEG_INF)

    sink = sinks.astype(jnp.float32).reshape(N_KV_HEADS, grp)[None, None, :, :, None, None]
    m = jnp.maximum(scores.max(axis=-1, keepdims=True), sink)
    p = jnp.exp(scores - m)
    denom = p.sum(axis=-1, keepdims=True) + jnp.exp(sink - m)
    probs = (p / denom).astype(v.dtype)
    out = jnp.einsum('bchgqk,bckhd->bcqhgd', probs, vb)
    return out.reshape(B, S, N_HEADS * HEAD_DIM)


def setup_inputs(seed: int = 0) -> dict:
    key = jax.random.key(seed)
    ks = jax.random.split(key, 24)
    f32 = jnp.float32
    out_scale = (2.0 * DEPTH) ** -0.5

    def nrm(k, shape, scale):
        return jax.random.normal(k, shape, f32) * scale

    kv_dim = 2 * N_KV_HEADS * HEAD_DIM
    return {
        "x": nrm(ks[0], (BATCH, SEQ, D_MODEL), 1.0),
        "mix_norm": 1.0 + nrm(ks[1], (DEPTH, D_MODEL), 0.1),
        "ffn_norm": 1.0 + nrm(ks[2], (DEPTH, D_MODEL), 0.1),
        "a_w_in": nrm(ks[3], (N_A_LAYERS, D_MODEL, 2 * A_HALF), D_MODEL ** -0.5),
        "a_norm_v": 1.0 + nrm(ks[4], (N_A_LAYERS, A_HALF), 0.1),
        "a_w_s": nrm(ks[5], (N_A_LAYERS, A_GROUPS, CHUNK, CHUNK), CHUNK ** -0.5),
        "a_b_s": 1.0 + nrm(ks[6], (N_A_LAYERS, A_GROUPS, CHUNK), 0.1),
        "a_w_out": nrm(ks[7], (N_A_LAYERS, A_HALF, D_MODEL), A_HALF ** -0.5 * out_scale),
        "kv_norm": 1.0 + nrm(ks[8], (D_MODEL,), 0.1),
        "w_kv": nrm(ks[9], (D_MODEL, kv_dim), D_MODEL ** -0.5),
        "b_kv": nrm(ks[10], (kv_dim,), 0.02),
        "b_w_q": nrm(ks[11], (N_B_LAYERS, D_MODEL, N_HEADS * HEAD_DIM), D_MODEL ** -0.5),
        "b_b_q": nrm(ks[12], (N_B_LAYERS, N_HEADS * HEAD_DIM), 0.02),
        "b_sinks": nrm(ks[13], (N_B_LAYERS, N_HEADS), 1.0),
        "b_w_o": nrm(ks[14], (N_B_LAYERS, N_HEADS * HEAD_DIM, D_MODEL), (N_HEADS * HEAD_DIM) ** -0.5 * out_scale),
        "b_b_o": nrm(ks[15], (N_B_LAYERS, D_MODEL), 0.02),
        "rel_bias": nrm(ks[16], (N_BUCKETS, N_HEADS), 0.5),
        "ffn_w_gate": nrm(ks[17], (DEPTH, D_MODEL, D_FF), D_MODEL ** -0.5),
        "ffn_w_up": nrm(ks[18], (DEPTH, D_MODEL, D_FF), D_MODEL ** -0.5),
        "ffn_w_down": nrm(ks[19], (DEPTH, D_FF, D_MODEL), D_FF ** -0.5 * out_scale),
        "final_norm": 1.0 + nrm(ks[20], (D_MODEL,), 0.1),
    }


def reference(x, mix_norm, ffn_norm, a_w_in, a_norm_v, a_w_s, a_b_s, a_w_out,
              kv_norm, w_kv, b_kv, b_w_q, b_b_q, b_sinks, b_w_o, b_b_o, rel_bias,
              ffn_w_gate, ffn_w_up, ffn_w_down, final_norm):
    B, S, _ = x.shape
    h = x
    k_shared = None
    v_shared = None
    for layer in range(DEPTH):
        xn = rmsnorm(h, mix_norm[layer])
        if layer < N_A_LAYERS:
            i = layer
            h = h + gmlp_mixer(xn, a_w_in[i], a_norm_v[i], a_w_s[i], a_b_s[i], a_w_out[i])
        else:
            i = layer - N_A_LAYERS
            q = (xn @ b_w_q[i] + b_b_q[i]).reshape(B, S, N_HEADS, HEAD_DIM)
            attn = banded_sink_attention(q, k_shared, v_shared, b_sinks[i], rel_bias)
            h = h + attn @ b_w_o[i] + b_b_o[i]
        h = h + swiglu(rmsnorm(h, ffn_norm[layer]), ffn_w_gate[layer], ffn_w_up[layer], ffn_w_down[layer])
        if layer == N_A_LAYERS - 1:
            kv = rmsnorm(h, kv_norm) @ w_kv + b_kv
            k_flat, v_flat = jnp.split(kv, 2, axis=-1)
            k_shared = k_flat.reshape(B, S, N_KV_HEADS, HEAD_DIM)
            v_shared = v_flat.reshape(B, S, N_KV_HEADS, HEAD_DIM)
    return rmsnorm(h, final_norm)
```

```python
import math
from contextlib import ExitStack

import numpy as np
import concourse.bass as bass
import concourse.mybir as mybir
from concourse.bass_utils import run_bass_kernel_spmd

F32 = mybir.dt.float32
BF16 = mybir.dt.bfloat16
AF = mybir.ActivationFunctionType
ALU = mybir.AluOpType
AX = mybir.AxisListType

D = 2048
KC = 16
DFF = 5632
NSC = 11
EPS = 1e-5
NMAIN = 1024
NHALO = 128
NSLOT = 4
MASKV = -30000.0

C_MIX = 0
C_FFN = 64
C_KV = 128
C_FIN = 144
C_NV = 160
C_BQ = 192
C_BO = 224
C_BK = 256
NCOL = 260

GELU_C = math.sqrt(2.0 / math.pi)


class Grp:
    def __init__(self, kind, t0, n):
        self.kind, self.t0, self.n = kind, t0, n
        self.a0 = t0 if kind == "h" else NHALO + t0
        self.tiles = [self.a0 // 128 + i for i in range(n // 128)]


G01 = [Grp("h", 0, 128), Grp("m", 0, 512), Grp("m", 512, 512)]
G23 = [Grp("m", 0, 512), Grp("m", 512, 512)]


class Plan:
    ENG = ("pe", "act", "dve", "pool", "sp")

    def __init__(self, nc, stack):
        self.nc, self.stack = nc, stack
        self.sems = []
        self.lists = {e: [] for e in self.ENG}
        self.waited = {e: {} for e in self.ENG}
        self.cnt = {}
        self.esem = {}
        for e in self.ENG:
            self.esem[e] = self.new_sem("s_" + e)
            self.cnt[e] = 0

    def new_sem(self, name):
        s = self.stack.enter_context(self.nc.semaphore(name))
        self.sems.append(s)
        return len(self.sems) - 1

    def wait(self, eng, tok):
        if tok is None:
            return
        si, val = tok
        if eng == "pe" and si == self.esem["pe"]:
            return
        if self.waited[eng].get(si, 0) >= val:
            return
        self.waited[eng][si] = val
        self.lists[eng].append(("w", si, val))

    def op(self, eng, fn, deps=(), ms=True, inc=None):
        for d in deps:
            if isinstance(d, list):
                for dd in d:
                    self.wait(eng, dd)
            else:
                self.wait(eng, d)
        if inc is not None:
            self.lists[eng].append(("o", fn, inc[0], inc[1]))
            return None
        if ms:
            self.cnt[eng] += 1
            self.lists[eng].append(("o", fn, self.esem[eng], 1))
            return (self.esem[eng], self.cnt[eng])
        self.lists[eng].append(("o", fn, None, 0))
        return None

    def last(self, eng):
        return (self.esem[eng], self.cnt[eng]) if self.cnt[eng] else None

    def fence(self):
        return [self.last("pe"), self.last("act"), self.last("dve")]

    def replay(self, eng, e):
        for it in self.lists[eng]:
            if it[0] == "w":
                e.wait_ge(self.sems[it[1]], it[2])
            else:
                ins = it[1](e)
                if it[2] is not None:
                    ins.then_inc(self.sems[it[2]], it[3])


class Pool:
    def __init__(self, aps):
        self.aps = aps
        self.free = [None] * len(aps)
        self.nxt = 0

    def get(self):
        i = self.nxt
        self.nxt = (i + 1) % len(self.aps)
        return i, self.aps[i], self.free[i]

    def rel(self, i, tok):
        self.free[i] = tok


def build(layers, is_first, is_last, dbg=()):
    layers = tuple(layers)
    do_g = any(l < 2 for l in layers)
    do_a = any(l >= 2 for l in layers)
    nc = bass.Bass("TRN2", target_bir_lowering=False)

    def dram(name, shape, kind="ExternalInput"):
        return nc.dram_tensor(name, list(shape), F32, kind=kind).ap()

    xin = dram("xin", [NHALO + NMAIN, D])
    n_out = NMAIN if is_last else NHALO + NMAIN
    out = dram("out", [n_out, D], kind="ExternalOutput")
    cols_d = dram("cols", [128, NCOL])
    cst_d = dram("cst", [128, 512])
    wg_d = dram("ffn_w_gate", [4, D, DFF])
    wu_d = dram("ffn_w_up", [4, D, DFF])
    wd_d = dram("ffn_w_down", [4, DFF, D])
    if do_g:
        win_d = dram("a_w_in", [2, D, 2 * D])
        wst_d = dram("a_w_sT", [2, 128, 8, 128])
        bs_d = dram("a_b_s", [2, 8, 128])
        wout_d = dram("a_w_out", [2, D, D])
    if do_a:
        wkv_d = dram("w_kv", [D, 512])
        bkv_d = dram("b_kv", [512])
        wq_d = dram("b_w_q", [2, D, D])
        wo_d = dram("b_w_o", [2, D, D])
        sink_d = dram("b_sinks", [2, 32])
        biasg_d = dram("biasg", [128, 32, 256])
        mask0_d = dram("mask0", [128, 256])

    with ExitStack() as st:
        pl = Plan(nc, st)

        def sb(name, shape, dt):
            return st.enter_context(nc.sbuf_tensor(name, list(shape), dt))

        hT_m = sb("hT_m", [128, KC, NMAIN], F32)
        hT_h = sb("hT_h", [128, KC, NHALO], F32)
        xn_m = sb("xn_m", [128, KC, NMAIN], BF16)
        xn_h = sb("xn_h", [128, KC, NHALO], BF16)
        ring_t = [sb(f"ring{i}", [128, 4096], BF16) for i in range(NSLOT)]
        cols = sb("cols_sb", [128, NCOL], F32)
        cst = sb("cst_sb", [128, 512], F32)
        ident_b = sb("ident_b", [128, 128], BF16)
        ones_f = sb("ones_f", [128, 128], F32)
        small = sb("small", [128, 256], F32)
        A = sb("arenaA", [128, 18432], BF16)
        Bq = sb("arenaB", [128, 4608], BF16)
        Vt = sb("Vt", [128, 9, 256], BF16)
        T = sb("arenaT", [128, 6144], BF16)
        ps = st.enter_context(nc.psum_tensor("ps", [128, 8, 512], F32))
        ident_f = cst[:, 0:128]
        triu = cst[:, 128:256]
        maskadd = cst[:, 256:512]

        def psf(b):
            return ps[:, b, :]

        def psb(b):
            return ps[:, b, :].bitcast(BF16)

        def f32view(t, off, n):
            return t[:, off:off + n].bitcast(F32)

        tbuf = [f32view(T, i * 1024, 1024) for i in range(6)]
        sqpool = Pool(tbuf[0:2])
        rspool = Pool(tbuf[2:3])
        tApool = Pool(tbuf[3:5] + tbuf[5:6])
        banks_free = [None] * 8

        class Banks:
            def __init__(self, ids):
                self.ids, self.nxt = ids, 0

            def get(self):
                b = self.ids[self.nxt]
                self.nxt = (self.nxt + 1) % len(self.ids)
                return b, banks_free[b]

            @staticmethod
            def rel(b, tok):
                banks_free[b] = tok

        banks = Banks(list(range(8)))

        sp_sems = [pl.new_sem(f"sp{i}") for i in range(8)]
        sp_n = [0]

        def sp_dma(out_ap, in_ap, deps=()):
            i = sp_n[0]
            sp_n[0] += 1
            si = sp_sems[i % 8]
            k = i // 8
            prev = (si, 16 * k) if k > 0 else None
            pl.op("sp", lambda e: e.dma_start(out=out_ap, in_=in_ap), deps=[prev] + list(deps), inc=(si, 16))
            return (si, 16 * (k + 1))

        ring_sem = [pl.new_sem(f"rg{i}") for i in range(NSLOT)]
        ring_fills = [0] * NSLOT
        ring_free = [None] * NSLOT
        ring_nxt = [0]

        def ring_load(dmas):
            s = ring_nxt[0]
            ring_nxt[0] = (s + 1) % NSLOT
            t = ring_t[s]
            for dst_fn, src in dmas:
                dst = dst_fn(t)
                pl.op("pool", lambda e, dst=dst, src=src: e.dma_start(out=dst, in_=src),
                      deps=[ring_free[s]], inc=(ring_sem[s], 16))
                ring_fills[s] += 1
            return s, t, (ring_sem[s], 16 * ring_fills[s])

        def ring_rel(s, tok):
            ring_free[s] = tok

        def colblock(w2d, c0):
            src = w2d.rearrange("(k p) n -> p k n", p=128)[:, :, c0:c0 + 256]
            return [(lambda t: t[:, :].rearrange("p (k c) -> p k c", k=KC), src)]

        def rowblock(w2d, r0):
            src = w2d[r0:r0 + 256, :].rearrange("(j p) n -> p j n", p=128)
            return [(lambda t: t[:, :].rearrange("p (j n) -> p j n", j=2), src)]

        def cview(t):
            return t[:, :].rearrange("p (k c) -> p k c", k=KC)

        def rview(t):
            return t[:, :].rearrange("p (j n) -> p j n", j=2)

        def hT(kc, g):
            return (hT_h if g.kind == "h" else hT_m)[:, kc, g.t0:g.t0 + g.n]

        def xnT(kc, g):
            return (xn_h if g.kind == "h" else xn_m)[:, kc, g.t0:g.t0 + g.n]

        def xn_tile(kc, t):
            return xn_h[:, kc, :] if t == 0 else xn_m[:, kc, (t - 1) * 128:t * 128]

        state = {"h_tok": None, "xn_tok": None}

        t_cols = sp_dma(cols[:, :], cols_d[:, :])
        t_cst = sp_dma(cst[:, :], cst_d[:, :])
        t_idb = pl.op("dve", lambda e: e.tensor_copy(out=ident_b[:, :], in_=ident_f), deps=[t_cst])
        t_ones = pl.op("dve", lambda e: e.memset(ones_f[:, :], 1.0 / D))
        epsc = small[:, 90:91]
        t_eps = pl.op("dve", lambda e: e.memset(epsc, EPS))

        def phase_load():
            tok = None
            n = 0
            for t in range(9):
                for qd in range(4):
                    bi, buf, bfree = tApool.get()
                    t_ld = sp_dma(buf[:, :], xin[t * 128:(t + 1) * 128, qd * 512:(qd + 1) * 512], deps=[bfree])
                    b, bf = banks.get()
                    for j in range(4):
                        t_mm = pl.op("pe", lambda e, b=b, j=j, buf=buf: e.transpose(
                            out=psf(b)[:, j * 128:(j + 1) * 128], in_=buf[:, j * 128:(j + 1) * 128], identity=ident_f),
                            deps=[t_ld, bf, t_cst], ms=(j == 3))
                    tApool.rel(bi, t_mm)
                    if t == 0:
                        dst = hT_h[:, qd * 4:(qd + 1) * 4, :]
                    else:
                        dst = hT_m[:, qd * 4:(qd + 1) * 4, (t - 1) * 128:t * 128]
                    src = psf(b).rearrange("p (j n) -> p j n", j=4)
                    tok = pl.op("dve", lambda e, dst=dst, src=src: e.tensor_copy(out=dst, in_=src), deps=[t_mm])
                    banks.rel(b, tok)
                    n += 1
            state["h_tok"] = tok

        def rstd_group(g):
            n = g.n
            b, bf = banks.get()
            t_mm = None
            for kc in range(KC):
                qi, sq, sqf = sqpool.get()
                t_sq = pl.op("act", lambda e, sq=sq, kc=kc: e.activation(out=sq[:, :n], in_=hT(kc, g), func=AF.Square),
                             deps=[sqf, state["h_tok"]])
                t_mm = pl.op("pe", lambda e, sq=sq, kc=kc, b=b: e.matmul(
                    psf(b)[:, :n], lhsT=ones_f[:, :], rhs=sq[:, :n], start=(kc == 0), stop=(kc == KC - 1)),
                    deps=[t_sq, bf if kc == 0 else None, t_ones], ms=True)
                sqpool.rel(qi, t_mm)
            ri, rs, rsf = rspool.get()
            t_r0 = pl.op("act", lambda e, b=b, rs=rs: e.activation(
                out=rs[:, :n], in_=psf(b)[:, :n], func=AF.Sqrt, bias=epsc, scale=1.0), deps=[t_mm, rsf, t_eps])
            banks.rel(b, t_r0)
            t_r = pl.op("dve", lambda e, rs=rs: e.reciprocal(out=rs[:, :n], in_=rs[:, :n]), deps=[t_r0])
            return ri, rs, t_r

        def rmsnorm(cbase, groups):
            tok = None
            for g in groups:
                n = g.n
                ri, rs, t_r = rstd_group(g)
                for kc in range(KC):
                    tok = pl.op("dve", lambda e, kc=kc, g=g, rs=rs: e.scalar_tensor_tensor(
                        out=xnT(kc, g), in0=hT(kc, g), scalar=cols[:, cbase + kc:cbase + kc + 1], in1=rs[:, :g.n],
                        op0=ALU.mult, op1=ALU.mult), deps=[t_r, t_cols, state["h_tok"]])
                rspool.rel(ri, tok)
            state["xn_tok"] = tok

        def accum(b, g, dc, t_mm, bias_col=None):
            n = g.n
            if bias_col is None:
                tok = pl.op("dve", lambda e: e.tensor_tensor(out=hT(dc, g), in0=psf(b)[:, :n], in1=hT(dc, g), op=ALU.add),
                            deps=[t_mm])
            else:
                tok = pl.op("dve", lambda e: e.scalar_tensor_tensor(
                    out=hT(dc, g), in0=psf(b)[:, :n], scalar=bias_col, in1=hT(dc, g), op0=ALU.add, op1=ALU.add),
                    deps=[t_mm])
            banks.rel(b, tok)
            state["h_tok"] = tok
            return tok

        def phase_ffn(l, groups):
            rmsnorm(C_FFN + 16 * l, groups)
            ntok = NHALO + NMAIN
            hid = [A[:, i * 4 * ntok:(i + 1) * 4 * ntok].rearrange("p (j n) -> p j n", j=4) for i in range(2)]
            sgp = Pool([f32view(A, 2 * 4 * ntok + i * 1024, 1024) for i in range(3)])
            hid_tok = [None, None]
            fen = pl.fence()

            def GU(s):
                hb = hid[s % 2]
                last = None
                for half in range(2):
                    c0 = s * 512 + half * 256
                    sg_, tg_, lg = ring_load(colblock(wg_d[l], c0))
                    su_, tu_, lu = ring_load(colblock(wu_d[l], c0))
                    wg, wu = cview(tg_), cview(tu_)
                    tU = None
                    for j in range(2):
                        jj = half * 2 + j
                        for g in groups:
                            n = g.n
                            bg, fg = banks.get()
                            bu, fu = banks.get()
                            for kc in range(KC):
                                tG = pl.op("pe", lambda e, kc=kc, g=g, bg=bg, j=j, wg=wg: e.matmul(
                                    psf(bg)[:, :g.n], lhsT=wg[:, kc, j * 128:(j + 1) * 128], rhs=xnT(kc, g),
                                    start=(kc == 0), stop=(kc == KC - 1)),
                                    deps=[lg, state["xn_tok"], fg], ms=(kc == KC - 1))
                            for kc in range(KC):
                                tU = pl.op("pe", lambda e, kc=kc, g=g, bu=bu, j=j, wu=wu: e.matmul(
                                    psf(bu)[:, :g.n], lhsT=wu[:, kc, j * 128:(j + 1) * 128], rhs=xnT(kc, g),
                                    start=(kc == 0), stop=(kc == KC - 1)),
                                    deps=[lu, fu], ms=(kc == KC - 1))
                            si, sgt, sf = sgp.get()
                            t1 = pl.op("act", lambda e, sgt=sgt, bg=bg, n=n: e.activation(
                                out=sgt[:, :n], in_=psf(bg)[:, :n], func=AF.Silu), deps=[tG, sf, fen])
                            banks.rel(bg, t1)
                            t2 = pl.op("dve", lambda e, sgt=sgt, bu=bu, n=n, jj=jj, g=g: e.tensor_tensor(
                                out=hb[:, jj, g.a0:g.a0 + n], in0=psf(bu)[:, :n], in1=sgt[:, :n], op=ALU.mult),
                                deps=[tU, t1, fen])
                            banks.rel(bu, t2)
                            sgp.rel(si, t2)
                            last = t2
                    ring_rel(sg_, tU)
                    ring_rel(su_, tU)
                hid_tok[s % 2] = last

            def DOWN(s):
                hb = hid[s % 2]
                r0 = s * 512
                s0, t0_, l0 = ring_load(rowblock(wd_d[l], r0))
                s1, t1_, l1 = ring_load(rowblock(wd_d[l], r0 + 256))
                wd = [rview(t0_), rview(t1_)]
                tmm = None
                for dc in range(KC):
                    for g in groups:
                        b, bf = banks.get()
                        for jj in range(4):
                            tmm = pl.op("pe", lambda e, jj=jj, g=g, b=b, dc=dc: e.matmul(
                                psf(b)[:, :g.n], lhsT=wd[jj // 2][:, jj % 2, dc * 128:(dc + 1) * 128],
                                rhs=hb[:, jj, g.a0:g.a0 + g.n], start=(jj == 0), stop=(jj == 3)),
                                deps=[l0, l1, hid_tok[s % 2], bf], ms=(jj == 3))
                        accum(b, g, dc, tmm)
                ring_rel(s0, tmm)
                ring_rel(s1, tmm)

            GU(0)
            for s in range(NSC):
                if s + 1 < NSC:
                    GU(s + 1)
                DOWN(s)

        def gelu_from_psum(src, n, hxp, wp, out_fn, extra_deps, out_scalar=None):
            hi, hx, hf = hxp.get()
            wi, w, wf = wp.get()
            t_h = pl.op("act", lambda e: e.activation(out=hx[:, :n], in_=src, func=AF.Copy, scale=0.5),
                        deps=[hf] + list(extra_deps))
            t_w = pl.op("act", lambda e: e.activation(out=w[:, :n], in_=src, func=AF.Square, scale=math.sqrt(0.044715)),
                        deps=[wf])
            t_t = pl.op("dve", lambda e: e.scalar_tensor_tensor(
                out=w[:, :n], in0=w[:, :n], scalar=1.0, in1=hx[:, :n], op0=ALU.add, op1=ALU.mult), deps=[t_h, t_w])
            t_th = pl.op("act", lambda e: e.activation(out=w[:, :n], in_=w[:, :n], func=AF.Tanh, scale=2.0 * GELU_C),
                         deps=[t_t])
            t_o = out_fn(hx, w, [t_th])
            hxp.rel(hi, t_o)
            wp.rel(wi, t_o)
            return t_w, t_o

        def phase_gmlp(l):
            i = l
            groups = G01
            rmsnorm(C_MIX + 16 * l, groups)
            fen = pl.fence()
            v = A[:, :].rearrange("p (t d) -> p t d", t=9)
            ntok = NHALO + NMAIN
            gated = [Bq[:, k * 2 * ntok:(k + 1) * 2 * ntok].rearrange("p (j n) -> p j n", j=2) for k in range(2)]
            h256 = [tbuf[3][:, 0:256], tbuf[3][:, 256:512], tbuf[4][:, 0:256]]
            w256 = [tbuf[4][:, 256:512], tbuf[5][:, 0:256], tbuf[5][:, 256:512]]
            hxp, wp = Pool(h256), Pool(w256)
            u256 = Pool([tbuf[0][:, 0:256], tbuf[0][:, 256:512], tbuf[1][:, 0:256]])
            g256 = Pool([tbuf[1][:, 256:512], tbuf[2][:, 0:256]])
            VtF = Vt[:, :, :].rearrange("p a b -> p (a b)")
            WsT = VtF[:, 0:1024].rearrange("p (g t) -> p g t", g=8)
            bbp = Pool([VtF[:, 1024:1280].bitcast(F32), VtF[:, 1280:1536].bitcast(F32)])
            junk = VtF[:, 1536:1792]
            ssv = small[:, 0:72].rearrange("p (t c) -> p t c", t=9)
            ss9 = small[:, 72:81]
            rsv = small[:, 81:90]

            t_z = pl.op("dve", lambda e: e.memset(ssv, 0.0), deps=[fen])
            t_ws = None
            for hh in range(2):
                ti, tb, tf = sqpool.get()
                t_ld = sp_dma(tb[:, :].rearrange("p (g t) -> p g t", g=4), wst_d[i, :, hh * 4:(hh + 1) * 4, :], deps=[tf, fen])
                for gg in range(4):
                    t_ws = pl.op("dve", lambda e, tb=tb, gg=gg, hh=hh: e.tensor_tensor(
                        out=WsT[:, hh * 4 + gg, :], in0=tb[:, gg * 128:(gg + 1) * 128], in1=triu, op=ALU.mult),
                        deps=[t_ld, t_cst, fen])
                sqpool.rel(ti, t_ws)

            for cb in range(8):
                sl, tl, ld = ring_load(colblock(win_d[i], D + cb * 256))
                wv = cview(tl)
                tmm = None
                for t in range(9):
                    b, bf = banks.get()
                    for kc in range(KC):
                        tmm = pl.op("pe", lambda e, kc=kc, t=t, b=b, wv=wv: e.matmul(
                            psf(b)[:, :256], lhsT=xn_tile(kc, t), rhs=wv[:, kc, :], start=(kc == 0), stop=(kc == KC - 1)),
                            deps=[ld, state["xn_tok"], bf], ms=(kc == KC - 1))

                    def out_fn(hx, w, deps, t=t, cb=cb):
                        t_v = pl.op("dve", lambda e: e.scalar_tensor_tensor(
                            out=v[:, t, cb * 256:(cb + 1) * 256], in0=w[:, :256], scalar=1.0, in1=hx[:, :256],
                            op0=ALU.add, op1=ALU.mult), deps=deps + [fen])
                        t_s = pl.op("act", lambda e: e.activation(
                            out=junk, in_=v[:, t, cb * 256:(cb + 1) * 256], func=AF.Square,
                            accum_out=ssv[:, t, cb:cb + 1]), deps=[t_v, t_z])
                        return t_s
                    t_w, t_o = gelu_from_psum(psf(b)[:, :256], 256, hxp, wp, out_fn, [tmm])
                    banks.rel(b, t_w)
                ring_rel(sl, tmm)
            t_a = pl.op("dve", lambda e: e.tensor_reduce(out=ss9, in_=ssv, axis=AX.X, op=ALU.add), deps=[pl.last("act")])
            t_b = pl.op("dve", lambda e: e.tensor_scalar(out=ss9, in0=ss9, scalar1=1.0 / D, scalar2=EPS,
                                                       op0=ALU.mult, op1=ALU.add), deps=[t_a])
            t_c0 = pl.op("act", lambda e: e.activation(out=rsv, in_=ss9, func=AF.Sqrt), deps=[t_b])
            t_c = pl.op("dve", lambda e: e.reciprocal(out=rsv, in_=rsv), deps=[t_c0])
            t_vn = None
            for t in range(9):
                t_vn = pl.op("act", lambda e, t=t: e.activation(out=v[:, t, :], in_=v[:, t, :], func=AF.Copy,
                                                              scale=rsv[:, t:t + 1]), deps=[t_c])

            gated_tok = [None, None]

            def UG(g8):
                gb = gated[g8 % 2]
                slU, tlU, ldU = ring_load(colblock(win_d[i], g8 * 256))
                wu = cview(tlU)
                bi, bb, bbf = bbp.get()
                t_bb = sp_dma(bb, bs_d[i, g8, :].partition_broadcast(128), deps=[bbf])
                last = None
                tmm = None
                for j in range(2):
                    dcg = 2 * g8 + j
                    for g in groups:
                        n = g.n
                        b, bf = banks.get()
                        for kc in range(KC):
                            tmm = pl.op("pe", lambda e, kc=kc, g=g, b=b, j=j: e.matmul(
                                psf(b)[:, :g.n], lhsT=wu[:, kc, j * 128:(j + 1) * 128], rhs=xnT(kc, g),
                                start=(kc == 0), stop=(kc == KC - 1)), deps=[ldU, bf], ms=(kc == KC - 1))
                        b2, bf2 = banks.get()
                        tmm2 = None
                        for ti, t in enumerate(g.tiles):
                            tmm2 = pl.op("pe", lambda e, ti=ti, t=t, b2=b2, dcg=dcg: e.matmul(
                                psf(b2)[:, ti * 128:(ti + 1) * 128], lhsT=v[:, t, dcg * 128:(dcg + 1) * 128],
                                rhs=WsT[:, g8, :], start=True, stop=True),
                                deps=[t_vn, t_ws, bf2], ms=(ti == len(g.tiles) - 1))
                        pieces = [(c, min(256, n - c)) for c in range(0, n, 256)]
                        for pi, (c0, cn) in enumerate(pieces):
                            ui, ut, uf = u256.get()

                            def out_fn(hx, w, deps, ut=ut, cn=cn):
                                return pl.op("dve", lambda e: e.scalar_tensor_tensor(
                                    out=ut[:, :cn], in0=w[:, :cn], scalar=1.0, in1=hx[:, :cn],
                                    op0=ALU.add, op1=ALU.mult), deps=deps + [uf])
                            t_w, t_u = gelu_from_psum(psf(b)[:, c0:c0 + cn], cn, hxp, wp, out_fn, [tmm])
                            if pi == len(pieces) - 1:
                                banks.rel(b, t_w)
                            gi, gt, gf = g256.get()
                            nt = cn // 128
                            t_g1 = pl.op("dve", lambda e, gt=gt, c0=c0, cn=cn, nt=nt, b2=b2, dcg=dcg, bb=bb: e.scalar_tensor_tensor(
                                out=gt[:, :cn].rearrange("p (a t) -> p a t", a=nt),
                                in0=psf(b2)[:, c0:c0 + cn].rearrange("p (a t) -> p a t", a=nt),
                                scalar=cols[:, C_NV + 16 * i + dcg:C_NV + 16 * i + dcg + 1],
                                in1=bb.unsqueeze(1).broadcast_to([128, nt, 128]),
                                op0=ALU.mult, op1=ALU.add), deps=[tmm2, t_bb, gf, t_cols])
                            if pi == len(pieces) - 1:
                                banks.rel(b2, t_g1)
                            t_g2 = pl.op("dve", lambda e, gt=gt, ut=ut, cn=cn, c0=c0, g=g, j=j: e.tensor_tensor(
                                out=gb[:, j, g.a0 + c0:g.a0 + c0 + cn], in0=gt[:, :cn], in1=ut[:, :cn], op=ALU.mult),
                                deps=[t_g1, t_u])
                            u256.rel(ui, t_g2)
                            g256.rel(gi, t_g2)
                            last = t_g2
                ring_rel(slU, tmm)
                bbp.rel(bi, last)
                gated_tok[g8 % 2] = last

            def OUT(g8):
                gb = gated[g8 % 2]
                slO, tlO, ldO = ring_load(rowblock(wout_d[i], g8 * 256))
                wo = rview(tlO)
                tmm = None
                for dc in range(KC):
                    for g in groups:
                        b, bf = banks.get()
                        for j in range(2):
                            tmm = pl.op("pe", lambda e, j=j, g=g, b=b, dc=dc: e.matmul(
                                psf(b)[:, :g.n], lhsT=wo[:, j, dc * 128:(dc + 1) * 128], rhs=gb[:, j, g.a0:g.a0 + g.n],
                                start=(j == 0), stop=(j == 1)), deps=[ldO, gated_tok[g8 % 2], bf], ms=(j == 1))
                        accum(b, g, dc, tmm)
                ring_rel(slO, tmm)

            UG(0)
            for g8 in range(8):
                if g8 + 1 < 8:
                    UG(g8 + 1)
                OUT(g8)

        KT = Bq[:, :].rearrange("p (m n) -> p m n", m=4)

        def phase_kv():
            groups = G01
            rmsnorm(C_KV, groups)
            fen = pl.fence()
            bv = tbuf[3][:, 0:256]
            t_bv = sp_dma(bv, bkv_d[256:512].partition_broadcast(128), deps=[fen])
            for half in range(2):
                dm = []
                for h2 in range(2):
                    src = wkv_d[:, (half * 2 + h2) * 64:(half * 2 + h2 + 1) * 64].rearrange("(k p) c -> p k c", p=128)
                    for dup in range(2):
                        dm.append((lambda t, dup=dup, h2=h2: t[:, :].rearrange(
                            "p (k h u c) -> p k h u c", k=KC, h=2, u=2)[:, :, h2, dup, :], src))
                sl, tl, ld = ring_load(dm)
                wk = cview(tl)
                tmm = None
                for hh in range(2):
                    m = half * 2 + hh
                    for g in groups:
                        b, bf = banks.get()
                        for kc in range(KC):
                            tmm = pl.op("pe", lambda e, kc=kc, g=g, b=b, hh=hh, wk=wk: e.matmul(
                                psf(b)[:, :g.n], lhsT=wk[:, kc, hh * 128:(hh + 1) * 128], rhs=xnT(kc, g),
                                start=(kc == 0), stop=(kc == KC - 1)), deps=[ld, state["xn_tok"], bf], ms=(kc == KC - 1))
                        t_e = pl.op("act", lambda e, g=g, b=b, m=m: e.activation(
                            out=KT[:, m, g.a0:g.a0 + g.n], in_=psf(b)[:, :g.n], func=AF.Identity,
                            bias=cols[:, C_BK + m:C_BK + m + 1], scale=1.0), deps=[tmm, t_cols, fen])
                        banks.rel(b, t_e)
                ring_rel(sl, tmm)
            sl, tl, ld = ring_load(colblock(wkv_d, 256))
            wv = cview(tl)
            tmm = None
            for t in range(9):
                b, bf = banks.get()
                for kc in range(KC):
                    tmm = pl.op("pe", lambda e, kc=kc, t=t, b=b: e.matmul(
                        psf(b)[:, :256], lhsT=xn_tile(kc, t), rhs=wv[:, kc, :], start=(kc == 0), stop=(kc == KC - 1)),
                        deps=[ld, bf], ms=(kc == KC - 1))
                t_e = pl.op("dve", lambda e, t=t, b=b: e.tensor_tensor(out=Vt[:, t, :], in0=psf(b)[:, :256], in1=bv, op=ALU.add),
                            deps=[tmm, t_bv, fen])
                banks.rel(b, t_e)
            ring_rel(sl, tmm)

        def phase_attn(l):
            i = l - 2
            groups = G23
            rmsnorm(C_MIX + 16 * l, groups)
            fen = pl.fence()
            attnT = A[:, 0:16384].rearrange("p (k n) -> p k n", k=KC)
            qTz = [A[:, 16384:17408], A[:, 17408:18432]]
            t_qz = pl.op("dve", lambda e: e.memset(A[:, 16384:18432], 0.0), deps=[fen])
            hh_f = hT_h[:, :, :].rearrange("p k n -> p (k n)")
            braw = hh_f[:, 0:512].rearrange("p (e k) -> p e k", e=2)
            biasp = [hh_f[:, 512:1024].rearrange("p (e k) -> p e k", e=2), hh_f[:, 1024:1536].rearrange("p (e k) -> p e k", e=2)]
            bias0 = hh_f[:, 1536:2048].rearrange("p (e k) -> p e k", e=2)
            scp = Pool([tbuf[3][:, :].rearrange("p (e k) -> p e k", e=2), tbuf[4][:, :].rearrange("p (e k) -> p e k", e=2)])
            xh_b = xn_h[:, :, :].rearrange("p k n -> p (k n)")
            pp = Pool([xh_b[:, k * 512:(k + 1) * 512].rearrange("p (e k) -> p e k", e=2) for k in range(2)])
            pTp = Pool([xh_b[:, 1024 + k * 512:1024 + (k + 1) * 512].rearrange("p (a q) -> p a q", a=4) for k in range(2)])
            t5b = tbuf[5][:, :].bitcast(BF16)
            op_ = Pool([t5b[:, k * 128:(k + 1) * 128] for k in range(3)])
            sink_bc = small[:, 96:128]
            mask0 = tbuf[2][:, 0:256]
            t_sk = sp_dma(sink_bc, sink_d[i, :].partition_broadcast(128), deps=[fen])
            t_m0 = sp_dma(mask0, mask0_d[:, :], deps=[fen])
            sm = small[:, 128:256]
            smp = Pool([sm[:, k * 16:(k + 1) * 16] for k in range(8)])
            poolS, poolT, poolO, poolX, poolQ = Banks([0, 1]), Banks([2, 3]), Banks([4, 5]), Banks([6]), Banks([7])
            bias_tok = [None, None]
            bias0_tok = [None]
            braw_free = [None]
            stt = {}
            wq_slot = [None]

            def S0(it):
                hp, qb = it
                c = stt.setdefault(it, {})
                if qb == 0:
                    if hp % 2 == 0:
                        if wq_slot[0] is not None:
                            ring_rel(wq_slot[0][0], wq_slot[0][3])
                        sl, tl, ld = ring_load(colblock(wq_d[i], (hp // 2) * 256))
                        wq_slot[0] = [sl, cview(tl), ld, None]
                    wq = wq_slot[0][1]
                    ld = wq_slot[0][2]
                    t_q = None
                    for g in groups:
                        b, bf = poolQ.get()
                        tmm = None
                        for kc in range(KC):
                            tmm = pl.op("pe", lambda e, kc=kc, g=g, b=b: e.matmul(
                                psf(b)[:, :g.n], lhsT=wq[:, kc, (hp % 2) * 128:(hp % 2 + 1) * 128], rhs=xnT(kc, g),
                                start=(kc == 0), stop=(kc == KC - 1)), deps=[ld, state["xn_tok"], bf], ms=(kc == KC - 1))
                        wq_slot[0][3] = tmm
                        for e2 in range(2):
                            t_q = pl.op("act", lambda e, g=g, b=b, e2=e2: e.activation(
                                out=qTz[e2][e2 * 64:(e2 + 1) * 64, g.t0:g.t0 + g.n], in_=psf(b)[e2 * 64:(e2 + 1) * 64, :g.n],
                                func=AF.Identity, bias=cols[e2 * 64:(e2 + 1) * 64, C_BQ + 16 * i + hp:C_BQ + 16 * i + hp + 1],
                                scale=1.0), deps=[tmm, t_cols, fen, t_qz])
                        Banks.rel(b, t_q)
                    c["q_tok_new"] = t_q
                    stt["q_tok"] = t_q
                    t_ld = sp_dma(braw, biasg_d[:, 2 * hp:2 * hp + 2, :], deps=[braw_free[0], fen])
                    t_b0 = pl.op("dve", lambda e: e.tensor_tensor(
                        out=bias0, in0=braw, in1=mask0.unsqueeze(1).broadcast_to([128, 2, 256]), op=ALU.add),
                        deps=[t_ld, t_m0, bias0_tok[0]])
                    bp = biasp[hp % 2]
                    t_b1 = pl.op("dve", lambda e: e.tensor_tensor(
                        out=bp, in0=braw, in1=maskadd.unsqueeze(1).broadcast_to([128, 2, 256]), op=ALU.add),
                        deps=[t_ld, t_cst, bias_tok[hp % 2]])
                    braw_free[0] = t_b1
                    stt["b0"] = t_b0
                    stt["b1"] = t_b1
                kvh = hp // 4
                b, bf = poolS.get()
                tmm = None
                for e2 in range(2):
                    tmm = pl.op("pe", lambda e, e2=e2, b=b: e.matmul(
                        psf(b)[:, e2 * 256:(e2 + 1) * 256], lhsT=qTz[e2][:, qb * 128:(qb + 1) * 128],
                        rhs=KT[:, kvh, qb * 128:qb * 128 + 256], start=True, stop=True),
                        deps=[stt["q_tok"], bf, fen], ms=(e2 == 1))
                c["bS"], c["tS"] = b, tmm

            def S1(it):
                hp, qb = it
                c = stt[it]
                b = c["bS"]
                si, sc, sf = scp.get()
                bsel = bias0 if qb == 0 else biasp[hp % 2]
                t_sc = pl.op("dve", lambda e: e.scalar_tensor_tensor(
                    out=sc, in0=psf(b)[:, :].rearrange("p (e k) -> p e k", e=2), scalar=0.125, in1=bsel,
                    op0=ALU.mult, op1=ALU.add), deps=[c["tS"], sf, stt["b0"], stt["b1"]])
                Banks.rel(b, t_sc)
                if qb == 0:
                    bias0_tok[0] = t_sc
                bias_tok[hp % 2] = t_sc
                mi, ms_, mf = smp.get()
                mx, negm, dd, rs, den, rden = (ms_[:, 0:2], ms_[:, 2:4], ms_[:, 4:6], ms_[:, 6:8], ms_[:, 8:10], ms_[:, 10:12])
                sk = sink_bc[:, 2 * hp:2 * hp + 2]
                t1 = pl.op("dve", lambda e: e.tensor_reduce(out=mx, in_=sc, axis=AX.X, op=ALU.max), deps=[t_sc, mf])
                t2 = pl.op("dve", lambda e: e.tensor_tensor(out=mx, in0=mx, in1=sk, op=ALU.max), deps=[t1, t_sk])
                t3 = pl.op("dve", lambda e: e.tensor_scalar(out=negm, in0=mx, scalar1=-1.0, scalar2=None, op0=ALU.mult), deps=[t2])
                t4 = pl.op("dve", lambda e: e.tensor_tensor(out=dd, in0=sk, in1=negm, op=ALU.add), deps=[t3])
                t_z = pl.op("dve", lambda e: e.memset(rs, 0.0), deps=[mf])
                pi, p, pf = pp.get()
                t_e = None
                for e2 in range(2):
                    t_e = pl.op("act", lambda e, e2=e2: e.activation(
                        out=p[:, e2, :], in_=sc[:, e2, :], func=AF.Exp, bias=negm[:, e2:e2 + 1], scale=1.0,
                        accum_out=rs[:, e2:e2 + 1]), deps=[t3, t_z, pf])
                scp.rel(si, t_e)
                t5 = pl.op("act", lambda e: e.activation(out=dd, in_=dd, func=AF.Exp), deps=[t4])
                t6 = pl.op("dve", lambda e: e.tensor_tensor(out=den, in0=rs, in1=dd, op=ALU.add), deps=[t5, t_e])
                t7 = pl.op("dve", lambda e: e.reciprocal(out=rden, in_=den), deps=[t6])
                c["rden"], c["t_rden"], c["mi"] = rden, t7, mi
                bT, bfT = poolT.get()
                tmm = None
                for e2 in range(2):
                    for kb in range(2):
                        a = e2 * 2 + kb
                        tmm = pl.op("pe", lambda e, a=a, e2=e2, kb=kb: e.transpose(
                            out=psb(bT)[:, a * 128:(a + 1) * 128], in_=p[:, e2, kb * 128:(kb + 1) * 128], identity=ident_b[:, :]),
                            deps=[t_e, bfT, t_idb], ms=(a == 3))
                pp.rel(pi, tmm)
                c["bT"], c["tT"] = bT, tmm

            def S2(it):
                hp, qb = it
                c = stt[it]
                kvh = hp // 4
                bT = c["bT"]
                ti, pT, tf = pTp.get()
                t_c = pl.op("act", lambda e: e.activation(
                    out=pT, in_=psb(bT)[:, 0:512].rearrange("p (a q) -> p a q", a=4), func=AF.Copy), deps=[c["tT"], tf])
                Banks.rel(bT, t_c)
                bO, bfO = poolO.get()
                tmm = None
                for e2 in range(2):
                    for kb in range(2):
                        tmm = pl.op("pe", lambda e, e2=e2, kb=kb: e.matmul(
                            psf(bO)[:, e2 * 64:(e2 + 1) * 64], lhsT=pT[:, e2 * 2 + kb, :],
                            rhs=Vt[:, qb + kb, kvh * 64:(kvh + 1) * 64], start=(kb == 0), stop=(kb == 1)),
                            deps=[t_c, bfO, fen], ms=(e2 == 1 and kb == 1))
                pTp.rel(ti, tmm)
                c["bO"], c["tO"] = bO, tmm

            def S3(it):
                hp, qb = it
                c = stt[it]
                bO = c["bO"]
                oi, o, of = op_.get()
                t_o = pl.op("dve", lambda e: e.tensor_tensor(
                    out=o.rearrange("p (e d) -> p e d", e=2), in0=psf(bO)[:, 0:128].rearrange("p (e d) -> p e d", e=2),
                    in1=c["rden"].unsqueeze(2).broadcast_to([128, 2, 64]), op=ALU.mult), deps=[c["tO"], c["t_rden"], of])
                Banks.rel(bO, t_o)
                smp.rel(c["mi"], t_o)
                bX, bfX = poolX.get()
                tmm = pl.op("pe", lambda e: e.transpose(out=psb(bX)[:, 0:128], in_=o, identity=ident_b[:, :]),
                            deps=[t_o, bfX], ms=True)
                op_.rel(oi, tmm)
                t_a = pl.op("act", lambda e: e.activation(out=attnT[:, hp, qb * 128:(qb + 1) * 128], in_=psb(bX)[:, 0:128],
                                                        func=AF.Copy), deps=[tmm, fen])
                Banks.rel(bX, t_a)
                stt["attn_tok"] = t_a
                del stt[it]

            its = [(hp, qb) for hp in range(16) for qb in range(8)]
            stages = [S0, S1, S2, S3]
            for k in (1, 2, 3):
                if f"st{k}" in dbg:
                    stages = stages[:k]
                    its = its[:16]
            for step in range(len(its) + len(stages) - 1):
                for sidx in reversed(range(len(stages))):
                    k = step - sidx
                    if 0 <= k < len(its):
                        stages[sidx](its[k])
            ring_rel(wq_slot[0][0], wq_slot[0][3])
            if len(stages) < 4:
                return
            for cbk in range(8):
                sl, tl, ld = ring_load(colblock(wo_d[i], cbk * 256))
                wo = cview(tl)
                tmm = None
                for j in range(2):
                    dc = cbk * 2 + j
                    for g in groups:
                        b, bf = banks.get()
                        for kc in range(KC):
                            tmm = pl.op("pe", lambda e, kc=kc, g=g, b=b, j=j, wo=wo: e.matmul(
                                psf(b)[:, :g.n], lhsT=wo[:, kc, j * 128:(j + 1) * 128], rhs=attnT[:, kc, g.t0:g.t0 + g.n],
                                start=(kc == 0), stop=(kc == KC - 1)), deps=[ld, stt["attn_tok"], bf], ms=(kc == KC - 1))
                        accum(b, g, dc, tmm, bias_col=cols[:, C_BO + 16 * i + dc:C_BO + 16 * i + dc + 1])
                ring_rel(sl, tmm)

        st_sems = [pl.new_sem("st0"), pl.new_sem("st1")]
        n_st = [0, 0]

        def phase_out(final):
            groups = G23 if final else G01
            fen = pl.fence()
            fT = [f32view(A, k * 1024, 1024) for k in range(4)]
            ostp = Pool([f32view(A, (4 + k) * 1024, 1024) for k in range(2)])
            fT_free = [None] * 4
            row0 = 0 if final else None
            for g in groups:
                n = g.n
                if final:
                    ri, rs, t_r = rstd_group(g)
                for kq in range(4):
                    srcs = []
                    if final:
                        for j in range(4):
                            kc = kq * 4 + j
                            t_f = pl.op("dve", lambda e, kc=kc, j=j, g=g, rs=rs: e.scalar_tensor_tensor(
                                out=fT[j][:, :g.n], in0=hT(kc, g), scalar=cols[:, C_FIN + kc:C_FIN + kc + 1], in1=rs[:, :g.n],
                                op0=ALU.mult, op1=ALU.mult), deps=[t_r, fT_free[j], fen, t_cols])
                            srcs.append((fT[j], 0, t_f))
                    else:
                        for j in range(4):
                            kc = kq * 4 + j
                            base = hT_h if g.kind == "h" else hT_m
                            srcs.append((base[:, kc, :], g.t0, state["h_tok"]))
                    for ti in range(n // 128):
                        b, bf = banks.get()
                        tmm = None
                        for j in range(4):
                            s_ap, off, s_tok = srcs[j]
                            tmm = pl.op("pe", lambda e, s_ap=s_ap, off=off, j=j, ti=ti, b=b: e.transpose(
                                out=psf(b)[:, j * 128:(j + 1) * 128], in_=s_ap[:, off + ti * 128:off + (ti + 1) * 128],
                                identity=ident_f), deps=[s_tok, bf, t_cst], ms=(j == 3))
                        oi, ost, of = ostp.get()
                        t_c = pl.op("act", lambda e, ost=ost, b=b: e.activation(out=ost[:, :], in_=psf(b), func=AF.Copy),
                                    deps=[tmm, of, fen])
                        banks.rel(b, t_c)
                        if final:
                            r = g.t0 + ti * 128
                        else:
                            r = g.a0 + ti * 128
                        pl.op("sp", lambda e, ost=ost, r=r, kq=kq: e.dma_start(
                            out=out[r:r + 128, kq * 512:(kq + 1) * 512], in_=ost[:, :]), deps=[t_c], inc=(st_sems[oi], 16))
                        n_st[oi] += 1
                        ostp.rel(oi, (st_sems[oi], 16 * n_st[oi]))
                    if final:
                        for j in range(4):
                            fT_free[j] = tmm
                if final:
                    rspool.rel(ri, pl.last("dve"))
            pl.wait("sp", (st_sems[0], 16 * n_st[0]))
            pl.wait("sp", (st_sems[1], 16 * n_st[1]))

        phase_load()
        for l in layers:
            if l < 2:
                phase_gmlp(l)
                phase_ffn(l, G01)
            else:
                if l == 2:
                    phase_kv()
                if "kvonly" not in dbg:
                    phase_attn(l)
                if "noffn" not in dbg:
                    phase_ffn(l, G23)
        phase_out(is_last)

        with nc.Block() as block:
            @block.tensor
            def _(e):
                pl.replay("pe", e)

            @block.scalar
            def _(e):
                pl.replay("act", e)

            @block.vector
            def _(e):
                pl.replay("dve", e)

            @block.gpsimd
            def _(e):
                pl.replay("pool", e)

            @block.sync
            def _(e):
                pl.replay("sp", e)
    return nc


def _colize(v):
    return np.ascontiguousarray(np.asarray(v, np.float32).reshape(-1, 128).T)


def _t5_bucket(dist):
    max_exact = 16
    d = np.maximum(dist, 1).astype(np.float32)
    large = max_exact + (np.log(d / np.float32(max_exact)) / np.float32(math.log(128 / max_exact))
                         * np.float32(32 - max_exact)).astype(np.int32)
    large = np.minimum(large, 31)
    return np.where(dist < max_exact, dist, large)


def _host_tables(inp):
    cols = np.zeros((128, NCOL), np.float32)
    for l in range(4):
        cols[:, C_MIX + 16 * l:C_MIX + 16 * (l + 1)] = _colize(inp["mix_norm"][l])
        cols[:, C_FFN + 16 * l:C_FFN + 16 * (l + 1)] = _colize(inp["ffn_norm"][l])
    cols[:, C_KV:C_KV + 16] = _colize(inp["kv_norm"])
    cols[:, C_FIN:C_FIN + 16] = _colize(inp["final_norm"])
    for i in range(2):
        cols[:, C_NV + 16 * i:C_NV + 16 * (i + 1)] = _colize(inp["a_norm_v"][i])
        cols[:, C_BQ + 16 * i:C_BQ + 16 * (i + 1)] = _colize(inp["b_b_q"][i])
        cols[:, C_BO + 16 * i:C_BO + 16 * (i + 1)] = _colize(inp["b_b_o"][i])
    bk = np.asarray(inp["b_kv"], np.float32)[:256].reshape(4, 64)
    cols[:, C_BK:C_BK + 4] = np.concatenate([bk, bk], axis=1).T
    cst = np.zeros((128, 512), np.float32)
    cst[:, 0:128] = np.eye(128, dtype=np.float32)
    s = np.arange(128)[:, None]
    t = np.arange(128)[None, :]
    cst[:, 128:256] = (s <= t).astype(np.float32)
    dist = np.arange(128)[:, None] + 128 - np.arange(256)[None, :]
    in_window = (dist >= 0) & (dist < 128)
    cst[:, 256:512] = np.where(in_window, 0.0, MASKV).astype(np.float32)
    mask0_first = np.where(in_window & (np.arange(256)[None, :] >= 128), 0.0, MASKV).astype(np.float32)
    mask0_second = cst[:, 256:512].copy()
    bucket = _t5_bucket(np.clip(dist, 0, None).astype(np.int32))
    rel = np.asarray(inp["rel_bias"], np.float32)
    biasg = np.ascontiguousarray(rel[bucket].transpose(0, 2, 1))
    return cols, cst, mask0_first, mask0_second, biasg


def _core_x(x, c):
    b, half = c // 2, c % 2
    xm = x[b, half * NMAIN:(half + 1) * NMAIN]
    if half == 0:
        halo = np.zeros((NHALO, D), np.float32)
    else:
        halo = x[b, NMAIN - NHALO:NMAIN]
    return np.ascontiguousarray(np.concatenate([halo, xm], axis=0))


_NC_CACHE = {}


def _get_nc(layers, is_first, is_last):
    key = (tuple(layers), is_first, is_last)
    if key not in _NC_CACHE:
        _NC_CACHE[key] = build(layers, is_first, is_last)
    return _NC_CACHE[key]


def _in_map(inp, tabs, xin, c, layers):
    cols, cst, m0f, m0s, biasg = tabs
    m = {"xin": xin, "cols": cols, "cst": cst,
         "ffn_w_gate": inp["ffn_w_gate"], "ffn_w_up": inp["ffn_w_up"], "ffn_w_down": inp["ffn_w_down"]}
    if any(l < 2 for l in layers):
        m["a_w_in"] = inp["a_w_in"]
        m["a_w_sT"] = inp["_a_w_sT"]
        m["a_b_s"] = inp["a_b_s"]
        m["a_w_out"] = inp["a_w_out"]
    if any(l >= 2 for l in layers):
        m["w_kv"] = inp["w_kv"]
        m["b_kv"] = inp["b_kv"]
        m["b_w_q"] = inp["b_w_q"]
        m["b_w_o"] = inp["b_w_o"]
        m["b_sinks"] = inp["b_sinks"]
        m["biasg"] = biasg
        m["mask0"] = m0f if c % 2 == 0 else m0s
    return m


LAUNCHES = [((0, 1, 2, 3), True, True)]


def kernel(**inputs):
    inp = {k: np.ascontiguousarray(np.asarray(v, np.float32)) for k, v in inputs.items()}
    inp["_a_w_sT"] = np.ascontiguousarray(inp["a_w_s"].transpose(0, 3, 1, 2))
    tabs = _host_tables(inp)
    ncores = 8
    cur = [_core_x(inp["x"], c) for c in range(ncores)]
    for layers, is_first, is_last in LAUNCHES:
        nc = _get_nc(layers, is_first, is_last)
        in_maps = [_in_map(inp, tabs, cur[c], c, layers) for c in range(ncores)]
        res = run_bass_kernel_spmd(nc, in_maps, core_ids=list(range(ncores)))
        cur = [np.asarray(res.results[c]["out"], np.float32) for c in range(ncores)]
    outp = np.zeros((4, 2 * NMAIN, D), np.float32)
    for c in range(ncores):
        outp[c // 2, (c % 2) * NMAIN:(c % 2 + 1) * NMAIN] = cur[c]
    return outp
```

```python
import math
from contextlib import ExitStack

import numpy as np
import concourse.bass as bass
import concourse.mybir as mybir
from concourse.bass_utils import run_bass_kernel_spmd

F32 = mybir.dt.float32
BF16 = mybir.dt.bfloat16
AF = mybir.ActivationFunctionType
ALU = mybir.AluOpType
AX = mybir.AxisListType

D = 2048
KC = 16
DFF = 5632
NSC = 11
EPS = 1e-5
NMAIN = 1024
NHALO = 128
NSLOT = 4
MASKV = -30000.0

C_MIX = 0
C_FFN = 64
C_KV = 128
C_FIN = 144
C_NV = 160
C_BQ = 192
C_BO = 224
C_BK = 256
NCOL = 260

GELU_C = math.sqrt(2.0 / math.pi)


class Grp:
    def __init__(self, kind, t0, n):
        self.kind, self.t0, self.n = kind, t0, n
        self.a0 = t0 if kind == "h" else NHALO + t0
        self.tiles = [self.a0 // 128 + i for i in range(n // 128)]


G01 = [Grp("h", 0, 128), Grp("m", 0, 512), Grp("m", 512, 512)]
G23 = [Grp("m", 0, 512), Grp("m", 512, 512)]


class Plan:
    ENG = ("pe", "act", "dve", "pool", "sp")

    def __init__(self, nc, stack):
        self.nc, self.stack = nc, stack
        self.sems = []
        self.lists = {e: [] for e in self.ENG}
        self.waited = {e: {} for e in self.ENG}
        self.cnt = {}
        self.esem = {}
        for e in self.ENG:
            self.esem[e] = self.new_sem("s_" + e)
            self.cnt[e] = 0

    def new_sem(self, name):
        s = self.stack.enter_context(self.nc.semaphore(name))
        self.sems.append(s)
        return len(self.sems) - 1

    def wait(self, eng, tok):
        if tok is None:
            return
        si, val = tok
        if eng == "pe" and si == self.esem["pe"]:
            return
        if self.waited[eng].get(si, 0) >= val:
            return
        self.waited[eng][si] = val
        self.lists[eng].append(("w", si, val))

    def op(self, eng, fn, deps=(), ms=True, inc=None):
        for d in deps:
            if isinstance(d, list):
                for dd in d:
                    self.wait(eng, dd)
            else:
                self.wait(eng, d)
        if inc is not None:
            self.lists[eng].append(("o", fn, inc[0], inc[1]))
            return None
        if ms:
            self.cnt[eng] += 1
            self.lists[eng].append(("o", fn, self.esem[eng], 1))
            return (self.esem[eng], self.cnt[eng])
        self.lists[eng].append(("o", fn, None, 0))
        return None

    def last(self, eng):
        return (self.esem[eng], self.cnt[eng]) if self.cnt[eng] else None

    def fence(self):
        return [self.last("pe"), self.last("act"), self.last("dve")]

    def replay(self, eng, e):
        for it in self.lists[eng]:
            if it[0] == "w":
                e.wait_ge(self.sems[it[1]], it[2])
            else:
                ins = it[1](e)
                if it[2] is not None:
                    ins.then_inc(self.sems[it[2]], it[3])


class Pool:
    def __init__(self, aps):
        self.aps = aps
        self.free = [None] * len(aps)
        self.nxt = 0

    def get(self):
        i = self.nxt
        self.nxt = (i + 1) % len(self.aps)
        return i, self.aps[i], self.free[i]

    def rel(self, i, tok):
        self.free[i] = tok


def build(layers, is_first, is_last, dbg=()):
    layers = tuple(layers)
    do_g = any(l < 2 for l in layers)
    do_a = any(l >= 2 for l in layers)
    nc = bass.Bass("TRN2", target_bir_lowering=False)

    def dram(name, shape, kind="ExternalInput"):
        return nc.dram_tensor(name, list(shape), F32, kind=kind).ap()

    xin = dram("xin", [NHALO + NMAIN, D])
    n_out = NMAIN if is_last else NHALO + NMAIN
    out = dram("out", [n_out, D], kind="ExternalOutput")
    cols_d = dram("cols", [128, NCOL])
    cst_d = dram("cst", [128, 512])
    wg_d = dram("ffn_w_gate", [4, D, DFF])
    wu_d = dram("ffn_w_up", [4, D, DFF])
    wd_d = dram("ffn_w_down", [4, DFF, D])
    if do_g:
        win_d = dram("a_w_in", [2, D, 2 * D])
        wst_d = dram("a_w_sT", [2, 128, 8, 128])
        bs_d = dram("a_b_s", [2, 8, 128])
        wout_d = dram("a_w_out", [2, D, D])
    if do_a:
        wkv_d = dram("w_kv", [D, 512])
        bkv_d = dram("b_kv", [512])
        wq_d = dram("b_w_q", [2, D, D])
        wo_d = dram("b_w_o", [2, D, D])
        sink_d = dram("b_sinks", [2, 32])
        biasg_d = dram("biasg", [128, 32, 256])
        mask0_d = dram("mask0", [128, 256])

    with ExitStack() as st:
        pl = Plan(nc, st)

        def sb(name, shape, dt):
            return st.enter_context(nc.sbuf_tensor(name, list(shape), dt))

        hT_m = sb("hT_m", [128, KC, NMAIN], F32)
        hT_h = sb("hT_h", [128, KC, NHALO], F32)
        xn_m = sb("xn_m", [128, KC, NMAIN], BF16)
        xn_h = sb("xn_h", [128, KC, NHALO], BF16)
        ring_t = [sb(f"ring{i}", [128, 4096], BF16) for i in range(NSLOT)]
        cols = sb("cols_sb", [128, NCOL], F32)
        cst = sb("cst_sb", [128, 512], F32)
        ident_b = sb("ident_b", [128, 128], BF16)
        ones_b = sb("ones_b", [128, 128], BF16)
        small = sb("small", [128, 256], F32)
        A = sb("arenaA", [128, 18432], BF16)
        Bq = sb("arenaB", [128, 4608], BF16)
        Vt = sb("Vt", [128, 9, 256], BF16)
        T = sb("arenaT", [128, 6144], BF16)
        ps = st.enter_context(nc.psum_tensor("ps", [128, 8, 512], F32))
        ident_f = cst[:, 0:128]
        triu = cst[:, 128:256]
        maskadd = cst[:, 256:512]

        def psf(b):
            return ps[:, b, :]

        def psb(b):
            return ps[:, b, :].bitcast(BF16)

        def f32view(t, off, n):
            return t[:, off:off + n].bitcast(F32)

        tbuf = [f32view(T, i * 1024, 1024) for i in range(6)]
        sqpool = Pool(tbuf[0:2])
        rspool = Pool(tbuf[2:3])
        tApool = Pool(tbuf[3:5] + tbuf[5:6])
        banks_free = [None] * 8

        class Banks:
            def __init__(self, ids):
                self.ids, self.nxt = ids, 0

            def get(self):
                b = self.ids[self.nxt]
                self.nxt = (self.nxt + 1) % len(self.ids)
                return b, banks_free[b]

            @staticmethod
            def rel(b, tok):
                banks_free[b] = tok

        banks = Banks(list(range(8)))

        sp_sems = [pl.new_sem(f"sp{i}") for i in range(8)]
        sp_n = [0]

        def sp_dma(out_ap, in_ap, deps=()):
            i = sp_n[0]
            sp_n[0] += 1
            si = sp_sems[i % 8]
            k = i // 8
            prev = (si, 16 * k) if k > 0 else None
            pl.op("sp", lambda e: e.dma_start(out=out_ap, in_=in_ap), deps=[prev] + list(deps), inc=(si, 16))
            return (si, 16 * (k + 1))

        ring_sem = [pl.new_sem(f"rg{i}") for i in range(NSLOT)]
        ring_fills = [0] * NSLOT
        ring_free = [None] * NSLOT
        ring_nxt = [0]

        def ring_load(dmas):
            s = ring_nxt[0]
            ring_nxt[0] = (s + 1) % NSLOT
            t = ring_t[s]
            for dst_fn, src in dmas:
                dst = dst_fn(t)
                pl.op("pool", lambda e, dst=dst, src=src: e.dma_start(out=dst, in_=src),
                      deps=[ring_free[s]], inc=(ring_sem[s], 16))
                ring_fills[s] += 1
            return s, t, (ring_sem[s], 16 * ring_fills[s])

        def ring_rel(s, tok):
            ring_free[s] = tok

        def colblock(w2d, c0):
            src = w2d.rearrange("(k p) n -> p k n", p=128)[:, :, c0:c0 + 256]
            return [(lambda t: t[:, :].rearrange("p (k c) -> p k c", k=KC), src)]

        def rowblock(w2d, r0):
            src = w2d[r0:r0 + 256, :].rearrange("(j p) n -> p j n", p=128)
            return [(lambda t: t[:, :].rearrange("p (j n) -> p j n", j=2), src)]

        def cview(t):
            return t[:, :].rearrange("p (k c) -> p k c", k=KC)

        def rview(t):
            return t[:, :].rearrange("p (j n) -> p j n", j=2)

        def hT(kc, g):
            return (hT_h if g.kind == "h" else hT_m)[:, kc, g.t0:g.t0 + g.n]

        def xnT(kc, g):
            return (xn_h if g.kind == "h" else xn_m)[:, kc, g.t0:g.t0 + g.n]

        def xn_tile(kc, t):
            return xn_h[:, kc, :] if t == 0 else xn_m[:, kc, (t - 1) * 128:t * 128]

        state = {"h_tok": None, "xn": {}}

        def xn_tok_g(g):
            return state["xn"].get((g.kind, g.t0))

        def xn_tok_t(t):
            return state["xn"].get(("h", 0)) if t == 0 else state["xn"].get(("m", 0 if t <= 4 else 512))

        t_cols = sp_dma(cols[:, :], cols_d[:, :])
        t_cst = sp_dma(cst[:, :], cst_d[:, :])
        t_idb = pl.op("dve", lambda e: e.tensor_copy(out=ident_b[:, :], in_=ident_f), deps=[t_cst])
        t_ones = pl.op("dve", lambda e: e.memset(ones_b[:, :], 1.0 / D))
        epsc = small[:, 90:91]
        t_eps = pl.op("dve", lambda e: e.memset(epsc, EPS))

        def phase_load():
            tok = None
            n = 0
            for t in range(9):
                for qd in range(4):
                    bi, buf, bfree = tApool.get()
                    t_ld = sp_dma(buf[:, :], xin[t * 128:(t + 1) * 128, qd * 512:(qd + 1) * 512], deps=[bfree])
                    b, bf = banks.get()
                    for j in range(4):
                        t_mm = pl.op("pe", lambda e, b=b, j=j, buf=buf: e.transpose(
                            out=psf(b)[:, j * 128:(j + 1) * 128], in_=buf[:, j * 128:(j + 1) * 128], identity=ident_f),
                            deps=[t_ld, bf, t_cst], ms=(j == 3))
                    tApool.rel(bi, t_mm)
                    if t == 0:
                        dst = hT_h[:, qd * 4:(qd + 1) * 4, :]
                    else:
                        dst = hT_m[:, qd * 4:(qd + 1) * 4, (t - 1) * 128:t * 128]
                    src = psf(b).rearrange("p (j n) -> p j n", j=4)
                    tok = pl.op("dve", lambda e, dst=dst, src=src: e.tensor_copy(out=dst, in_=src), deps=[t_mm])
                    banks.rel(b, tok)
                    n += 1
            state["h_tok"] = tok

        def rstd_group(g):
            n = g.n
            b, bf = banks.get()
            t_mm = None
            for kc in range(KC):
                qi, sq, sqf = sqpool.get()
                sqb = sq.bitcast(BF16)
                t_sq = pl.op("act", lambda e, sqb=sqb, kc=kc: e.activation(out=sqb[:, :n], in_=hT(kc, g), func=AF.Square),
                             deps=[sqf, state["h_tok"]])
                t_mm = pl.op("pe", lambda e, sqb=sqb, kc=kc, b=b: e.matmul(
                    psf(b)[:, :n], lhsT=ones_b[:, :], rhs=sqb[:, :n], start=(kc == 0), stop=(kc == KC - 1)),
                    deps=[t_sq, bf if kc == 0 else None, t_ones], ms=True)
                sqpool.rel(qi, t_mm)
            ri, rs, rsf = rspool.get()
            t_r0 = pl.op("act", lambda e, b=b, rs=rs: e.activation(
                out=rs[:, :n], in_=psf(b)[:, :n], func=AF.Sqrt, bias=epsc, scale=1.0), deps=[t_mm, rsf, t_eps])
            banks.rel(b, t_r0)
            t_r = pl.op("dve", lambda e, rs=rs: e.reciprocal(out=rs[:, :n], in_=rs[:, :n]), deps=[t_r0])
            return ri, rs, t_r

        def rmsnorm(cbase, groups):
            tok = None
            for g in groups:
                n = g.n
                ri, rs, t_r = rstd_group(g)
                for kc in range(KC):
                    tok = pl.op("dve", lambda e, kc=kc, g=g, rs=rs: e.scalar_tensor_tensor(
                        out=xnT(kc, g), in0=hT(kc, g), scalar=cols[:, cbase + kc:cbase + kc + 1], in1=rs[:, :g.n],
                        op0=ALU.mult, op1=ALU.mult), deps=[t_r, t_cols, state["h_tok"]])
                rspool.rel(ri, tok)
                state["xn"][(g.kind, g.t0)] = tok

        def accum(b, g, dc, t_mm, bias_col=None):
            n = g.n
            if bias_col is None:
                tok = pl.op("dve", lambda e: e.tensor_tensor(out=hT(dc, g), in0=psf(b)[:, :n], in1=hT(dc, g), op=ALU.add),
                            deps=[t_mm])
            else:
                tok = pl.op("dve", lambda e: e.scalar_tensor_tensor(
                    out=hT(dc, g), in0=psf(b)[:, :n], scalar=bias_col, in1=hT(dc, g), op0=ALU.add, op1=ALU.add),
                    deps=[t_mm])
            banks.rel(b, tok)
            state["h_tok"] = tok
            return tok

        def phase_ffn(l, groups):
            rmsnorm(C_FFN + 16 * l, groups)
            ntok = NHALO + NMAIN
            hid = [A[:, i * 4 * ntok:(i + 1) * 4 * ntok].rearrange("p (j n) -> p j n", j=4) for i in range(2)]
            sgp = Pool([f32view(A, 2 * 4 * ntok + i * 1024, 1024) for i in range(3)])
            hid_tok = [None, None]
            fen = pl.fence()

            def GU(s):
                hb = hid[s % 2]
                last = None
                for half in range(2):
                    c0 = s * 512 + half * 256
                    sg_, tg_, lg = ring_load(colblock(wg_d[l], c0))
                    su_, tu_, lu = ring_load(colblock(wu_d[l], c0))
                    wg, wu = cview(tg_), cview(tu_)
                    tU = None
                    for j in range(2):
                        jj = half * 2 + j
                        for g in groups:
                            n = g.n
                            bg, fg = banks.get()
                            bu, fu = banks.get()
                            for kc in range(KC):
                                tG = pl.op("pe", lambda e, kc=kc, g=g, bg=bg, j=j, wg=wg: e.matmul(
                                    psf(bg)[:, :g.n], lhsT=wg[:, kc, j * 128:(j + 1) * 128], rhs=xnT(kc, g),
                                    start=(kc == 0), stop=(kc == KC - 1)),
                                    deps=[lg, xn_tok_g(g), fg], ms=(kc == KC - 1))
                            for kc in range(KC):
                                tU = pl.op("pe", lambda e, kc=kc, g=g, bu=bu, j=j, wu=wu: e.matmul(
                                    psf(bu)[:, :g.n], lhsT=wu[:, kc, j * 128:(j + 1) * 128], rhs=xnT(kc, g),
                                    start=(kc == 0), stop=(kc == KC - 1)),
                                    deps=[lu, fu], ms=(kc == KC - 1))
                            si, sgt, sf = sgp.get()
                            t1 = pl.op("act", lambda e, sgt=sgt, bg=bg, n=n: e.activation(
                                out=sgt[:, :n], in_=psf(bg)[:, :n], func=AF.Silu), deps=[tG, sf, fen])
                            banks.rel(bg, t1)
                            t2 = pl.op("dve", lambda e, sgt=sgt, bu=bu, n=n, jj=jj, g=g: e.tensor_tensor(
                                out=hb[:, jj, g.a0:g.a0 + n], in0=psf(bu)[:, :n], in1=sgt[:, :n], op=ALU.mult),
                                deps=[tU, t1, fen])
                            banks.rel(bu, t2)
                            sgp.rel(si, t2)
                            last = t2
                    ring_rel(sg_, tU)
                    ring_rel(su_, tU)
                hid_tok[s % 2] = last

            def DOWN(s):
                hb = hid[s % 2]
                r0 = s * 512
                s0, t0_, l0 = ring_load(rowblock(wd_d[l], r0))
                s1, t1_, l1 = ring_load(rowblock(wd_d[l], r0 + 256))
                wd = [rview(t0_), rview(t1_)]
                tmm = None
                for dc in range(KC):
                    for g in groups:
                        b, bf = banks.get()
                        for jj in range(4):
                            tmm = pl.op("pe", lambda e, jj=jj, g=g, b=b, dc=dc: e.matmul(
                                psf(b)[:, :g.n], lhsT=wd[jj // 2][:, jj % 2, dc * 128:(dc + 1) * 128],
                                rhs=hb[:, jj, g.a0:g.a0 + g.n], start=(jj == 0), stop=(jj == 3)),
                                deps=[l0, l1, hid_tok[s % 2], bf], ms=(jj == 3))
                        accum(b, g, dc, tmm)
                ring_rel(s0, tmm)
                ring_rel(s1, tmm)

            GU(0)
            for s in range(NSC):
                if s + 1 < NSC:
                    GU(s + 1)
                DOWN(s)

        def gelu_from_psum(src, n, hxp, wp, out_fn, extra_deps, sq_on_dve=False):
            hi, hx, hf = hxp.get()
            wi, w, wf = wp.get()
            t_h = pl.op("act", lambda e: e.activation(out=hx[:, :n], in_=src, func=AF.Copy, scale=0.5),
                        deps=[hf] + list(extra_deps))
            if sq_on_dve:
                t_w = pl.op("dve", lambda e: e.scalar_tensor_tensor(
                    out=w[:, :n], in0=hx[:, :n], scalar=4.0 * 0.044715, in1=hx[:, :n], op0=ALU.mult, op1=ALU.mult),
                    deps=[t_h, wf])
            else:
                t_w = pl.op("act", lambda e: e.activation(out=w[:, :n], in_=src, func=AF.Square, scale=math.sqrt(0.044715)),
                            deps=[wf])
            t_t = pl.op("dve", lambda e: e.scalar_tensor_tensor(
                out=w[:, :n], in0=w[:, :n], scalar=1.0, in1=hx[:, :n], op0=ALU.add, op1=ALU.mult), deps=[t_h, t_w])
            t_th = pl.op("act", lambda e: e.activation(out=w[:, :n], in_=w[:, :n], func=AF.Tanh, scale=2.0 * GELU_C),
                         deps=[t_t])
            t_o = out_fn(hx, w, [t_th])
            hxp.rel(hi, t_o)
            wp.rel(wi, t_o)
            return (t_h if sq_on_dve else t_w), t_o

        def phase_gmlp(l):
            i = l
            groups = G01
            rmsnorm(C_MIX + 16 * l, groups)
            fen = pl.fence()
            v = A[:, :].rearrange("p (t d) -> p t d", t=9)
            ntok = NHALO + NMAIN
            gated = [Bq[:, k * 2 * ntok:(k + 1) * 2 * ntok].rearrange("p (j n) -> p j n", j=2) for k in range(2)]
            h256 = [tbuf[3][:, 0:256], tbuf[3][:, 256:512], tbuf[4][:, 0:256]]
            w256 = [tbuf[4][:, 256:512], tbuf[5][:, 0:256], tbuf[5][:, 256:512]]
            hxp, wp = Pool(h256), Pool(w256)
            u256 = Pool([tbuf[0][:, 0:256], tbuf[0][:, 256:512], tbuf[1][:, 0:256]])
            g256 = Pool([tbuf[1][:, 256:512], tbuf[2][:, 0:256]])
            VtF = Vt[:, :, :].rearrange("p a b -> p (a b)")
            WsT = VtF[:, 0:1024].rearrange("p (g t) -> p g t", g=8)
            bbp = Pool([VtF[:, 1024:1280].bitcast(F32), VtF[:, 1280:1536].bitcast(F32)])
            junk = VtF[:, 1536:1792]
            ssv = small[:, 0:72].rearrange("p (t c) -> p t c", t=9)
            ss9 = small[:, 72:81]
            rsv = small[:, 81:90]

            t_z = pl.op("dve", lambda e: e.memset(ssv, 0.0), deps=[fen])
            t_ws = None
            for hh in range(2):
                ti, tb, tf = sqpool.get()
                t_ld = sp_dma(tb[:, :].rearrange("p (g t) -> p g t", g=4), wst_d[i, :, hh * 4:(hh + 1) * 4, :], deps=[tf, fen])
                for gg in range(4):
                    t_ws = pl.op("dve", lambda e, tb=tb, gg=gg, hh=hh: e.tensor_tensor(
                        out=WsT[:, hh * 4 + gg, :], in0=tb[:, gg * 128:(gg + 1) * 128], in1=triu, op=ALU.mult),
                        deps=[t_ld, t_cst, fen])
                sqpool.rel(ti, t_ws)

            for cb in range(8):
                sl, tl, ld = ring_load(colblock(win_d[i], D + cb * 256))
                wv = cview(tl)
                tmm = None
                for t in range(9):
                    b, bf = banks.get()
                    for kc in range(KC):
                        tmm = pl.op("pe", lambda e, kc=kc, t=t, b=b, wv=wv: e.matmul(
                            psf(b)[:, :256], lhsT=xn_tile(kc, t), rhs=wv[:, kc, :], start=(kc == 0), stop=(kc == KC - 1)),
                            deps=[ld, xn_tok_t(t), bf], ms=(kc == KC - 1))

                    def out_fn(hx, w, deps, t=t, cb=cb):
                        t_v = pl.op("dve", lambda e: e.scalar_tensor_tensor(
                            out=v[:, t, cb * 256:(cb + 1) * 256], in0=w[:, :256], scalar=1.0, in1=hx[:, :256],
                            op0=ALU.add, op1=ALU.mult), deps=deps + [fen])
                        t_s = pl.op("act", lambda e: e.activation(
                            out=junk, in_=v[:, t, cb * 256:(cb + 1) * 256], func=AF.Square,
                            accum_out=ssv[:, t, cb:cb + 1]), deps=[t_v, t_z])
                        return t_s
                    t_w, t_o = gelu_from_psum(psf(b)[:, :256], 256, hxp, wp, out_fn, [tmm], sq_on_dve=True)
                    banks.rel(b, t_w)
                ring_rel(sl, tmm)
            t_a = pl.op("dve", lambda e: e.tensor_reduce(out=ss9, in_=ssv, axis=AX.X, op=ALU.add), deps=[pl.last("act")])
            t_b = pl.op("dve", lambda e: e.tensor_scalar(out=ss9, in0=ss9, scalar1=1.0 / D, scalar2=EPS,
                                                       op0=ALU.mult, op1=ALU.add), deps=[t_a])
            t_c0 = pl.op("act", lambda e: e.activation(out=rsv, in_=ss9, func=AF.Sqrt), deps=[t_b])
            t_c = pl.op("dve", lambda e: e.reciprocal(out=rsv, in_=rsv), deps=[t_c0])
            t_vn = None
            for t in range(9):
                t_vn = pl.op("act", lambda e, t=t: e.activation(out=v[:, t, :], in_=v[:, t, :], func=AF.Copy,
                                                              scale=rsv[:, t:t + 1]), deps=[t_c])

            gated_tok = [None, None]

            def UG(g8):
                gb = gated[g8 % 2]
                slU, tlU, ldU = ring_load(colblock(win_d[i], g8 * 256))
                wu = cview(tlU)
                bi, bb, bbf = bbp.get()
                t_bb = sp_dma(bb, bs_d[i, g8, :].partition_broadcast(128), deps=[bbf])
                last = None
                tmm = None
                for j in range(2):
                    dcg = 2 * g8 + j
                    for g in groups:
                        n = g.n
                        b, bf = banks.get()
                        for kc in range(KC):
                            tmm = pl.op("pe", lambda e, kc=kc, g=g, b=b, j=j: e.matmul(
                                psf(b)[:, :g.n], lhsT=wu[:, kc, j * 128:(j + 1) * 128], rhs=xnT(kc, g),
                                start=(kc == 0), stop=(kc == KC - 1)), deps=[ldU, bf], ms=(kc == KC - 1))
                        b2, bf2 = banks.get()
                        tmm2 = None
                        for ti, t in enumerate(g.tiles):
                            tmm2 = pl.op("pe", lambda e, ti=ti, t=t, b2=b2, dcg=dcg: e.matmul(
                                psf(b2)[:, ti * 128:(ti + 1) * 128], lhsT=v[:, t, dcg * 128:(dcg + 1) * 128],
                                rhs=WsT[:, g8, :], start=True, stop=True),
                                deps=[t_vn, t_ws, bf2], ms=(ti == len(g.tiles) - 1))
                        pieces = [(c, min(256, n - c)) for c in range(0, n, 256)]
                        for pi, (c0, cn) in enumerate(pieces):
                            ui, ut, uf = u256.get()

                            def out_fn(hx, w, deps, ut=ut, cn=cn):
                                return pl.op("dve", lambda e: e.scalar_tensor_tensor(
                                    out=ut[:, :cn], in0=w[:, :cn], scalar=1.0, in1=hx[:, :cn],
                                    op0=ALU.add, op1=ALU.mult), deps=deps + [uf])
                            t_w, t_u = gelu_from_psum(psf(b)[:, c0:c0 + cn], cn, hxp, wp, out_fn, [tmm])
                            if pi == len(pieces) - 1:
                                banks.rel(b, t_w)
                            gi, gt, gf = g256.get()
                            nt = cn // 128
                            t_g1 = pl.op("dve", lambda e, gt=gt, c0=c0, cn=cn, nt=nt, b2=b2, dcg=dcg, bb=bb: e.scalar_tensor_tensor(
                                out=gt[:, :cn].rearrange("p (a t) -> p a t", a=nt),
                                in0=psf(b2)[:, c0:c0 + cn].rearrange("p (a t) -> p a t", a=nt),
                                scalar=cols[:, C_NV + 16 * i + dcg:C_NV + 16 * i + dcg + 1],
                                in1=bb.unsqueeze(1).broadcast_to([128, nt, 128]),
                                op0=ALU.mult, op1=ALU.add), deps=[tmm2, t_bb, gf, t_cols])
                            if pi == len(pieces) - 1:
                                banks.rel(b2, t_g1)
                            t_g2 = pl.op("dve", lambda e, gt=gt, ut=ut, cn=cn, c0=c0, g=g, j=j: e.tensor_tensor(
                                out=gb[:, j, g.a0 + c0:g.a0 + c0 + cn], in0=gt[:, :cn], in1=ut[:, :cn], op=ALU.mult),
                                deps=[t_g1, t_u])
                            u256.rel(ui, t_g2)
                            g256.rel(gi, t_g2)
                            last = t_g2
                ring_rel(slU, tmm)
                bbp.rel(bi, last)
                gated_tok[g8 % 2] = last

            def OUT(g8):
                gb = gated[g8 % 2]
                slO, tlO, ldO = ring_load(rowblock(wout_d[i], g8 * 256))
                wo = rview(tlO)
                tmm = None
                for dc in range(KC):
                    for g in groups:
                        b, bf = banks.get()
                        for j in range(2):
                            tmm = pl.op("pe", lambda e, j=j, g=g, b=b, dc=dc: e.matmul(
                                psf(b)[:, :g.n], lhsT=wo[:, j, dc * 128:(dc + 1) * 128], rhs=gb[:, j, g.a0:g.a0 + g.n],
                                start=(j == 0), stop=(j == 1)), deps=[ldO, gated_tok[g8 % 2], bf], ms=(j == 1))
                        accum(b, g, dc, tmm)
                ring_rel(slO, tmm)

            UG(0)
            for g8 in range(8):
                if g8 + 1 < 8:
                    UG(g8 + 1)
                OUT(g8)

        KT = Bq[:, :].rearrange("p (m n) -> p m n", m=4)

        def phase_kv():
            groups = G01
            rmsnorm(C_KV, groups)
            fen = pl.fence()
            bv = tbuf[3][:, 0:256]
            t_bv = sp_dma(bv, bkv_d[256:512].partition_broadcast(128), deps=[fen])
            for half in range(2):
                dm = []
                for h2 in range(2):
                    src = wkv_d[:, (half * 2 + h2) * 64:(half * 2 + h2 + 1) * 64].rearrange("(k p) c -> p k c", p=128)
                    for dup in range(2):
                        dm.append((lambda t, dup=dup, h2=h2: t[:, :].rearrange(
                            "p (k h u c) -> p k h u c", k=KC, h=2, u=2)[:, :, h2, dup, :], src))
                sl, tl, ld = ring_load(dm)
                wk = cview(tl)
                tmm = None
                for hh in range(2):
                    m = half * 2 + hh
                    for g in groups:
                        b, bf = banks.get()
                        for kc in range(KC):
                            tmm = pl.op("pe", lambda e, kc=kc, g=g, b=b, hh=hh, wk=wk: e.matmul(
                                psf(b)[:, :g.n], lhsT=wk[:, kc, hh * 128:(hh + 1) * 128], rhs=xnT(kc, g),
                                start=(kc == 0), stop=(kc == KC - 1)), deps=[ld, xn_tok_g(g), bf], ms=(kc == KC - 1))
                        t_e = pl.op("act", lambda e, g=g, b=b, m=m: e.activation(
                            out=KT[:, m, g.a0:g.a0 + g.n], in_=psf(b)[:, :g.n], func=AF.Identity,
                            bias=cols[:, C_BK + m:C_BK + m + 1], scale=1.0), deps=[tmm, t_cols, fen])
                        banks.rel(b, t_e)
                ring_rel(sl, tmm)
            sl, tl, ld = ring_load(colblock(wkv_d, 256))
            wv = cview(tl)
            tmm = None
            for t in range(9):
                b, bf = banks.get()
                for kc in range(KC):
                    tmm = pl.op("pe", lambda e, kc=kc, t=t, b=b: e.matmul(
                        psf(b)[:, :256], lhsT=xn_tile(kc, t), rhs=wv[:, kc, :], start=(kc == 0), stop=(kc == KC - 1)),
                        deps=[ld, bf], ms=(kc == KC - 1))
                t_e = pl.op("dve", lambda e, t=t, b=b: e.tensor_tensor(out=Vt[:, t, :], in0=psf(b)[:, :256], in1=bv, op=ALU.add),
                            deps=[tmm, t_bv, fen])
                banks.rel(b, t_e)
            ring_rel(sl, tmm)

        def phase_attn(l):
            i = l - 2
            groups = G23
            rmsnorm(C_MIX + 16 * l, groups)
            fen = pl.fence()
            attnT = A[:, 0:16384].rearrange("p (k n) -> p k n", k=KC)
            qTz = [A[:, 16384:17408], A[:, 17408:18432]]
            t_qz = pl.op("dve", lambda e: e.memset(A[:, 16384:18432], 0.0), deps=[fen])
            hh_f = hT_h[:, :, :].rearrange("p k n -> p (k n)")
            braw = hh_f[:, 0:512].rearrange("p (e k) -> p e k", e=2)
            biasp = [hh_f[:, 512:1024].rearrange("p (e k) -> p e k", e=2), hh_f[:, 1024:1536].rearrange("p (e k) -> p e k", e=2)]
            bias0 = hh_f[:, 1536:2048].rearrange("p (e k) -> p e k", e=2)
            scp = Pool([tbuf[3][:, :].rearrange("p (e k) -> p e k", e=2), tbuf[4][:, :].rearrange("p (e k) -> p e k", e=2)])
            xh_b = xn_h[:, :, :].rearrange("p k n -> p (k n)")
            pp = Pool([xh_b[:, k * 512:(k + 1) * 512].rearrange("p (e k) -> p e k", e=2) for k in range(2)])
            pTp = Pool([xh_b[:, 1024 + k * 512:1024 + (k + 1) * 512].rearrange("p (a q) -> p a q", a=4) for k in range(2)])
            t5b = tbuf[5][:, :].bitcast(BF16)
            op_ = Pool([t5b[:, k * 128:(k + 1) * 128] for k in range(3)])
            sink_bc = small[:, 96:128]
            mask0 = tbuf[2][:, 0:256]
            t_sk = sp_dma(sink_bc, sink_d[i, :].partition_broadcast(128), deps=[fen])
            t_m0 = sp_dma(mask0, mask0_d[:, :], deps=[fen])
            sm = small[:, 128:256]
            smp = Pool([sm[:, k * 16:(k + 1) * 16] for k in range(8)])
            poolS, poolT, poolO, poolX = Banks([0, 1]), Banks([2, 3]), Banks([4, 5]), Banks([6, 7])
            bias_tok = [None, None]
            bias0_tok = [None]
            braw_free = [None]
            stt = {}
            wq_slot = [None]

            def SA(it):
                hp, qb = it
                c = stt.setdefault(it, {})
                if qb == 0:
                    if hp % 2 == 0:
                        if wq_slot[0] is not None:
                            ring_rel(wq_slot[0][0], wq_slot[0][3])
                        sl, tl, ld = ring_load(colblock(wq_d[i], (hp // 2) * 256))
                        wq_slot[0] = [sl, cview(tl), ld, None]
                    wq = wq_slot[0][1]
                    ld = wq_slot[0][2]
                    t_q = None
                    for g in groups:
                        b, bf = poolS.get()
                        tmm = None
                        for kc in range(KC):
                            tmm = pl.op("pe", lambda e, kc=kc, g=g, b=b: e.matmul(
                                psf(b)[:, :g.n], lhsT=wq[:, kc, (hp % 2) * 128:(hp % 2 + 1) * 128], rhs=xnT(kc, g),
                                start=(kc == 0), stop=(kc == KC - 1)), deps=[ld, xn_tok_g(g), bf], ms=(kc == KC - 1))
                        wq_slot[0][3] = tmm
                        for e2 in range(2):
                            t_q = pl.op("act", lambda e, g=g, b=b, e2=e2: e.activation(
                                out=qTz[e2][e2 * 64:(e2 + 1) * 64, g.t0:g.t0 + g.n], in_=psf(b)[e2 * 64:(e2 + 1) * 64, :g.n],
                                func=AF.Identity, bias=cols[e2 * 64:(e2 + 1) * 64, C_BQ + 16 * i + hp:C_BQ + 16 * i + hp + 1],
                                scale=1.0), deps=[tmm, t_cols, fen, t_qz])
                        Banks.rel(b, t_q)
                    stt["q_tok"] = t_q
                    t_ld = sp_dma(braw, biasg_d[:, 2 * hp:2 * hp + 2, :], deps=[braw_free[0], fen])
                    t_b0 = pl.op("dve", lambda e: e.tensor_tensor(
                        out=bias0, in0=braw, in1=mask0.unsqueeze(1).broadcast_to([128, 2, 256]), op=ALU.add),
                        deps=[t_ld, t_m0, bias0_tok[0]])
                    bp = biasp[hp % 2]
                    t_b1 = pl.op("dve", lambda e: e.tensor_tensor(
                        out=bp, in0=braw, in1=maskadd.unsqueeze(1).broadcast_to([128, 2, 256]), op=ALU.add),
                        deps=[t_ld, t_cst, bias_tok[hp % 2]])
                    braw_free[0] = t_b1
                    stt["b0"] = t_b0
                    stt["b1"] = t_b1
                c["b0"], c["b1"] = stt["b0"], stt["b1"]
                kvh = hp // 4
                b, bf = poolS.get()
                tmm = None
                for e2 in range(2):
                    tmm = pl.op("pe", lambda e, e2=e2, b=b: e.matmul(
                        psf(b)[:, e2 * 256:(e2 + 1) * 256], lhsT=qTz[e2][:, qb * 128:(qb + 1) * 128],
                        rhs=KT[:, kvh, qb * 128:qb * 128 + 256], start=True, stop=True),
                        deps=[stt["q_tok"], bf, fen], ms=(e2 == 1))
                c["bS"], c["tS"] = b, tmm

            def SB(it):
                hp, qb = it
                c = stt[it]
                b = c["bS"]
                si, sc, sf = scp.get()
                bsel = bias0 if qb == 0 else biasp[hp % 2]
                t_sc = pl.op("dve", lambda e: e.scalar_tensor_tensor(
                    out=sc, in0=psf(b)[:, :].rearrange("p (e k) -> p e k", e=2), scalar=0.125, in1=bsel,
                    op0=ALU.mult, op1=ALU.add), deps=[c["tS"], sf, c["b0"], c["b1"]])
                Banks.rel(b, t_sc)
                if qb == 0:
                    bias0_tok[0] = t_sc
                bias_tok[hp % 2] = t_sc
                mi, ms_, mf = smp.get()
                mx, negm, dd, rs, den, rden = (ms_[:, 0:2], ms_[:, 2:4], ms_[:, 4:6], ms_[:, 6:8], ms_[:, 8:10], ms_[:, 10:12])
                sk = sink_bc[:, 2 * hp:2 * hp + 2]
                t1 = pl.op("dve", lambda e: e.tensor_reduce(out=mx, in_=sc, axis=AX.X, op=ALU.max), deps=[t_sc, mf])
                t2 = pl.op("dve", lambda e: e.tensor_tensor(out=mx, in0=mx, in1=sk, op=ALU.max), deps=[t1, t_sk])
                t3 = pl.op("dve", lambda e: e.tensor_scalar(out=negm, in0=mx, scalar1=-1.0, scalar2=None, op0=ALU.mult), deps=[t2])
                t4 = pl.op("dve", lambda e: e.tensor_tensor(out=dd, in0=sk, in1=negm, op=ALU.add), deps=[t3])
                t_z = pl.op("dve", lambda e: e.memset(rs, 0.0), deps=[mf])
                c.update(si=si, sc=sc, mi=mi, negm=negm, dd=dd, rs=rs, den=den, rden=rden, tB=[t3, t4, t_z])

            def SC(it):
                c = stt[it]
                sc, negm, rs, dd = c["sc"], c["negm"], c["rs"], c["dd"]
                pi, p, pf = pp.get()
                t_e = None
                for e2 in range(2):
                    t_e = pl.op("act", lambda e, e2=e2: e.activation(
                        out=p[:, e2, :], in_=sc[:, e2, :], func=AF.Exp, bias=negm[:, e2:e2 + 1], scale=1.0,
                        accum_out=rs[:, e2:e2 + 1]), deps=c["tB"] + [pf])
                scp.rel(c["si"], t_e)
                t5 = pl.op("act", lambda e: e.activation(out=dd, in_=dd, func=AF.Exp), deps=c["tB"])
                c.update(pi=pi, p=p, tC=t5, tE=t_e)

            def SD(it):
                c = stt[it]
                p, den, rden, rs, dd = c["p"], c["den"], c["rden"], c["rs"], c["dd"]
                bT, bfT = poolT.get()
                tmm = None
                for e2 in range(2):
                    for kb in range(2):
                        a = e2 * 2 + kb
                        tmm = pl.op("pe", lambda e, a=a, e2=e2, kb=kb: e.transpose(
                            out=psb(bT)[:, a * 128:(a + 1) * 128], in_=p[:, e2, kb * 128:(kb + 1) * 128], identity=ident_b[:, :]),
                            deps=[c["tE"], bfT, t_idb], ms=(a == 3))
                pp.rel(c["pi"], tmm)
                c["bT"], c["tT"] = bT, tmm
                t6 = pl.op("dve", lambda e: e.tensor_tensor(out=den, in0=rs, in1=dd, op=ALU.add), deps=[c["tC"], c["tE"]])
                t7 = pl.op("dve", lambda e: e.reciprocal(out=rden, in_=den), deps=[t6])
                c["t_rden"] = t7

            def SE(it):
                c = stt[it]
                bT = c["bT"]
                ti, pT, tf = pTp.get()
                t_c = pl.op("act", lambda e: e.activation(
                    out=pT, in_=psb(bT)[:, 0:512].rearrange("p (a q) -> p a q", a=4), func=AF.Copy), deps=[c["tT"], tf])
                Banks.rel(bT, t_c)
                c.update(ti=ti, pT=pT, tPT=t_c)

            def SF(it):
                hp, qb = it
                c = stt[it]
                kvh = hp // 4
                pT = c["pT"]
                bO, bfO = poolO.get()
                tmm = None
                for e2 in range(2):
                    for kb in range(2):
                        tmm = pl.op("pe", lambda e, e2=e2, kb=kb: e.matmul(
                            psf(bO)[:, e2 * 64:(e2 + 1) * 64], lhsT=pT[:, e2 * 2 + kb, :],
                            rhs=Vt[:, qb + kb, kvh * 64:(kvh + 1) * 64], start=(kb == 0), stop=(kb == 1)),
                            deps=[c["tPT"], bfO, fen], ms=(e2 == 1 and kb == 1))
                pTp.rel(c["ti"], tmm)
                c["bO"], c["tO"] = bO, tmm

            def SG(it):
                c = stt[it]
                bO = c["bO"]
                oi, o, of = op_.get()
                t_o = pl.op("dve", lambda e: e.tensor_tensor(
                    out=o.rearrange("p (e d) -> p e d", e=2), in0=psf(bO)[:, 0:128].rearrange("p (e d) -> p e d", e=2),
                    in1=c["rden"].unsqueeze(2).broadcast_to([128, 2, 64]), op=ALU.mult), deps=[c["tO"], c["t_rden"], of])
                Banks.rel(bO, t_o)
                smp.rel(c["mi"], t_o)
                c.update(oi=oi, o=o, t_o=t_o)

            def SH(it):
                c = stt[it]
                o = c["o"]
                bX, bfX = poolX.get()
                tmm = pl.op("pe", lambda e: e.transpose(out=psb(bX)[:, 0:128], in_=o, identity=ident_b[:, :]),
                            deps=[c["t_o"], bfX], ms=True)
                op_.rel(c["oi"], tmm)
                c["bX"], c["tX"] = bX, tmm

            def SI(it):
                hp, qb = it
                c = stt[it]
                bX = c["bX"]
                t_a = pl.op("act", lambda e: e.activation(out=attnT[:, hp, qb * 128:(qb + 1) * 128], in_=psb(bX)[:, 0:128],
                                                        func=AF.Copy), deps=[c["tX"], fen])
                Banks.rel(bX, t_a)
                stt["attn_tok"] = t_a
                del stt[it]

            its = [(hp, qb) for hp in range(16) for qb in range(8)]
            stages = [SA, SB, SC, SD, SE, SF, SG, SH, SI]
            for k in range(1, 9):
                if f"st{k}" in dbg:
                    stages = stages[:k]
                    its = its[:16]
            for step in range(len(its) + len(stages) - 1):
                for sidx in reversed(range(len(stages))):
                    k = step - sidx
                    if 0 <= k < len(its):
                        stages[sidx](its[k])
            ring_rel(wq_slot[0][0], wq_slot[0][3])
            if len(stages) < 9:
                return
            for cbk in range(8):
                sl, tl, ld = ring_load(colblock(wo_d[i], cbk * 256))
                wo = cview(tl)
                tmm = None
                for j in range(2):
                    dc = cbk * 2 + j
                    for g in groups:
                        b, bf = banks.get()
                        for kc in range(KC):
                            tmm = pl.op("pe", lambda e, kc=kc, g=g, b=b, j=j, wo=wo: e.matmul(
                                psf(b)[:, :g.n], lhsT=wo[:, kc, j * 128:(j + 1) * 128], rhs=attnT[:, kc, g.t0:g.t0 + g.n],
                                start=(kc == 0), stop=(kc == KC - 1)), deps=[ld, stt["attn_tok"], bf], ms=(kc == KC - 1))
                        accum(b, g, dc, tmm, bias_col=cols[:, C_BO + 16 * i + dc:C_BO + 16 * i + dc + 1])
                ring_rel(sl, tmm)

        st_sems = [pl.new_sem("st0"), pl.new_sem("st1")]
        n_st = [0, 0]

        def phase_out(final):
            groups = G23 if final else G01
            fen = pl.fence()
            fT = [f32view(A, k * 1024, 1024) for k in range(4)]
            ostp = Pool([f32view(A, (4 + k) * 1024, 1024) for k in range(2)])
            fT_free = [None] * 4
            row0 = 0 if final else None
            for g in groups:
                n = g.n
                if final:
                    ri, rs, t_r = rstd_group(g)
                for kq in range(4):
                    srcs = []
                    if final:
                        for j in range(4):
                            kc = kq * 4 + j
                            t_f = pl.op("dve", lambda e, kc=kc, j=j, g=g, rs=rs: e.scalar_tensor_tensor(
                                out=fT[j][:, :g.n], in0=hT(kc, g), scalar=cols[:, C_FIN + kc:C_FIN + kc + 1], in1=rs[:, :g.n],
                                op0=ALU.mult, op1=ALU.mult), deps=[t_r, fT_free[j], fen, t_cols])
                            srcs.append((fT[j], 0, t_f))
                    else:
                        for j in range(4):
                            kc = kq * 4 + j
                            base = hT_h if g.kind == "h" else hT_m
                            srcs.append((base[:, kc, :], g.t0, state["h_tok"]))
                    for ti in range(n // 128):
                        b, bf = banks.get()
                        tmm = None
                        for j in range(4):
                            s_ap, off, s_tok = srcs[j]
                            tmm = pl.op("pe", lambda e, s_ap=s_ap, off=off, j=j, ti=ti, b=b: e.transpose(
                                out=psf(b)[:, j * 128:(j + 1) * 128], in_=s_ap[:, off + ti * 128:off + (ti + 1) * 128],
                                identity=ident_f), deps=[s_tok, bf, t_cst], ms=(j == 3))
                        oi, ost, of = ostp.get()
                        t_c = pl.op("act", lambda e, ost=ost, b=b: e.activation(out=ost[:, :], in_=psf(b), func=AF.Copy),
                                    deps=[tmm, of, fen])
                        banks.rel(b, t_c)
                        if final:
                            r = g.t0 + ti * 128
                        else:
                            r = g.a0 + ti * 128
                        pl.op("sp", lambda e, ost=ost, r=r, kq=kq: e.dma_start(
                            out=out[r:r + 128, kq * 512:(kq + 1) * 512], in_=ost[:, :]), deps=[t_c], inc=(st_sems[oi], 16))
                        n_st[oi] += 1
                        ostp.rel(oi, (st_sems[oi], 16 * n_st[oi]))
                    if final:
                        for j in range(4):
                            fT_free[j] = tmm
                if final:
                    rspool.rel(ri, pl.last("dve"))
            pl.wait("sp", (st_sems[0], 16 * n_st[0]))
            pl.wait("sp", (st_sems[1], 16 * n_st[1]))

        phase_load()
        for l in layers:
            if l < 2:
                phase_gmlp(l)
                phase_ffn(l, G01)
            else:
                if l == 2:
                    phase_kv()
                if "kvonly" not in dbg:
                    phase_attn(l)
                if "noffn" not in dbg:
                    phase_ffn(l, G23)
        phase_out(is_last)

        with nc.Block() as block:
            @block.tensor
            def _(e):
                pl.replay("pe", e)

            @block.scalar
            def _(e):
                pl.replay("act", e)

            @block.vector
            def _(e):
                pl.replay("dve", e)

            @block.gpsimd
            def _(e):
                pl.replay("pool", e)

            @block.sync
            def _(e):
                pl.replay("sp", e)
    return nc


def _colize(v):
    return np.ascontiguousarray(np.asarray(v, np.float32).reshape(-1, 128).T)


def _t5_bucket(dist):
    max_exact = 16
    d = np.maximum(dist, 1).astype(np.float32)
    large = max_exact + (np.log(d / np.float32(max_exact)) / np.float32(math.log(128 / max_exact))
                         * np.float32(32 - max_exact)).astype(np.int32)
    large = np.minimum(large, 31)
    return np.where(dist < max_exact, dist, large)


def _host_tables(inp):
    cols = np.zeros((128, NCOL), np.float32)
    for l in range(4):
        cols[:, C_MIX + 16 * l:C_MIX + 16 * (l + 1)] = _colize(inp["mix_norm"][l])
        cols[:, C_FFN + 16 * l:C_FFN + 16 * (l + 1)] = _colize(inp["ffn_norm"][l])
    cols[:, C_KV:C_KV + 16] = _colize(inp["kv_norm"])
    cols[:, C_FIN:C_FIN + 16] = _colize(inp["final_norm"])
    for i in range(2):
        cols[:, C_NV + 16 * i:C_NV + 16 * (i + 1)] = _colize(inp["a_norm_v"][i])
        cols[:, C_BQ + 16 * i:C_BQ + 16 * (i + 1)] = _colize(inp["b_b_q"][i])
        cols[:, C_BO + 16 * i:C_BO + 16 * (i + 1)] = _colize(inp["b_b_o"][i])
    bk = np.asarray(inp["b_kv"], np.float32)[:256].reshape(4, 64)
    cols[:, C_BK:C_BK + 4] = np.concatenate([bk, bk], axis=1).T
    cst = np.zeros((128, 512), np.float32)
    cst[:, 0:128] = np.eye(128, dtype=np.float32)
    s = np.arange(128)[:, None]
    t = np.arange(128)[None, :]
    cst[:, 128:256] = (s <= t).astype(np.float32)
    dist = np.arange(128)[:, None] + 128 - np.arange(256)[None, :]
    in_window = (dist >= 0) & (dist < 128)
    cst[:, 256:512] = np.where(in_window, 0.0, MASKV).astype(np.float32)
    mask0_first = np.where(in_window & (np.arange(256)[None, :] >= 128), 0.0, MASKV).astype(np.float32)
    mask0_second = cst[:, 256:512].copy()
    bucket = _t5_bucket(np.clip(dist, 0, None).astype(np.int32))
    rel = np.asarray(inp["rel_bias"], np.float32)
    biasg = np.ascontiguousarray(rel[bucket].transpose(0, 2, 1))
    return cols, cst, mask0_first, mask0_second, biasg


def _core_x(x, c):
    b, half = c // 2, c % 2
    xm = x[b, half * NMAIN:(half + 1) * NMAIN]
    if half == 0:
        halo = np.zeros((NHALO, D), np.float32)
    else:
        halo = x[b, NMAIN - NHALO:NMAIN]
    return np.ascontiguousarray(np.concatenate([halo, xm], axis=0))


_NC_CACHE = {}


def _get_nc(layers, is_first, is_last):
    key = (tuple(layers), is_first, is_last)
    if key not in _NC_CACHE:
        _NC_CACHE[key] = build(layers, is_first, is_last)
    return _NC_CACHE[key]


def _in_map(inp, tabs, xin, c, layers):
    cols, cst, m0f, m0s, biasg = tabs
    m = {"xin": xin, "cols": cols, "cst": cst,
         "ffn_w_gate": inp["ffn_w_gate"], "ffn_w_up": inp["ffn_w_up"], "ffn_w_down": inp["ffn_w_down"]}
    if any(l < 2 for l in layers):
        m["a_w_in"] = inp["a_w_in"]
        m["a_w_sT"] = inp["_a_w_sT"]
        m["a_b_s"] = inp["a_b_s"]
        m["a_w_out"] = inp["a_w_out"]
    if any(l >= 2 for l in layers):
        m["w_kv"] = inp["w_kv"]
        m["b_kv"] = inp["b_kv"]
        m["b_w_q"] = inp["b_w_q"]
        m["b_w_o"] = inp["b_w_o"]
        m["b_sinks"] = inp["b_sinks"]
        m["biasg"] = biasg
        m["mask0"] = m0f if c % 2 == 0 else m0s
    return m


LAUNCHES = [((0, 1, 2, 3), True, True)]


def kernel(**inputs):
    inp = {k: np.ascontiguousarray(np.asarray(v, np.float32)) for k, v in inputs.items()}
    inp["_a_w_sT"] = np.ascontiguousarray(inp["a_w_s"].transpose(0, 3, 1, 2))
    tabs = _host_tables(inp)
    ncores = 8
    cur = [_core_x(inp["x"], c) for c in range(ncores)]
    for layers, is_first, is_last in LAUNCHES:
        nc = _get_nc(layers, is_first, is_last)
        in_maps = [_in_map(inp, tabs, cur[c], c, layers) for c in range(ncores)]
        res = run_bass_kernel_spmd(nc, in_maps, core_ids=list(range(ncores)))
        cur = [np.asarray(res.results[c]["out"], np.float32) for c in range(ncores)]
    outp = np.zeros((4, 2 * NMAIN, D), np.float32)
    for c in range(ncores):
        outp[c // 2, (c % 2) * NMAIN:(c % 2 + 1) * NMAIN] = cur[c]
    return outp
```

```python
import math
from contextlib import ExitStack

import numpy as np
import concourse.bass as bass
import concourse.mybir as mybir
from concourse.bass_utils import run_bass_kernel_spmd

F32 = mybir.dt.float32
BF16 = mybir.dt.bfloat16
AF = mybir.ActivationFunctionType
ALU = mybir.AluOpType
AX = mybir.AxisListType

D = 2048
KC = 16
DFF = 5632
NSC = 11
EPS = 1e-5
NMAIN = 1024
NHALO = 128
NSLOT = 4
MASKV = -30000.0

C_MIX = 0
C_FFN = 64
C_KV = 128
C_FIN = 144
C_NV = 160
C_BQ = 192
C_BO = 224
C_BK = 256
NCOL = 260

GELU_C = math.sqrt(2.0 / math.pi)


class Grp:
    def __init__(self, kind, t0, n):
        self.kind, self.t0, self.n = kind, t0, n
        self.a0 = t0 if kind == "h" else NHALO + t0
        self.tiles = [self.a0 // 128 + i for i in range(n // 128)]


G01 = [Grp("h", 0, 128), Grp("m", 0, 512), Grp("m", 512, 512)]
G23 = [Grp("m", 0, 512), Grp("m", 512, 512)]


class Plan:
    ENG = ("pe", "act", "dve", "pool", "sp")

    def __init__(self, nc, stack):
        self.nc, self.stack = nc, stack
        self.sems = []
        self.lists = {e: [] for e in self.ENG}
        self.waited = {e: {} for e in self.ENG}
        self.cnt = {}
        self.esem = {}
        for e in self.ENG:
            self.esem[e] = self.new_sem("s_" + e)
            self.cnt[e] = 0

    def new_sem(self, name):
        s = self.stack.enter_context(self.nc.semaphore(name))
        self.sems.append(s)
        return len(self.sems) - 1

    def wait(self, eng, tok):
        if tok is None:
            return
        si, val = tok
        if eng == "pe" and si == self.esem["pe"]:
            return
        if self.waited[eng].get(si, 0) >= val:
            return
        self.waited[eng][si] = val
        self.lists[eng].append(("w", si, val))

    def op(self, eng, fn, deps=(), ms=True, inc=None):
        for d in deps:
            if isinstance(d, list):
                for dd in d:
                    self.wait(eng, dd)
            else:
                self.wait(eng, d)
        if inc is not None:
            self.lists[eng].append(("o", fn, inc[0], inc[1]))
            return None
        if ms:
            self.cnt[eng] += 1
            self.lists[eng].append(("o", fn, self.esem[eng], 1))
            return (self.esem[eng], self.cnt[eng])
        self.lists[eng].append(("o", fn, None, 0))
        return None

    def last(self, eng):
        return (self.esem[eng], self.cnt[eng]) if self.cnt[eng] else None

    def fence(self):
        return [self.last("pe"), self.last("act"), self.last("dve")]

    def replay(self, eng, e):
        for it in self.lists[eng]:
            if it[0] == "w":
                e.wait_ge(self.sems[it[1]], it[2])
            else:
                ins = it[1](e)
                if it[2] is not None:
                    ins.then_inc(self.sems[it[2]], it[3])


class Pool:
    def __init__(self, aps):
        self.aps = aps
        self.free = [None] * len(aps)
        self.nxt = 0

    def get(self):
        i = self.nxt
        self.nxt = (i + 1) % len(self.aps)
        return i, self.aps[i], self.free[i]

    def rel(self, i, tok):
        self.free[i] = tok


def build(layers, is_first, is_last, dbg=()):
    layers = tuple(layers)
    do_g = any(l < 2 for l in layers)
    do_a = any(l >= 2 for l in layers)
    nc = bass.Bass("TRN2", target_bir_lowering=False)

    def dram(name, shape, kind="ExternalInput"):
        return nc.dram_tensor(name, list(shape), F32, kind=kind).ap()

    xin = dram("xin", [NHALO + NMAIN, D])
    n_out = NMAIN if is_last else NHALO + NMAIN
    out = dram("out", [n_out, D], kind="ExternalOutput")
    cols_d = dram("cols", [128, NCOL])
    cst_d = dram("cst", [128, 512])
    wg_d = dram("ffn_w_gate", [4, D, DFF])
    wu_d = dram("ffn_w_up", [4, D, DFF])
    wd_d = dram("ffn_w_down", [4, DFF, D])
    if do_g:
        win_d = dram("a_w_in", [2, D, 2 * D])
        wst_d = dram("a_w_sT", [2, 128, 8, 128])
        bs_d = dram("a_b_s", [2, 8, 128])
        wout_d = dram("a_w_out", [2, D, D])
    if do_a:
        wkv_d = dram("w_kv", [D, 512])
        bkv_d = dram("b_kv", [512])
        wq_d = dram("b_w_q", [2, D, D])
        wo_d = dram("b_w_o", [2, D, D])
        sink_d = dram("b_sinks", [2, 32])
        biasg_d = dram("biasg", [128, 32, 256])
        mask0_d = dram("mask0", [128, 256])

    with ExitStack() as st:
        pl = Plan(nc, st)

        def sb(name, shape, dt):
            return st.enter_context(nc.sbuf_tensor(name, list(shape), dt))

        hT_m = sb("hT_m", [128, KC, NMAIN], F32)
        hT_h = sb("hT_h", [128, KC, NHALO], F32)
        xn_m = sb("xn_m", [128, KC, NMAIN], BF16)
        xn_h = sb("xn_h", [128, KC, NHALO], BF16)
        ring_t = [sb(f"ring{i}", [128, 4096], BF16) for i in range(NSLOT)]
        cols = sb("cols_sb", [128, NCOL], F32)
        cst = sb("cst_sb", [128, 512], F32)
        ident_b = sb("ident_b", [128, 128], BF16)
        ones_b = sb("ones_b", [128, 128], BF16)
        small = sb("small", [128, 256], F32)
        A = sb("arenaA", [128, 18432], BF16)
        Bq = sb("arenaB", [128, 4608], BF16)
        Vt = sb("Vt", [128, 9, 256], BF16)
        T = sb("arenaT", [128, 6144], BF16)
        ps = st.enter_context(nc.psum_tensor("ps", [128, 8, 512], F32))
        ident_f = cst[:, 0:128]
        triu = cst[:, 128:256]
        maskadd = cst[:, 256:512]

        def psf(b):
            return ps[:, b, :]

        def psb(b):
            return ps[:, b, :].bitcast(BF16)

        def f32view(t, off, n):
            return t[:, off:off + n].bitcast(F32)

        tbuf = [f32view(T, i * 1024, 1024) for i in range(6)]
        sqpool = Pool(tbuf[0:2])
        rspool = Pool(tbuf[2:3])
        tApool = Pool(tbuf[3:5] + tbuf[5:6])
        banks_free = [None] * 8

        class Banks:
            def __init__(self, ids):
                self.ids, self.nxt = ids, 0

            def get(self):
                b = self.ids[self.nxt]
                self.nxt = (self.nxt + 1) % len(self.ids)
                return b, banks_free[b]

            @staticmethod
            def rel(b, tok):
                banks_free[b] = tok

        banks = Banks(list(range(8)))

        sp_sems = [pl.new_sem(f"sp{i}") for i in range(8)]
        sp_n = [0]

        def sp_dma(out_ap, in_ap, deps=()):
            i = sp_n[0]
            sp_n[0] += 1
            si = sp_sems[i % 8]
            k = i // 8
            prev = (si, 16 * k) if k > 0 else None
            pl.op("sp", lambda e: e.dma_start(out=out_ap, in_=in_ap), deps=[prev] + list(deps), inc=(si, 16))
            return (si, 16 * (k + 1))

        ring_sem = [pl.new_sem(f"rg{i}") for i in range(NSLOT)]
        ring_fills = [0] * NSLOT
        ring_free = [None] * NSLOT
        ring_nxt = [0]

        def ring_load(dmas):
            s = ring_nxt[0]
            ring_nxt[0] = (s + 1) % NSLOT
            t = ring_t[s]
            for dst_fn, src in dmas:
                dst = dst_fn(t)
                pl.op("pool", lambda e, dst=dst, src=src: e.dma_start(out=dst, in_=src),
                      deps=[ring_free[s]], inc=(ring_sem[s], 16))
                ring_fills[s] += 1
            return s, t, (ring_sem[s], 16 * ring_fills[s])

        def ring_rel(s, tok):
            ring_free[s] = tok

        def colblock(w2d, c0):
            src = w2d.rearrange("(k p) n -> p k n", p=128)[:, :, c0:c0 + 256]
            return [(lambda t: t[:, :].rearrange("p (k c) -> p k c", k=KC), src)]

        def rowblock(w2d, r0):
            src = w2d[r0:r0 + 256, :].rearrange("(j p) n -> p j n", p=128)
            return [(lambda t: t[:, :].rearrange("p (j n) -> p j n", j=2), src)]

        def cview(t):
            return t[:, :].rearrange("p (k c) -> p k c", k=KC)

        def rview(t):
            return t[:, :].rearrange("p (j n) -> p j n", j=2)

        def hT(kc, g):
            return (hT_h if g.kind == "h" else hT_m)[:, kc, g.t0:g.t0 + g.n]

        def xnT(kc, g):
            return (xn_h if g.kind == "h" else xn_m)[:, kc, g.t0:g.t0 + g.n]

        def xn_tile(kc, t):
            return xn_h[:, kc, :] if t == 0 else xn_m[:, kc, (t - 1) * 128:t * 128]

        state = {"h_tok": None, "xn": {}}

        def xn_tok_g(g):
            return state["xn"].get((g.kind, g.t0))

        def xn_tok_t(t):
            return state["xn"].get(("h", 0)) if t == 0 else state["xn"].get(("m", 0 if t <= 4 else 512))

        t_cols = sp_dma(cols[:, :], cols_d[:, :])
        t_cst = sp_dma(cst[:, :], cst_d[:, :])
        t_idb = pl.op("dve", lambda e: e.tensor_copy(out=ident_b[:, :], in_=ident_f), deps=[t_cst])
        t_ones = pl.op("dve", lambda e: e.memset(ones_b[:, :], 1.0 / D))
        epsc = small[:, 90:91]
        t_eps = pl.op("dve", lambda e: e.memset(epsc, EPS))

        def phase_load():
            tok = None
            n = 0
            for t in range(9):
                for qd in range(4):
                    bi, buf, bfree = tApool.get()
                    t_ld = sp_dma(buf[:, :], xin[t * 128:(t + 1) * 128, qd * 512:(qd + 1) * 512], deps=[bfree])
                    b, bf = banks.get()
                    for j in range(4):
                        t_mm = pl.op("pe", lambda e, b=b, j=j, buf=buf: e.transpose(
                            out=psf(b)[:, j * 128:(j + 1) * 128], in_=buf[:, j * 128:(j + 1) * 128], identity=ident_f),
                            deps=[t_ld, bf, t_cst], ms=(j == 3))
                    tApool.rel(bi, t_mm)
                    if t == 0:
                        dst = hT_h[:, qd * 4:(qd + 1) * 4, :]
                    else:
                        dst = hT_m[:, qd * 4:(qd + 1) * 4, (t - 1) * 128:t * 128]
                    src = psf(b).rearrange("p (j n) -> p j n", j=4)
                    tok = pl.op("dve", lambda e, dst=dst, src=src: e.tensor_copy(out=dst, in_=src), deps=[t_mm])
                    banks.rel(b, tok)
                    n += 1
            state["h_tok"] = tok

        def rstd_group(g):
            n = g.n
            b, bf = banks.get()
            t_mm = None
            for kc in range(KC):
                qi, sq, sqf = sqpool.get()
                sqb = sq.bitcast(BF16)
                t_sq = pl.op("act", lambda e, sqb=sqb, kc=kc: e.activation(out=sqb[:, :n], in_=hT(kc, g), func=AF.Square),
                             deps=[sqf, state["h_tok"]])
                t_mm = pl.op("pe", lambda e, sqb=sqb, kc=kc, b=b: e.matmul(
                    psf(b)[:, :n], lhsT=ones_b[:, :], rhs=sqb[:, :n], start=(kc == 0), stop=(kc == KC - 1)),
                    deps=[t_sq, bf if kc == 0 else None, t_ones], ms=True)
                sqpool.rel(qi, t_mm)
            ri, rs, rsf = rspool.get()
            t_r0 = pl.op("act", lambda e, b=b, rs=rs: e.activation(
                out=rs[:, :n], in_=psf(b)[:, :n], func=AF.Sqrt, bias=epsc, scale=1.0), deps=[t_mm, rsf, t_eps])
            banks.rel(b, t_r0)
            t_r = pl.op("dve", lambda e, rs=rs: e.reciprocal(out=rs[:, :n], in_=rs[:, :n]), deps=[t_r0])
            return ri, rs, t_r

        def rmsnorm(cbase, groups):
            tok = None
            for g in groups:
                n = g.n
                ri, rs, t_r = rstd_group(g)
                for kc in range(KC):
                    tok = pl.op("dve", lambda e, kc=kc, g=g, rs=rs: e.scalar_tensor_tensor(
                        out=xnT(kc, g), in0=hT(kc, g), scalar=cols[:, cbase + kc:cbase + kc + 1], in1=rs[:, :g.n],
                        op0=ALU.mult, op1=ALU.mult), deps=[t_r, t_cols, state["h_tok"]])
                rspool.rel(ri, tok)
                state["xn"][(g.kind, g.t0)] = tok

        def accum(b, g, dc, t_mm, bias_col=None):
            n = g.n
            if bias_col is None:
                tok = pl.op("dve", lambda e: e.tensor_tensor(out=hT(dc, g), in0=psf(b)[:, :n], in1=hT(dc, g), op=ALU.add),
                            deps=[t_mm])
            else:
                tok = pl.op("dve", lambda e: e.scalar_tensor_tensor(
                    out=hT(dc, g), in0=psf(b)[:, :n], scalar=bias_col, in1=hT(dc, g), op0=ALU.add, op1=ALU.add),
                    deps=[t_mm])
            banks.rel(b, tok)
            state["h_tok"] = tok
            return tok

        def phase_ffn(l, groups):
            rmsnorm(C_FFN + 16 * l, groups)
            ntok = NHALO + NMAIN
            hid = [A[:, i * 4 * ntok:(i + 1) * 4 * ntok].rearrange("p (j n) -> p j n", j=4) for i in range(2)]
            sgp = Pool([f32view(A, 2 * 4 * ntok + i * 1024, 1024) for i in range(3)])
            hid_tok = [None, None]
            fen = pl.fence()

            def GU(s):
                hb = hid[s % 2]
                last = None
                for half in range(2):
                    c0 = s * 512 + half * 256
                    sg_, tg_, lg = ring_load(colblock(wg_d[l], c0))
                    su_, tu_, lu = ring_load(colblock(wu_d[l], c0))
                    wg, wu = cview(tg_), cview(tu_)
                    tU = None
                    for j in range(2):
                        jj = half * 2 + j
                        for g in groups:
                            n = g.n
                            bg, fg = banks.get()
                            bu, fu = banks.get()
                            for kc in range(KC):
                                tG = pl.op("pe", lambda e, kc=kc, g=g, bg=bg, j=j, wg=wg: e.matmul(
                                    psf(bg)[:, :g.n], lhsT=wg[:, kc, j * 128:(j + 1) * 128], rhs=xnT(kc, g),
                                    start=(kc == 0), stop=(kc == KC - 1)),
                                    deps=[lg, xn_tok_g(g), fg], ms=(kc == KC - 1))
                            for kc in range(KC):
                                tU = pl.op("pe", lambda e, kc=kc, g=g, bu=bu, j=j, wu=wu: e.matmul(
                                    psf(bu)[:, :g.n], lhsT=wu[:, kc, j * 128:(j + 1) * 128], rhs=xnT(kc, g),
                                    start=(kc == 0), stop=(kc == KC - 1)),
                                    deps=[lu, fu], ms=(kc == KC - 1))
                            si, sgt, sf = sgp.get()
                            t1 = pl.op("act", lambda e, sgt=sgt, bg=bg, n=n: e.activation(
                                out=sgt[:, :n], in_=psf(bg)[:, :n], func=AF.Silu), deps=[tG, sf, fen])
                            banks.rel(bg, t1)
                            t2 = pl.op("dve", lambda e, sgt=sgt, bu=bu, n=n, jj=jj, g=g: e.tensor_tensor(
                                out=hb[:, jj, g.a0:g.a0 + n], in0=psf(bu)[:, :n], in1=sgt[:, :n], op=ALU.mult),
                                deps=[tU, t1, fen])
                            banks.rel(bu, t2)
                            sgp.rel(si, t2)
                            last = t2
                    ring_rel(sg_, tU)
                    ring_rel(su_, tU)
                hid_tok[s % 2] = last

            def DOWN(s):
                hb = hid[s % 2]
                r0 = s * 512
                s0, t0_, l0 = ring_load(rowblock(wd_d[l], r0))
                s1, t1_, l1 = ring_load(rowblock(wd_d[l], r0 + 256))
                wd = [rview(t0_), rview(t1_)]
                tmm = None
                for dc in range(KC):
                    for g in groups:
                        b, bf = banks.get()
                        for jj in range(4):
                            tmm = pl.op("pe", lambda e, jj=jj, g=g, b=b, dc=dc: e.matmul(
                                psf(b)[:, :g.n], lhsT=wd[jj // 2][:, jj % 2, dc * 128:(dc + 1) * 128],
                                rhs=hb[:, jj, g.a0:g.a0 + g.n], start=(jj == 0), stop=(jj == 3)),
                                deps=[l0, l1, hid_tok[s % 2], bf], ms=(jj == 3))
                        accum(b, g, dc, tmm)
                ring_rel(s0, tmm)
                ring_rel(s1, tmm)

            GU(0)
            for s in range(NSC):
                if s + 1 < NSC:
                    GU(s + 1)
                DOWN(s)

        def gelu_from_psum(src, n, hxp, wp, out_fn, extra_deps, sq_on_dve=False):
            hi, hx, hf = hxp.get()
            wi, w, wf = wp.get()
            t_h = pl.op("act", lambda e: e.activation(out=hx[:, :n], in_=src, func=AF.Copy, scale=0.5),
                        deps=[hf] + list(extra_deps))
            if sq_on_dve:
                t_w = pl.op("dve", lambda e: e.scalar_tensor_tensor(
                    out=w[:, :n], in0=hx[:, :n], scalar=4.0 * 0.044715, in1=hx[:, :n], op0=ALU.mult, op1=ALU.mult),
                    deps=[t_h, wf])
            else:
                t_w = pl.op("act", lambda e: e.activation(out=w[:, :n], in_=src, func=AF.Square, scale=math.sqrt(0.044715)),
                            deps=[wf])
            t_t = pl.op("dve", lambda e: e.scalar_tensor_tensor(
                out=w[:, :n], in0=w[:, :n], scalar=1.0, in1=hx[:, :n], op0=ALU.add, op1=ALU.mult), deps=[t_h, t_w])
            t_th = pl.op("act", lambda e: e.activation(out=w[:, :n], in_=w[:, :n], func=AF.Tanh, scale=2.0 * GELU_C),
                         deps=[t_t])
            t_o = out_fn(hx, w, [t_th])
            hxp.rel(hi, t_o)
            wp.rel(wi, t_o)
            return (t_h if sq_on_dve else t_w), t_o

        def phase_gmlp(l):
            i = l
            groups = G01
            rmsnorm(C_MIX + 16 * l, groups)
            fen = pl.fence()
            v = A[:, :].rearrange("p (t d) -> p t d", t=9)
            ntok = NHALO + NMAIN
            gated = [Bq[:, k * 2 * ntok:(k + 1) * 2 * ntok].rearrange("p (j n) -> p j n", j=2) for k in range(2)]
            h256 = [tbuf[3][:, 0:256], tbuf[3][:, 256:512], tbuf[4][:, 0:256]]
            w256 = [tbuf[4][:, 256:512], tbuf[5][:, 0:256], tbuf[5][:, 256:512]]
            hxp, wp = Pool(h256), Pool(w256)
            halves = [tbuf[k][:, c:c + 256] for k in range(6) for c in (0, 256)]
            hxpA, wpA = Pool(halves[0::2]), Pool(halves[1::2])
            u256 = Pool([tbuf[0][:, 0:256], tbuf[0][:, 256:512], tbuf[1][:, 0:256]])
            g256 = Pool([tbuf[1][:, 256:512], tbuf[2][:, 0:256]])
            VtF = Vt[:, :, :].rearrange("p a b -> p (a b)")
            WsT = VtF[:, 0:1024].rearrange("p (g t) -> p g t", g=8)
            bbp = Pool([VtF[:, 1024:1280].bitcast(F32), VtF[:, 1280:1536].bitcast(F32)])
            junk = VtF[:, 1536:1792]
            ssv = small[:, 0:72].rearrange("p (t c) -> p t c", t=9)
            ss9 = small[:, 72:81]
            rsv = small[:, 81:90]

            t_z = pl.op("dve", lambda e: e.memset(ssv, 0.0), deps=[fen])
            t_ws = None
            for hh in range(2):
                ti, tb, tf = sqpool.get()
                t_ld = sp_dma(tb[:, :].rearrange("p (g t) -> p g t", g=4), wst_d[i, :, hh * 4:(hh + 1) * 4, :], deps=[tf, fen])
                for gg in range(4):
                    t_ws = pl.op("dve", lambda e, tb=tb, gg=gg, hh=hh: e.tensor_tensor(
                        out=WsT[:, hh * 4 + gg, :], in0=tb[:, gg * 128:(gg + 1) * 128], in1=triu, op=ALU.mult),
                        deps=[t_ld, t_cst, fen])
                sqpool.rel(ti, t_ws)

            for cb in range(8):
                sl, tl, ld = ring_load(colblock(win_d[i], D + cb * 256))
                wv = cview(tl)
                tmm = None
                for t in range(9):
                    b, bf = banks.get()
                    for kc in range(KC):
                        tmm = pl.op("pe", lambda e, kc=kc, t=t, b=b, wv=wv: e.matmul(
                            psf(b)[:, :256], lhsT=xn_tile(kc, t), rhs=wv[:, kc, :], start=(kc == 0), stop=(kc == KC - 1)),
                            deps=[ld, xn_tok_t(t), bf], ms=(kc == KC - 1))

                    def out_fn(hx, w, deps, t=t, cb=cb):
                        t_v = pl.op("dve", lambda e: e.scalar_tensor_tensor(
                            out=v[:, t, cb * 256:(cb + 1) * 256], in0=w[:, :256], scalar=1.0, in1=hx[:, :256],
                            op0=ALU.add, op1=ALU.mult), deps=deps + [fen])
                        t_s = pl.op("act", lambda e: e.activation(
                            out=junk, in_=v[:, t, cb * 256:(cb + 1) * 256], func=AF.Square,
                            accum_out=ssv[:, t, cb:cb + 1]), deps=[t_v, t_z])
                        return t_s
                    t_w, t_o = gelu_from_psum(psf(b)[:, :256], 256, hxpA, wpA, out_fn, [tmm, t_ws], sq_on_dve=True)
                    banks.rel(b, t_w)
                ring_rel(sl, tmm)
            t_a = pl.op("dve", lambda e: e.tensor_reduce(out=ss9, in_=ssv, axis=AX.X, op=ALU.add), deps=[pl.last("act")])
            t_b = pl.op("dve", lambda e: e.tensor_scalar(out=ss9, in0=ss9, scalar1=1.0 / D, scalar2=EPS,
                                                       op0=ALU.mult, op1=ALU.add), deps=[t_a])
            t_c0 = pl.op("act", lambda e: e.activation(out=rsv, in_=ss9, func=AF.Sqrt), deps=[t_b])
            t_c = pl.op("dve", lambda e: e.reciprocal(out=rsv, in_=rsv), deps=[t_c0])
            t_vn = None
            for t in range(9):
                t_vn = pl.op("act", lambda e, t=t: e.activation(out=v[:, t, :], in_=v[:, t, :], func=AF.Copy,
                                                              scale=rsv[:, t:t + 1]), deps=[t_c])

            fenB = pl.fence()
            last_g = [None]

            def UG(g8):
                slU, tlU, ldU = ring_load(colblock(win_d[i], g8 * 256))
                wu = cview(tlU)
                bi, bb, bbf = bbp.get()
                t_bb = sp_dma(bb, bs_d[i, g8, :].partition_broadcast(128), deps=[bbf])
                last = None
                tmm = None
                for j in range(2):
                    dcg = 2 * g8 + j
                    for g in groups:
                        n = g.n
                        b, bf = banks.get()
                        for kc in range(KC):
                            tmm = pl.op("pe", lambda e, kc=kc, g=g, b=b, j=j: e.matmul(
                                psf(b)[:, :g.n], lhsT=wu[:, kc, j * 128:(j + 1) * 128], rhs=xnT(kc, g),
                                start=(kc == 0), stop=(kc == KC - 1)), deps=[ldU, bf], ms=(kc == KC - 1))
                        b2, bf2 = banks.get()
                        tmm2 = None
                        for ti, t in enumerate(g.tiles):
                            tmm2 = pl.op("pe", lambda e, ti=ti, t=t, b2=b2, dcg=dcg: e.matmul(
                                psf(b2)[:, ti * 128:(ti + 1) * 128], lhsT=v[:, t, dcg * 128:(dcg + 1) * 128],
                                rhs=WsT[:, g8, :], start=True, stop=True),
                                deps=[t_vn, t_ws, bf2], ms=(ti == len(g.tiles) - 1))
                        pieces = [(c, min(256, n - c)) for c in range(0, n, 256)]
                        for pi, (c0, cn) in enumerate(pieces):
                            ui, ut, uf = u256.get()

                            def out_fn(hx, w, deps, ut=ut, cn=cn):
                                return pl.op("dve", lambda e: e.scalar_tensor_tensor(
                                    out=ut[:, :cn], in0=w[:, :cn], scalar=1.0, in1=hx[:, :cn],
                                    op0=ALU.add, op1=ALU.mult), deps=deps + [uf])
                            t_w, t_u = gelu_from_psum(psf(b)[:, c0:c0 + cn], cn, hxp, wp, out_fn, [tmm, fenB])
                            if pi == len(pieces) - 1:
                                banks.rel(b, t_w)
                            gi, gt, gf = g256.get()
                            nt = cn // 128
                            t_g1 = pl.op("dve", lambda e, gt=gt, c0=c0, cn=cn, nt=nt, b2=b2, dcg=dcg, bb=bb: e.scalar_tensor_tensor(
                                out=gt[:, :cn].rearrange("p (a t) -> p a t", a=nt),
                                in0=psf(b2)[:, c0:c0 + cn].rearrange("p (a t) -> p a t", a=nt),
                                scalar=cols[:, C_NV + 16 * i + dcg:C_NV + 16 * i + dcg + 1],
                                in1=bb.unsqueeze(1).broadcast_to([128, nt, 128]),
                                op0=ALU.mult, op1=ALU.add), deps=[tmm2, t_bb, gf, t_cols, fenB])
                            if pi == len(pieces) - 1:
                                banks.rel(b2, t_g1)
                            tl0 = g.tiles[c0 // 128]
                            t_g2 = pl.op("dve", lambda e, gt=gt, ut=ut, cn=cn, nt=nt, tl0=tl0, dcg=dcg: e.tensor_tensor(
                                out=v[:, tl0:tl0 + nt, dcg * 128:(dcg + 1) * 128],
                                in0=gt[:, :cn].rearrange("p (a t) -> p a t", a=nt),
                                in1=ut[:, :cn].rearrange("p (a t) -> p a t", a=nt), op=ALU.mult),
                                deps=[t_g1, t_u, tmm2])
                            u256.rel(ui, t_g2)
                            g256.rel(gi, t_g2)
                            last = t_g2
                ring_rel(slU, tmm)
                bbp.rel(bi, last)
                last_g[0] = last

            for g8 in range(8):
                UG(g8)
            for cbk in range(8):
                slO, tlO, ldO = ring_load(colblock(wout_d[i], cbk * 256))
                wo = cview(tlO)
                tmm = None
                for j in range(2):
                    dc = cbk * 2 + j
                    for g in groups:
                        b, bf = banks.get()
                        nt = g.n // 128
                        t0_ = g.tiles[0]
                        for kc in range(KC):
                            tmm = pl.op("pe", lambda e, kc=kc, g=g, b=b, j=j, wo=wo, nt=nt, t0_=t0_: e.matmul(
                                psf(b)[:, :g.n].rearrange("p (a t) -> p a t", a=nt),
                                lhsT=wo[:, kc, j * 128:(j + 1) * 128],
                                rhs=v[:, t0_:t0_ + nt, kc * 128:(kc + 1) * 128],
                                start=(kc == 0), stop=(kc == KC - 1)), deps=[ldO, last_g[0], bf], ms=(kc == KC - 1))
                        accum(b, g, dc, tmm)
                ring_rel(slO, tmm)

        KT = Bq[:, :].rearrange("p (m n) -> p m n", m=4)

        def phase_kv():
            groups = G01
            rmsnorm(C_KV, groups)
            fen = pl.fence()
            bv = tbuf[3][:, 0:256]
            t_bv = sp_dma(bv, bkv_d[256:512].partition_broadcast(128), deps=[fen])
            for half in range(2):
                dm = []
                for h2 in range(2):
                    src = wkv_d[:, (half * 2 + h2) * 64:(half * 2 + h2 + 1) * 64].rearrange("(k p) c -> p k c", p=128)
                    for dup in range(2):
                        dm.append((lambda t, dup=dup, h2=h2: t[:, :].rearrange(
                            "p (k h u c) -> p k h u c", k=KC, h=2, u=2)[:, :, h2, dup, :], src))
                sl, tl, ld = ring_load(dm)
                wk = cview(tl)
                tmm = None
                for hh in range(2):
                    m = half * 2 + hh
                    for g in groups:
                        b, bf = banks.get()
                        for kc in range(KC):
                            tmm = pl.op("pe", lambda e, kc=kc, g=g, b=b, hh=hh, wk=wk: e.matmul(
                                psf(b)[:, :g.n], lhsT=wk[:, kc, hh * 128:(hh + 1) * 128], rhs=xnT(kc, g),
                                start=(kc == 0), stop=(kc == KC - 1)), deps=[ld, xn_tok_g(g), bf], ms=(kc == KC - 1))
                        t_e = pl.op("act", lambda e, g=g, b=b, m=m: e.activation(
                            out=KT[:, m, g.a0:g.a0 + g.n], in_=psf(b)[:, :g.n], func=AF.Identity,
                            bias=cols[:, C_BK + m:C_BK + m + 1], scale=1.0), deps=[tmm, t_cols, fen])
                        banks.rel(b, t_e)
                ring_rel(sl, tmm)
            sl, tl, ld = ring_load(colblock(wkv_d, 256))
            wv = cview(tl)
            tmm = None
            for t in range(9):
                b, bf = banks.get()
                for kc in range(KC):
                    tmm = pl.op("pe", lambda e, kc=kc, t=t, b=b: e.matmul(
                        psf(b)[:, :256], lhsT=xn_tile(kc, t), rhs=wv[:, kc, :], start=(kc == 0), stop=(kc == KC - 1)),
                        deps=[ld, bf], ms=(kc == KC - 1))
                t_e = pl.op("dve", lambda e, t=t, b=b: e.tensor_tensor(out=Vt[:, t, :], in0=psf(b)[:, :256], in1=bv, op=ALU.add),
                            deps=[tmm, t_bv, fen])
                banks.rel(b, t_e)
            ring_rel(sl, tmm)

        def phase_attn(l):
            i = l - 2
            groups = G23
            rmsnorm(C_MIX + 16 * l, groups)
            fen = pl.fence()
            attnT = A[:, 0:16384].rearrange("p (k n) -> p k n", k=KC)
            qTz = [A[:, 16384:17408], A[:, 17408:18432]]
            t_qz = pl.op("dve", lambda e: e.memset(A[:, 16384:18432], 0.0), deps=[fen])
            hh_f = hT_h[:, :, :].rearrange("p k n -> p (k n)")
            braw = hh_f[:, 0:512].rearrange("p (e k) -> p e k", e=2)
            biasp = [hh_f[:, 512:1024].rearrange("p (e k) -> p e k", e=2), hh_f[:, 1024:1536].rearrange("p (e k) -> p e k", e=2)]
            bias0 = hh_f[:, 1536:2048].rearrange("p (e k) -> p e k", e=2)
            scp = Pool([tbuf[3][:, :].rearrange("p (e k) -> p e k", e=2), tbuf[4][:, :].rearrange("p (e k) -> p e k", e=2)])
            xh_b = xn_h[:, :, :].rearrange("p k n -> p (k n)")
            pp = Pool([xh_b[:, k * 512:(k + 1) * 512].rearrange("p (e k) -> p e k", e=2) for k in range(2)])
            pTp = Pool([xh_b[:, 1024 + k * 512:1024 + (k + 1) * 512].rearrange("p (a q) -> p a q", a=4) for k in range(2)])
            t5b = tbuf[5][:, :].bitcast(BF16)
            op_ = Pool([t5b[:, k * 128:(k + 1) * 128] for k in range(3)])
            sink_bc = small[:, 96:128]
            mask0 = tbuf[2][:, 0:256]
            t_sk = sp_dma(sink_bc, sink_d[i, :].partition_broadcast(128), deps=[fen])
            t_m0 = sp_dma(mask0, mask0_d[:, :], deps=[fen])
            sm = small[:, 128:256]
            smp = Pool([sm[:, k * 16:(k + 1) * 16] for k in range(8)])
            poolS, poolT, poolO, poolX = Banks([0, 1]), Banks([2, 3]), Banks([4, 5]), Banks([6, 7])
            bias_tok = [None, None]
            bias0_tok = [None]
            braw_free = [None]
            stt = {}
            wq_slot = [None]

            def SA(it):
                hp, qb = it
                c = stt.setdefault(it, {})
                if qb == 0:
                    if hp % 2 == 0:
                        if wq_slot[0] is not None:
                            ring_rel(wq_slot[0][0], wq_slot[0][3])
                        sl, tl, ld = ring_load(colblock(wq_d[i], (hp // 2) * 256))
                        wq_slot[0] = [sl, cview(tl), ld, None]
                    wq = wq_slot[0][1]
                    ld = wq_slot[0][2]
                    t_q = None
                    for g in groups:
                        b, bf = poolS.get()
                        tmm = None
                        for kc in range(KC):
                            tmm = pl.op("pe", lambda e, kc=kc, g=g, b=b: e.matmul(
                                psf(b)[:, :g.n], lhsT=wq[:, kc, (hp % 2) * 128:(hp % 2 + 1) * 128], rhs=xnT(kc, g),
                                start=(kc == 0), stop=(kc == KC - 1)), deps=[ld, xn_tok_g(g), bf], ms=(kc == KC - 1))
                        wq_slot[0][3] = tmm
                        for e2 in range(2):
                            t_q = pl.op("act", lambda e, g=g, b=b, e2=e2: e.activation(
                                out=qTz[e2][e2 * 64:(e2 + 1) * 64, g.t0:g.t0 + g.n], in_=psf(b)[e2 * 64:(e2 + 1) * 64, :g.n],
                                func=AF.Identity, bias=cols[e2 * 64:(e2 + 1) * 64, C_BQ + 16 * i + hp:C_BQ + 16 * i + hp + 1],
                                scale=1.0), deps=[tmm, t_cols, fen, t_qz])
                        Banks.rel(b, t_q)
                    stt["q_tok"] = t_q
                    t_ld = sp_dma(braw, biasg_d[:, 2 * hp:2 * hp + 2, :], deps=[braw_free[0], fen])
                    t_b0 = pl.op("dve", lambda e: e.tensor_tensor(
                        out=bias0, in0=braw, in1=mask0.unsqueeze(1).broadcast_to([128, 2, 256]), op=ALU.add),
                        deps=[t_ld, t_m0, bias0_tok[0]])
                    bp = biasp[hp % 2]
                    t_b1 = pl.op("dve", lambda e: e.tensor_tensor(
                        out=bp, in0=braw, in1=maskadd.unsqueeze(1).broadcast_to([128, 2, 256]), op=ALU.add),
                        deps=[t_ld, t_cst, bias_tok[hp % 2]])
                    braw_free[0] = t_b1
                    stt["b0"] = t_b0
                    stt["b1"] = t_b1
                c["b0"], c["b1"] = stt["b0"], stt["b1"]
                kvh = hp // 4
                b, bf = poolS.get()
                tmm = None
                for e2 in range(2):
                    tmm = pl.op("pe", lambda e, e2=e2, b=b: e.matmul(
                        psf(b)[:, e2 * 256:(e2 + 1) * 256], lhsT=qTz[e2][:, qb * 128:(qb + 1) * 128],
                        rhs=KT[:, kvh, qb * 128:qb * 128 + 256], start=True, stop=True),
                        deps=[stt["q_tok"], bf, fen], ms=(e2 == 1))
                c["bS"], c["tS"] = b, tmm

            def SB(it):
                hp, qb = it
                c = stt[it]
                b = c["bS"]
                si, sc, sf = scp.get()
                bsel = bias0 if qb == 0 else biasp[hp % 2]
                t_sc = pl.op("dve", lambda e: e.scalar_tensor_tensor(
                    out=sc, in0=psf(b)[:, :].rearrange("p (e k) -> p e k", e=2), scalar=0.125, in1=bsel,
                    op0=ALU.mult, op1=ALU.add), deps=[c["tS"], sf, c["b0"], c["b1"]])
                Banks.rel(b, t_sc)
                if qb == 0:
                    bias0_tok[0] = t_sc
                bias_tok[hp % 2] = t_sc
                mi, ms_, mf = smp.get()
                mx, negm, dd, rs, den, rden = (ms_[:, 0:2], ms_[:, 2:4], ms_[:, 4:6], ms_[:, 6:8], ms_[:, 8:10], ms_[:, 10:12])
                sk = sink_bc[:, 2 * hp:2 * hp + 2]
                t1 = pl.op("dve", lambda e: e.tensor_reduce(out=mx, in_=sc, axis=AX.X, op=ALU.max), deps=[t_sc, mf])
                t2 = pl.op("dve", lambda e: e.tensor_tensor(out=mx, in0=mx, in1=sk, op=ALU.max), deps=[t1, t_sk])
                t3 = pl.op("dve", lambda e: e.tensor_scalar(out=negm, in0=mx, scalar1=-1.0, scalar2=None, op0=ALU.mult), deps=[t2])
                t4 = pl.op("dve", lambda e: e.tensor_tensor(out=dd, in0=sk, in1=negm, op=ALU.add), deps=[t3])
                t_z = pl.op("dve", lambda e: e.memset(rs, 0.0), deps=[mf])
                c.update(si=si, sc=sc, mi=mi, negm=negm, dd=dd, rs=rs, den=den, rden=rden, tB=[t3, t4, t_z])

            def SC(it):
                c = stt[it]
                sc, negm, rs, dd = c["sc"], c["negm"], c["rs"], c["dd"]
                pi, p, pf = pp.get()
                t_e = None
                for e2 in range(2):
                    t_e = pl.op("act", lambda e, e2=e2: e.activation(
                        out=p[:, e2, :], in_=sc[:, e2, :], func=AF.Exp, bias=negm[:, e2:e2 + 1], scale=1.0,
                        accum_out=rs[:, e2:e2 + 1]), deps=c["tB"] + [pf])
                scp.rel(c["si"], t_e)
                t5 = pl.op("act", lambda e: e.activation(out=dd, in_=dd, func=AF.Exp), deps=c["tB"])
                c.update(pi=pi, p=p, tC=t5, tE=t_e)

            def SD(it):
                c = stt[it]
                p, den, rden, rs, dd = c["p"], c["den"], c["rden"], c["rs"], c["dd"]
                bT, bfT = poolT.get()
                tmm = None
                for e2 in range(2):
                    for kb in range(2):
                        a = e2 * 2 + kb
                        tmm = pl.op("pe", lambda e, a=a, e2=e2, kb=kb: e.transpose(
                            out=psb(bT)[:, a * 128:(a + 1) * 128], in_=p[:, e2, kb * 128:(kb + 1) * 128], identity=ident_b[:, :]),
                            deps=[c["tE"], bfT, t_idb], ms=(a == 3))
                pp.rel(c["pi"], tmm)
                c["bT"], c["tT"] = bT, tmm
                t6 = pl.op("dve", lambda e: e.tensor_tensor(out=den, in0=rs, in1=dd, op=ALU.add), deps=[c["tC"], c["tE"]])
                t7 = pl.op("dve", lambda e: e.reciprocal(out=rden, in_=den), deps=[t6])
                c["t_rden"] = t7

            def SE(it):
                c = stt[it]
                bT = c["bT"]
                ti, pT, tf = pTp.get()
                t_c = pl.op("act", lambda e: e.activation(
                    out=pT, in_=psb(bT)[:, 0:512].rearrange("p (a q) -> p a q", a=4), func=AF.Copy), deps=[c["tT"], tf])
                Banks.rel(bT, t_c)
                c.update(ti=ti, pT=pT, tPT=t_c)

            def SF(it):
                hp, qb = it
                c = stt[it]
                kvh = hp // 4
                pT = c["pT"]
                bO, bfO = poolO.get()
                tmm = None
                for e2 in range(2):
                    for kb in range(2):
                        tmm = pl.op("pe", lambda e, e2=e2, kb=kb: e.matmul(
                            psf(bO)[:, e2 * 64:(e2 + 1) * 64], lhsT=pT[:, e2 * 2 + kb, :],
                            rhs=Vt[:, qb + kb, kvh * 64:(kvh + 1) * 64], start=(kb == 0), stop=(kb == 1)),
                            deps=[c["tPT"], bfO, fen], ms=(e2 == 1 and kb == 1))
                pTp.rel(c["ti"], tmm)
                c["bO"], c["tO"] = bO, tmm

            def SG(it):
                c = stt[it]
                bO = c["bO"]
                oi, o, of = op_.get()
                t_o = pl.op("dve", lambda e: e.tensor_tensor(
                    out=o.rearrange("p (e d) -> p e d", e=2), in0=psf(bO)[:, 0:128].rearrange("p (e d) -> p e d", e=2),
                    in1=c["rden"].unsqueeze(2).broadcast_to([128, 2, 64]), op=ALU.mult), deps=[c["tO"], c["t_rden"], of])
                Banks.rel(bO, t_o)
                smp.rel(c["mi"], t_o)
                c.update(oi=oi, o=o, t_o=t_o)

            def SH(it):
                c = stt[it]
                o = c["o"]
                bX, bfX = poolX.get()
                tmm = pl.op("pe", lambda e: e.transpose(out=psb(bX)[:, 0:128], in_=o, identity=ident_b[:, :]),
                            deps=[c["t_o"], bfX], ms=True)
                op_.rel(c["oi"], tmm)
                c["bX"], c["tX"] = bX, tmm

            def SI(it):
                hp, qb = it
                c = stt[it]
                bX = c["bX"]
                t_a = pl.op("act", lambda e: e.activation(out=attnT[:, hp, qb * 128:(qb + 1) * 128], in_=psb(bX)[:, 0:128],
                                                        func=AF.Copy), deps=[c["tX"], fen])
                Banks.rel(bX, t_a)
                stt["attn_tok"] = t_a
                del stt[it]

            its = [(hp, qb) for hp in range(16) for qb in range(8)]
            stages = [SA, SB, SC, SD, SE, SF, SG, SH, SI]
            for k in range(1, 9):
                if f"st{k}" in dbg:
                    stages = stages[:k]
                    its = its[:16]
            for step in range(len(its) + len(stages) - 1):
                for sidx in reversed(range(len(stages))):
                    k = step - sidx
                    if 0 <= k < len(its):
                        stages[sidx](its[k])
            ring_rel(wq_slot[0][0], wq_slot[0][3])
            if len(stages) < 9:
                return
            for cbk in range(8):
                sl, tl, ld = ring_load(colblock(wo_d[i], cbk * 256))
                wo = cview(tl)
                tmm = None
                for j in range(2):
                    dc = cbk * 2 + j
                    for g in groups:
                        b, bf = banks.get()
                        for kc in range(KC):
                            tmm = pl.op("pe", lambda e, kc=kc, g=g, b=b, j=j, wo=wo: e.matmul(
                                psf(b)[:, :g.n], lhsT=wo[:, kc, j * 128:(j + 1) * 128], rhs=attnT[:, kc, g.t0:g.t0 + g.n],
                                start=(kc == 0), stop=(kc == KC - 1)), deps=[ld, stt["attn_tok"], bf], ms=(kc == KC - 1))
                        accum(b, g, dc, tmm, bias_col=cols[:, C_BO + 16 * i + dc:C_BO + 16 * i + dc + 1])
                ring_rel(sl, tmm)

        st_sems = [pl.new_sem("st0"), pl.new_sem("st1")]
        n_st = [0, 0]

        def phase_out(final):
            groups = G23 if final else G01
            fen = pl.fence()
            fT = [f32view(A, k * 1024, 1024) for k in range(4)]
            ostp = Pool([f32view(A, (4 + k) * 1024, 1024) for k in range(2)])
            fT_free = [None] * 4
            row0 = 0 if final else None
            for g in groups:
                n = g.n
                if final:
                    ri, rs, t_r = rstd_group(g)
                for kq in range(4):
                    srcs = []
                    if final:
                        for j in range(4):
                            kc = kq * 4 + j
                            t_f = pl.op("dve", lambda e, kc=kc, j=j, g=g, rs=rs: e.scalar_tensor_tensor(
                                out=fT[j][:, :g.n], in0=hT(kc, g), scalar=cols[:, C_FIN + kc:C_FIN + kc + 1], in1=rs[:, :g.n],
                                op0=ALU.mult, op1=ALU.mult), deps=[t_r, fT_free[j], fen, t_cols])
                            srcs.append((fT[j], 0, t_f))
                    else:
                        for j in range(4):
                            kc = kq * 4 + j
                            base = hT_h if g.kind == "h" else hT_m
                            srcs.append((base[:, kc, :], g.t0, state["h_tok"]))
                    for ti in range(n // 128):
                        b, bf = banks.get()
                        tmm = None
                        for j in range(4):
                            s_ap, off, s_tok = srcs[j]
                            tmm = pl.op("pe", lambda e, s_ap=s_ap, off=off, j=j, ti=ti, b=b: e.transpose(
                                out=psf(b)[:, j * 128:(j + 1) * 128], in_=s_ap[:, off + ti * 128:off + (ti + 1) * 128],
                                identity=ident_f), deps=[s_tok, bf, t_cst], ms=(j == 3))
                        oi, ost, of = ostp.get()
                        t_c = pl.op("act", lambda e, ost=ost, b=b: e.activation(out=ost[:, :], in_=psf(b), func=AF.Copy),
                                    deps=[tmm, of, fen])
                        banks.rel(b, t_c)
                        if final:
                            r = g.t0 + ti * 128
                        else:
                            r = g.a0 + ti * 128
                        pl.op("sp", lambda e, ost=ost, r=r, kq=kq: e.dma_start(
                            out=out[r:r + 128, kq * 512:(kq + 1) * 512], in_=ost[:, :]), deps=[t_c], inc=(st_sems[oi], 16))
                        n_st[oi] += 1
                        ostp.rel(oi, (st_sems[oi], 16 * n_st[oi]))
                    if final:
                        for j in range(4):
                            fT_free[j] = tmm
                if final:
                    rspool.rel(ri, pl.last("dve"))
            pl.wait("sp", (st_sems[0], 16 * n_st[0]))
            pl.wait("sp", (st_sems[1], 16 * n_st[1]))

        phase_load()
        for l in layers:
            if l < 2:
                phase_gmlp(l)
                phase_ffn(l, G01)
            else:
                if l == 2:
                    phase_kv()
                if "kvonly" not in dbg:
                    phase_attn(l)
                if "noffn" not in dbg:
                    phase_ffn(l, G23)
        phase_out(is_last)

        with nc.Block() as block:
            @block.tensor
            def _(e):
                pl.replay("pe", e)

            @block.scalar
            def _(e):
                pl.replay("act", e)

            @block.vector
            def _(e):
                pl.replay("dve", e)

            @block.gpsimd
            def _(e):
                pl.replay("pool", e)

            @block.sync
            def _(e):
                pl.replay("sp", e)
    return nc


def _colize(v):
    return np.ascontiguousarray(np.asarray(v, np.float32).reshape(-1, 128).T)


def _t5_bucket(dist):
    max_exact = 16
    d = np.maximum(dist, 1).astype(np.float32)
    large = max_exact + (np.log(d / np.float32(max_exact)) / np.float32(math.log(128 / max_exact))
                         * np.float32(32 - max_exact)).astype(np.int32)
    large = np.minimum(large, 31)
    return np.where(dist < max_exact, dist, large)


def _host_tables(inp):
    cols = np.zeros((128, NCOL), np.float32)
    for l in range(4):
        cols[:, C_MIX + 16 * l:C_MIX + 16 * (l + 1)] = _colize(inp["mix_norm"][l])
        cols[:, C_FFN + 16 * l:C_FFN + 16 * (l + 1)] = _colize(inp["ffn_norm"][l])
    cols[:, C_KV:C_KV + 16] = _colize(inp["kv_norm"])
    cols[:, C_FIN:C_FIN + 16] = _colize(inp["final_norm"])
    for i in range(2):
        cols[:, C_NV + 16 * i:C_NV + 16 * (i + 1)] = _colize(inp["a_norm_v"][i])
        cols[:, C_BQ + 16 * i:C_BQ + 16 * (i + 1)] = _colize(inp["b_b_q"][i])
        cols[:, C_BO + 16 * i:C_BO + 16 * (i + 1)] = _colize(inp["b_b_o"][i])
    bk = np.asarray(inp["b_kv"], np.float32)[:256].reshape(4, 64)
    cols[:, C_BK:C_BK + 4] = np.concatenate([bk, bk], axis=1).T
    cst = np.zeros((128, 512), np.float32)
    cst[:, 0:128] = np.eye(128, dtype=np.float32)
    s = np.arange(128)[:, None]
    t = np.arange(128)[None, :]
    cst[:, 128:256] = (s <= t).astype(np.float32)
    dist = np.arange(128)[:, None] + 128 - np.arange(256)[None, :]
    in_window = (dist >= 0) & (dist < 128)
    cst[:, 256:512] = np.where(in_window, 0.0, MASKV).astype(np.float32)
    mask0_first = np.where(in_window & (np.arange(256)[None, :] >= 128), 0.0, MASKV).astype(np.float32)
    mask0_second = cst[:, 256:512].copy()
    bucket = _t5_bucket(np.clip(dist, 0, None).astype(np.int32))
    rel = np.asarray(inp["rel_bias"], np.float32)
    biasg = np.ascontiguousarray(rel[bucket].transpose(0, 2, 1))
    return cols, cst, mask0_first, mask0_second, biasg


def _core_x(x, c):
    b, half = c // 2, c % 2
    xm = x[b, half * NMAIN:(half + 1) * NMAIN]
    if half == 0:
        halo = np.zeros((NHALO, D), np.float32)
    else:
        halo = x[b, NMAIN - NHALO:NMAIN]
    return np.ascontiguousarray(np.concatenate([halo, xm], axis=0))


_NC_CACHE = {}


def _get_nc(layers, is_first, is_last):
    key = (tuple(layers), is_first, is_last)
    if key not in _NC_CACHE:
        _NC_CACHE[key] = build(layers, is_first, is_last)
    return _NC_CACHE[key]


def _in_map(inp, tabs, xin, c, layers):
    cols, cst, m0f, m0s, biasg = tabs
    m = {"xin": xin, "cols": cols, "cst": cst,
         "ffn_w_gate": inp["ffn_w_gate"], "ffn_w_up": inp["ffn_w_up"], "ffn_w_down": inp["ffn_w_down"]}
    if any(l < 2 for l in layers):
        m["a_w_in"] = inp["a_w_in"]
        m["a_w_sT"] = inp["_a_w_sT"]
        m["a_b_s"] = inp["a_b_s"]
        m["a_w_out"] = inp["a_w_out"]
    if any(l >= 2 for l in layers):
        m["w_kv"] = inp["w_kv"]
        m["b_kv"] = inp["b_kv"]
        m["b_w_q"] = inp["b_w_q"]
        m["b_w_o"] = inp["b_w_o"]
        m["b_sinks"] = inp["b_sinks"]
        m["biasg"] = biasg
        m["mask0"] = m0f if c % 2 == 0 else m0s
    return m


LAUNCHES = [((0, 1, 2, 3), True, True)]


def kernel(**inputs):
    inp = {k: np.ascontiguousarray(np.asarray(v, np.float32)) for k, v in inputs.items()}
    inp["_a_w_sT"] = np.ascontiguousarray(inp["a_w_s"].transpose(0, 3, 1, 2))
    tabs = _host_tables(inp)
    ncores = 8
    cur = [_core_x(inp["x"], c) for c in range(ncores)]
    for layers, is_first, is_last in LAUNCHES:
        nc = _get_nc(layers, is_first, is_last)
        in_maps = [_in_map(inp, tabs, cur[c], c, layers) for c in range(ncores)]
        res = run_bass_kernel_spmd(nc, in_maps, core_ids=list(range(ncores)))
        cur = [np.asarray(res.results[c]["out"], np.float32) for c in range(ncores)]
    outp = np.zeros((4, 2 * NMAIN, D), np.float32)
    for c in range(ncores):
        outp[c // 2, (c % 2) * NMAIN:(c % 2 + 1) * NMAIN] = cur[c]
    return outp
```

```python
import math
from contextlib import ExitStack

import numpy as np
import concourse.bass as bass
import concourse.mybir as mybir
from concourse.bass_utils import run_bass_kernel_spmd

F32 = mybir.dt.float32
BF16 = mybir.dt.bfloat16
AF = mybir.ActivationFunctionType
ALU = mybir.AluOpType
AX = mybir.AxisListType

D = 2048
KC = 16
DFF = 5632
NSC = 11
EPS = 1e-5
NMAIN = 1024
NHALO = 128
NSLOT = 4
MASKV = -30000.0

C_MIX = 0
C_FFN = 64
C_KV = 128
C_FIN = 144
C_NV = 160
C_BQ = 192
C_BO = 224
C_BK = 256
NCOL = 260

GELU_C = math.sqrt(2.0 / math.pi)


class Grp:
    def __init__(self, kind, t0, n):
        self.kind, self.t0, self.n = kind, t0, n
        self.a0 = t0 if kind == "h" else NHALO + t0
        self.tiles = [self.a0 // 128 + i for i in range(n // 128)]


G01 = [Grp("h", 0, 128), Grp("m", 0, 512), Grp("m", 512, 512)]
G23 = [Grp("m", 0, 512), Grp("m", 512, 512)]


class Plan:
    ENG = ("pe", "act", "dve", "pool", "sp")

    def __init__(self, nc, stack):
        self.nc, self.stack = nc, stack
        self.sems = []
        self.lists = {e: [] for e in self.ENG}
        self.waited = {e: {} for e in self.ENG}
        self.cnt = {}
        self.esem = {}
        for e in self.ENG:
            self.esem[e] = self.new_sem("s_" + e)
            self.cnt[e] = 0

    def new_sem(self, name):
        s = self.stack.enter_context(self.nc.semaphore(name))
        self.sems.append(s)
        return len(self.sems) - 1

    def wait(self, eng, tok):
        if tok is None:
            return
        si, val = tok
        if eng == "pe" and si == self.esem["pe"]:
            return
        if self.waited[eng].get(si, 0) >= val:
            return
        self.waited[eng][si] = val
        self.lists[eng].append(("w", si, val))

    def op(self, eng, fn, deps=(), ms=True, inc=None):
        for d in deps:
            if isinstance(d, list):
                for dd in d:
                    self.wait(eng, dd)
            else:
                self.wait(eng, d)
        if inc is not None:
            self.lists[eng].append(("o", fn, inc[0], inc[1]))
            return None
        if ms:
            self.cnt[eng] += 1
            self.lists[eng].append(("o", fn, self.esem[eng], 1))
            return (self.esem[eng], self.cnt[eng])
        self.lists[eng].append(("o", fn, None, 0))
        return None

    def last(self, eng):
        return (self.esem[eng], self.cnt[eng]) if self.cnt[eng] else None

    def fence(self):
        return [self.last("pe"), self.last("act"), self.last("dve")]

    def replay(self, eng, e):
        for it in self.lists[eng]:
            if it[0] == "w":
                e.wait_ge(self.sems[it[1]], it[2])
            else:
                ins = it[1](e)
                if it[2] is not None:
                    ins.then_inc(self.sems[it[2]], it[3])


class Pool:
    def __init__(self, aps):
        self.aps = aps
        self.free = [None] * len(aps)
        self.nxt = 0

    def get(self):
        i = self.nxt
        self.nxt = (i + 1) % len(self.aps)
        return i, self.aps[i], self.free[i]

    def rel(self, i, tok):
        self.free[i] = tok


def build(layers, is_first, is_last, dbg=()):
    layers = tuple(layers)
    do_g = any(l < 2 for l in layers)
    do_a = any(l >= 2 for l in layers)
    nc = bass.Bass("TRN2", target_bir_lowering=False)

    def dram(name, shape, kind="ExternalInput"):
        return nc.dram_tensor(name, list(shape), F32, kind=kind).ap()

    xin = dram("xin", [NHALO + NMAIN, D])
    n_out = NMAIN if is_last else NHALO + NMAIN
    out = dram("out", [n_out, D], kind="ExternalOutput")
    cols_d = dram("cols", [128, NCOL])
    cst_d = dram("cst", [128, 512])
    wg_d = dram("ffn_w_gate", [4, D, DFF])
    wu_d = dram("ffn_w_up", [4, D, DFF])
    wd_d = dram("ffn_w_down", [4, DFF, D])
    if do_g:
        win_d = dram("a_w_in", [2, D, 2 * D])
        wst_d = dram("a_w_sT", [2, 128, 8, 128])
        bs_d = dram("a_b_s", [2, 8, 128])
        wout_d = dram("a_w_out", [2, D, D])
    if do_a:
        wkv_d = dram("w_kv", [D, 512])
        bkv_d = dram("b_kv", [512])
        wq_d = dram("b_w_q", [2, D, D])
        wo_d = dram("b_w_o", [2, D, D])
        sink_d = dram("b_sinks", [2, 32])
        biasg_d = dram("biasg", [128, 32, 256])
        mask0_d = dram("mask0", [128, 256])

    with ExitStack() as st:
        pl = Plan(nc, st)

        def sb(name, shape, dt):
            return st.enter_context(nc.sbuf_tensor(name, list(shape), dt))

        hT_m = sb("hT_m", [128, KC, NMAIN], F32)
        hT_h = sb("hT_h", [128, KC, NHALO], F32)
        xn_m = sb("xn_m", [128, KC, NMAIN], BF16)
        xn_h = sb("xn_h", [128, KC, NHALO], BF16)
        ring_t = [sb(f"ring{i}", [128, 4096], BF16) for i in range(NSLOT)]
        cols = sb("cols_sb", [128, NCOL], F32)
        cst = sb("cst_sb", [128, 512], F32)
        ident_b = sb("ident_b", [128, 128], BF16)
        ones_b = sb("ones_b", [128, 128], BF16)
        small = sb("small", [128, 256], F32)
        A = sb("arenaA", [128, 18432], BF16)
        Bq = sb("arenaB", [128, 4608], BF16)
        Vt = sb("Vt", [128, 9, 256], BF16)
        T = sb("arenaT", [128, 6144], BF16)
        ps = st.enter_context(nc.psum_tensor("ps", [128, 8, 512], F32))
        ident_f = cst[:, 0:128]
        triu = cst[:, 128:256]
        maskadd = cst[:, 256:512]

        def psf(b):
            return ps[:, b, :]

        def psb(b):
            return ps[:, b, :].bitcast(BF16)

        def f32view(t, off, n):
            return t[:, off:off + n].bitcast(F32)

        tbuf = [f32view(T, i * 1024, 1024) for i in range(6)]
        sqpool = Pool(tbuf[0:2])
        rspool = Pool(tbuf[2:3])
        tApool = Pool(tbuf[3:5] + tbuf[5:6])
        banks_free = [None] * 8

        class Banks:
            def __init__(self, ids):
                self.ids, self.nxt = ids, 0

            def get(self):
                b = self.ids[self.nxt]
                self.nxt = (self.nxt + 1) % len(self.ids)
                return b, banks_free[b]

            @staticmethod
            def rel(b, tok):
                banks_free[b] = tok

        banks = Banks(list(range(8)))

        sp_sems = [pl.new_sem(f"sp{i}") for i in range(8)]
        sp_n = [0]

        def sp_dma(out_ap, in_ap, deps=()):
            i = sp_n[0]
            sp_n[0] += 1
            si = sp_sems[i % 8]
            k = i // 8
            prev = (si, 16 * k) if k > 0 else None
            pl.op("sp", lambda e: e.dma_start(out=out_ap, in_=in_ap), deps=[prev] + list(deps), inc=(si, 16))
            return (si, 16 * (k + 1))

        ring_sem = [pl.new_sem(f"rg{i}") for i in range(NSLOT)]
        ring_fills = [0] * NSLOT
        ring_free = [None] * NSLOT
        ring_nxt = [0]

        def ring_load(dmas):
            s = ring_nxt[0]
            ring_nxt[0] = (s + 1) % NSLOT
            t = ring_t[s]
            for dst_fn, src in dmas:
                dst = dst_fn(t)
                pl.op("pool", lambda e, dst=dst, src=src: e.dma_start(out=dst, in_=src),
                      deps=[ring_free[s]], inc=(ring_sem[s], 16))
                ring_fills[s] += 1
            return s, t, (ring_sem[s], 16 * ring_fills[s])

        def ring_rel(s, tok):
            ring_free[s] = tok

        def colblock(w2d, c0):
            src = w2d.rearrange("(k p) n -> p k n", p=128)[:, :, c0:c0 + 256]
            return [(lambda t: t[:, :].rearrange("p (k c) -> p k c", k=KC), src)]

        def rowblock(w2d, r0):
            src = w2d[r0:r0 + 256, :].rearrange("(j p) n -> p j n", p=128)
            return [(lambda t: t[:, :].rearrange("p (j n) -> p j n", j=2), src)]

        def cview(t):
            return t[:, :].rearrange("p (k c) -> p k c", k=KC)

        def rview(t):
            return t[:, :].rearrange("p (j n) -> p j n", j=2)

        def hT(kc, g):
            return (hT_h if g.kind == "h" else hT_m)[:, kc, g.t0:g.t0 + g.n]

        def xnT(kc, g):
            return (xn_h if g.kind == "h" else xn_m)[:, kc, g.t0:g.t0 + g.n]

        def xn_tile(kc, t):
            return xn_h[:, kc, :] if t == 0 else xn_m[:, kc, (t - 1) * 128:t * 128]

        state = {"h_tok": None, "xn": {}}

        def xn_tok_g(g):
            return state["xn"].get((g.kind, g.t0))

        def xn_tok_t(t):
            return state["xn"].get(("h", 0)) if t == 0 else state["xn"].get(("m", 0 if t <= 4 else 512))

        t_cols = sp_dma(cols[:, :], cols_d[:, :])
        t_cst = sp_dma(cst[:, :], cst_d[:, :])
        t_idb = pl.op("dve", lambda e: e.tensor_copy(out=ident_b[:, :], in_=ident_f), deps=[t_cst])
        t_ones = pl.op("dve", lambda e: e.memset(ones_b[:, :], 1.0 / D))
        epsc = small[:, 90:91]
        t_eps = pl.op("dve", lambda e: e.memset(epsc, EPS))

        def phase_load():
            tok = None
            n = 0
            for t in range(9):
                for qd in range(4):
                    bi, buf, bfree = tApool.get()
                    t_ld = sp_dma(buf[:, :], xin[t * 128:(t + 1) * 128, qd * 512:(qd + 1) * 512], deps=[bfree])
                    b, bf = banks.get()
                    for j in range(4):
                        t_mm = pl.op("pe", lambda e, b=b, j=j, buf=buf: e.transpose(
                            out=psf(b)[:, j * 128:(j + 1) * 128], in_=buf[:, j * 128:(j + 1) * 128], identity=ident_f),
                            deps=[t_ld, bf, t_cst], ms=(j == 3))
                    tApool.rel(bi, t_mm)
                    if t == 0:
                        dst = hT_h[:, qd * 4:(qd + 1) * 4, :]
                    else:
                        dst = hT_m[:, qd * 4:(qd + 1) * 4, (t - 1) * 128:t * 128]
                    src = psf(b).rearrange("p (j n) -> p j n", j=4)
                    tok = pl.op("dve", lambda e, dst=dst, src=src: e.tensor_copy(out=dst, in_=src), deps=[t_mm])
                    banks.rel(b, tok)
                    n += 1
            state["h_tok"] = tok

        def rstd_group(g):
            n = g.n
            b, bf = banks.get()
            t_mm = None
            for kc in range(KC):
                qi, sq, sqf = sqpool.get()
                sqb = sq.bitcast(BF16)
                t_sq = pl.op("act", lambda e, sqb=sqb, kc=kc: e.activation(out=sqb[:, :n], in_=hT(kc, g), func=AF.Square),
                             deps=[sqf, state["h_tok"]])
                t_mm = pl.op("pe", lambda e, sqb=sqb, kc=kc, b=b: e.matmul(
                    psf(b)[:, :n], lhsT=ones_b[:, :], rhs=sqb[:, :n], start=(kc == 0), stop=(kc == KC - 1)),
                    deps=[t_sq, bf if kc == 0 else None, t_ones], ms=True)
                sqpool.rel(qi, t_mm)
            ri, rs, rsf = rspool.get()
            t_r0 = pl.op("act", lambda e, b=b, rs=rs: e.activation(
                out=rs[:, :n], in_=psf(b)[:, :n], func=AF.Sqrt, bias=epsc, scale=1.0), deps=[t_mm, rsf, t_eps])
            banks.rel(b, t_r0)
            t_r = pl.op("dve", lambda e, rs=rs: e.reciprocal(out=rs[:, :n], in_=rs[:, :n]), deps=[t_r0])
            return ri, rs, t_r

        def rmsnorm(cbase, groups):
            tok = None
            for g in groups:
                n = g.n
                ri, rs, t_r = rstd_group(g)
                for kc in range(KC):
                    tok = pl.op("dve", lambda e, kc=kc, g=g, rs=rs: e.scalar_tensor_tensor(
                        out=xnT(kc, g), in0=hT(kc, g), scalar=cols[:, cbase + kc:cbase + kc + 1], in1=rs[:, :g.n],
                        op0=ALU.mult, op1=ALU.mult), deps=[t_r, t_cols, state["h_tok"]])
                rspool.rel(ri, tok)
                state["xn"][(g.kind, g.t0)] = tok

        def accum(b, g, dc, t_mm, bias_col=None):
            n = g.n
            if bias_col is None:
                tok = pl.op("dve", lambda e: e.tensor_tensor(out=hT(dc, g), in0=psf(b)[:, :n], in1=hT(dc, g), op=ALU.add),
                            deps=[t_mm])
            else:
                tok = pl.op("dve", lambda e: e.scalar_tensor_tensor(
                    out=hT(dc, g), in0=psf(b)[:, :n], scalar=bias_col, in1=hT(dc, g), op0=ALU.add, op1=ALU.add),
                    deps=[t_mm])
            banks.rel(b, tok)
            state["h_tok"] = tok
            return tok

        def phase_ffn(l, groups):
            rmsnorm(C_FFN + 16 * l, groups)
            ntok = NHALO + NMAIN
            hid = [A[:, i * 4 * ntok:(i + 1) * 4 * ntok].rearrange("p (j n) -> p j n", j=4) for i in range(2)]
            sgp = Pool([f32view(A, 2 * 4 * ntok + i * 1024, 1024) for i in range(3)])
            hid_tok = [None, None]
            fen = pl.fence()

            def GU(s):
                hb = hid[s % 2]
                last = None
                for half in range(2):
                    c0 = s * 512 + half * 256
                    sg_, tg_, lg = ring_load(colblock(wg_d[l], c0))
                    su_, tu_, lu = ring_load(colblock(wu_d[l], c0))
                    wg, wu = cview(tg_), cview(tu_)
                    tU = None
                    for j in range(2):
                        jj = half * 2 + j
                        for g in groups:
                            n = g.n
                            bg, fg = banks.get()
                            bu, fu = banks.get()
                            for kc in range(KC):
                                tG = pl.op("pe", lambda e, kc=kc, g=g, bg=bg, j=j, wg=wg: e.matmul(
                                    psf(bg)[:, :g.n], lhsT=wg[:, kc, j * 128:(j + 1) * 128], rhs=xnT(kc, g),
                                    start=(kc == 0), stop=(kc == KC - 1)),
                                    deps=[lg, xn_tok_g(g), fg], ms=(kc == KC - 1))
                            for kc in range(KC):
                                tU = pl.op("pe", lambda e, kc=kc, g=g, bu=bu, j=j, wu=wu: e.matmul(
                                    psf(bu)[:, :g.n], lhsT=wu[:, kc, j * 128:(j + 1) * 128], rhs=xnT(kc, g),
                                    start=(kc == 0), stop=(kc == KC - 1)),
                                    deps=[lu, fu], ms=(kc == KC - 1))
                            si, sgt, sf = sgp.get()
                            t1 = pl.op("act", lambda e, sgt=sgt, bg=bg, n=n: e.activation(
                                out=sgt[:, :n], in_=psf(bg)[:, :n], func=AF.Silu), deps=[tG, sf, fen])
                            banks.rel(bg, t1)
                            t2 = pl.op("dve", lambda e, sgt=sgt, bu=bu, n=n, jj=jj, g=g: e.tensor_tensor(
                                out=hb[:, jj, g.a0:g.a0 + n], in0=psf(bu)[:, :n], in1=sgt[:, :n], op=ALU.mult),
                                deps=[tU, t1, fen])
                            banks.rel(bu, t2)
                            sgp.rel(si, t2)
                            last = t2
                    ring_rel(sg_, tU)
                    ring_rel(su_, tU)
                hid_tok[s % 2] = last

            def DOWN(s):
                hb = hid[s % 2]
                r0 = s * 512
                s0, t0_, l0 = ring_load(rowblock(wd_d[l], r0))
                s1, t1_, l1 = ring_load(rowblock(wd_d[l], r0 + 256))
                wd = [rview(t0_), rview(t1_)]
                tmm = None
                for dc in range(KC):
                    for g in groups:
                        b, bf = banks.get()
                        for jj in range(4):
                            tmm = pl.op("pe", lambda e, jj=jj, g=g, b=b, dc=dc: e.matmul(
                                psf(b)[:, :g.n], lhsT=wd[jj // 2][:, jj % 2, dc * 128:(dc + 1) * 128],
                                rhs=hb[:, jj, g.a0:g.a0 + g.n], start=(jj == 0), stop=(jj == 3)),
                                deps=[l0, l1, hid_tok[s % 2], bf], ms=(jj == 3))
                        accum(b, g, dc, tmm)
                ring_rel(s0, tmm)
                ring_rel(s1, tmm)

            GU(0)
            for s in range(NSC):
                if s + 1 < NSC:
                    GU(s + 1)
                DOWN(s)

        def gelu_from_psum(src, n, hxp, wp, out_fn, extra_deps, sq_on_dve=False):
            hi, hx, hf = hxp.get()
            wi, w, wf = wp.get()
            t_h = pl.op("act", lambda e: e.activation(out=hx[:, :n], in_=src, func=AF.Copy, scale=0.5),
                        deps=[hf] + list(extra_deps))
            if sq_on_dve:
                t_w = pl.op("dve", lambda e: e.scalar_tensor_tensor(
                    out=w[:, :n], in0=hx[:, :n], scalar=4.0 * 0.044715, in1=hx[:, :n], op0=ALU.mult, op1=ALU.mult),
                    deps=[t_h, wf])
            else:
                t_w = pl.op("act", lambda e: e.activation(out=w[:, :n], in_=src, func=AF.Square, scale=math.sqrt(0.044715)),
                            deps=[wf])
            t_t = pl.op("dve", lambda e: e.scalar_tensor_tensor(
                out=w[:, :n], in0=w[:, :n], scalar=1.0, in1=hx[:, :n], op0=ALU.add, op1=ALU.mult), deps=[t_h, t_w])
            t_th = pl.op("act", lambda e: e.activation(out=w[:, :n], in_=w[:, :n], func=AF.Tanh, scale=2.0 * GELU_C),
                         deps=[t_t])
            t_o = out_fn(hx, w, [t_th])
            hxp.rel(hi, t_o)
            wp.rel(wi, t_o)
            return (t_h if sq_on_dve else t_w), t_o

        def phase_gmlp(l):
            i = l
            groups = G01
            rmsnorm(C_MIX + 16 * l, groups)
            fen = pl.fence()
            v = A[:, :].rearrange("p (t d) -> p t d", t=9)
            ntok = NHALO + NMAIN
            gated = [Bq[:, k * 2 * ntok:(k + 1) * 2 * ntok].rearrange("p (j n) -> p j n", j=2) for k in range(2)]
            h256 = [tbuf[3][:, 0:256], tbuf[3][:, 256:512], tbuf[4][:, 0:256]]
            w256 = [tbuf[4][:, 256:512], tbuf[5][:, 0:256], tbuf[5][:, 256:512]]
            hxp, wp = Pool(h256), Pool(w256)
            halves = [tbuf[k][:, c:c + 256] for k in range(6) for c in (0, 256)]
            hxpA, wpA = Pool(halves[0::2]), Pool(halves[1::2])
            u256 = Pool([tbuf[0][:, 0:256], tbuf[0][:, 256:512], tbuf[1][:, 0:256]]
                        + [Bq[:, k * 512:(k + 1) * 512].bitcast(F32) for k in range(9)])
            g256 = Pool([tbuf[1][:, 256:512], tbuf[2][:, 0:256]])
            VtF = Vt[:, :, :].rearrange("p a b -> p (a b)")
            WsT = VtF[:, 0:1024].rearrange("p (g t) -> p g t", g=8)
            bbp = Pool([VtF[:, 1024:1280].bitcast(F32), VtF[:, 1280:1536].bitcast(F32)])
            junk = VtF[:, 1536:1792]
            ssv = small[:, 0:72].rearrange("p (t c) -> p t c", t=9)
            ss9 = small[:, 72:81]
            rsv = small[:, 81:90]

            t_z = pl.op("dve", lambda e: e.memset(ssv, 0.0), deps=[fen])
            t_ws = None
            for hh in range(2):
                ti, tb, tf = sqpool.get()
                t_ld = sp_dma(tb[:, :].rearrange("p (g t) -> p g t", g=4), wst_d[i, :, hh * 4:(hh + 1) * 4, :], deps=[tf, fen])
                for gg in range(4):
                    t_ws = pl.op("dve", lambda e, tb=tb, gg=gg, hh=hh: e.tensor_tensor(
                        out=WsT[:, hh * 4 + gg, :], in0=tb[:, gg * 128:(gg + 1) * 128], in1=triu, op=ALU.mult),
                        deps=[t_ld, t_cst, fen])
                sqpool.rel(ti, t_ws)

            unitsA = [(cb, t) for cb in range(8) for t in range(9)]
            stA = {}
            slotA = [None]
            junk_tok = [None]

            def A1(u):
                cb, t = u
                c = stA.setdefault(u, {})
                if t == 0:
                    sl, tl, ld = ring_load(colblock(win_d[i], D + cb * 256))
                    slotA[0] = (sl, cview(tl), ld)
                sl, wv, ld = slotA[0]
                b, bf = banks.get()
                tmm = None
                for kc in range(KC):
                    tmm = pl.op("pe", lambda e, kc=kc, b=b: e.matmul(
                        psf(b)[:, :256], lhsT=xn_tile(kc, t), rhs=wv[:, kc, :], start=(kc == 0), stop=(kc == KC - 1)),
                        deps=[ld, xn_tok_t(t), bf], ms=(kc == KC - 1))
                if t == 8:
                    ring_rel(sl, tmm)
                hi, hx, hf = hxpA.get()
                t_h = pl.op("act", lambda e: e.activation(out=hx[:, :256], in_=psf(b)[:, :256], func=AF.Copy, scale=0.5),
                            deps=[hf, tmm, t_ws])
                banks.rel(b, t_h)
                c.update(hi=hi, hx=hx, t_h=t_h)

            def A2(u):
                c = stA[u]
                hx = c["hx"]
                wi, w, wf = wpA.get()
                t_w = pl.op("dve", lambda e: e.scalar_tensor_tensor(
                    out=w[:, :256], in0=hx[:, :256], scalar=4.0 * 0.044715, in1=hx[:, :256], op0=ALU.mult, op1=ALU.mult),
                    deps=[c["t_h"], wf, t_ws])
                t_t = pl.op("dve", lambda e: e.scalar_tensor_tensor(
                    out=w[:, :256], in0=w[:, :256], scalar=1.0, in1=hx[:, :256], op0=ALU.add, op1=ALU.mult), deps=[t_w])
                c.update(wi=wi, w=w, t_t=t_t)

            def A3(u):
                c = stA[u]
                w = c["w"]
                c["t_th"] = pl.op("act", lambda e: e.activation(out=w[:, :256], in_=w[:, :256], func=AF.Tanh,
                                                              scale=2.0 * GELU_C), deps=[c["t_t"]])

            def A4(u):
                cb, t = u
                c = stA[u]
                hx, w = c["hx"], c["w"]
                t_v = pl.op("dve", lambda e: e.scalar_tensor_tensor(
                    out=v[:, t, cb * 256:(cb + 1) * 256], in0=w[:, :256], scalar=1.0, in1=hx[:, :256],
                    op0=ALU.add, op1=ALU.mult), deps=[c["t_th"], fen])
                hxpA.rel(c["hi"], t_v)
                wpA.rel(c["wi"], t_v)
                c["t_v"] = t_v

            def A5(u):
                cb, t = u
                c = stA.pop(u)
                junk_tok[0] = pl.op("act", lambda e: e.activation(
                    out=junk, in_=v[:, t, cb * 256:(cb + 1) * 256], func=AF.Square,
                    accum_out=ssv[:, t, cb:cb + 1]), deps=[c["t_v"], t_z, junk_tok[0]])

            stagesA = [A1, A2, A3, A4, A5]
            for step in range(len(unitsA) + len(stagesA) - 1):
                for sidx in reversed(range(len(stagesA))):
                    k = step - sidx
                    if 0 <= k < len(unitsA):
                        stagesA[sidx](unitsA[k])
            t_a = pl.op("dve", lambda e: e.tensor_reduce(out=ss9, in_=ssv, axis=AX.X, op=ALU.add), deps=[pl.last("act")])
            t_b = pl.op("dve", lambda e: e.tensor_scalar(out=ss9, in0=ss9, scalar1=1.0 / D, scalar2=EPS,
                                                       op0=ALU.mult, op1=ALU.add), deps=[t_a])
            t_c0 = pl.op("act", lambda e: e.activation(out=rsv, in_=ss9, func=AF.Sqrt), deps=[t_b])
            t_c = pl.op("dve", lambda e: e.reciprocal(out=rsv, in_=rsv), deps=[t_c0])
            t_vn = None
            for t in range(9):
                t_vn = pl.op("act", lambda e, t=t: e.activation(out=v[:, t, :], in_=v[:, t, :], func=AF.Copy,
                                                              scale=rsv[:, t:t + 1]), deps=[t_c])

            fenB = pl.fence()
            last_g = [None]

            def UG(g8):
                slU, tlU, ldU = ring_load(colblock(win_d[i], g8 * 256))
                wu = cview(tlU)
                bi, bb, bbf = bbp.get()
                t_bb = sp_dma(bb, bs_d[i, g8, :].partition_broadcast(128), deps=[bbf])
                last = None
                tmm = None
                upieces = {}
                for j in range(2):
                    for g in groups:
                        n = g.n
                        b, bf = banks.get()
                        for kc in range(KC):
                            tmm = pl.op("pe", lambda e, kc=kc, g=g, b=b, j=j: e.matmul(
                                psf(b)[:, :g.n], lhsT=wu[:, kc, j * 128:(j + 1) * 128], rhs=xnT(kc, g),
                                start=(kc == 0), stop=(kc == KC - 1)), deps=[ldU, bf], ms=(kc == KC - 1))
                        pieces = [(c, min(256, n - c)) for c in range(0, n, 256)]
                        for pi, (c0, cn) in enumerate(pieces):
                            ui, ut, uf = u256.get()

                            def out_fn(hx, w, deps, ut=ut, cn=cn, uf=uf):
                                return pl.op("dve", lambda e: e.scalar_tensor_tensor(
                                    out=ut[:, :cn], in0=w[:, :cn], scalar=1.0, in1=hx[:, :cn],
                                    op0=ALU.add, op1=ALU.mult), deps=deps + [uf])
                            t_w, t_u = gelu_from_psum(psf(b)[:, c0:c0 + cn], cn, hxp, wp, out_fn, [tmm, fenB])
                            if pi == len(pieces) - 1:
                                banks.rel(b, t_w)
                            upieces[(j, g.a0, pi)] = (ui, ut, t_u)
                for j in range(2):
                    dcg = 2 * g8 + j
                    for g in groups:
                        n = g.n
                        b2, bf2 = banks.get()
                        tmm2 = None
                        for ti, t in enumerate(g.tiles):
                            tmm2 = pl.op("pe", lambda e, ti=ti, t=t, b2=b2, dcg=dcg: e.matmul(
                                psf(b2)[:, ti * 128:(ti + 1) * 128], lhsT=v[:, t, dcg * 128:(dcg + 1) * 128],
                                rhs=WsT[:, g8, :], start=True, stop=True),
                                deps=[t_vn, t_ws, bf2], ms=(ti == len(g.tiles) - 1))
                        pieces = [(c, min(256, n - c)) for c in range(0, n, 256)]
                        for pi, (c0, cn) in enumerate(pieces):
                            ui, ut, t_u = upieces[(j, g.a0, pi)]
                            gi, gt, gf = g256.get()
                            nt = cn // 128
                            t_g1 = pl.op("dve", lambda e, gt=gt, c0=c0, cn=cn, nt=nt, b2=b2, dcg=dcg, bb=bb: e.scalar_tensor_tensor(
                                out=gt[:, :cn].rearrange("p (a t) -> p a t", a=nt),
                                in0=psf(b2)[:, c0:c0 + cn].rearrange("p (a t) -> p a t", a=nt),
                                scalar=cols[:, C_NV + 16 * i + dcg:C_NV + 16 * i + dcg + 1],
                                in1=bb.unsqueeze(1).broadcast_to([128, nt, 128]),
                                op0=ALU.mult, op1=ALU.add), deps=[tmm2, t_bb, gf, t_cols, fenB])
                            if pi == len(pieces) - 1:
                                banks.rel(b2, t_g1)
                            tl0 = g.tiles[c0 // 128]
                            t_g2 = pl.op("dve", lambda e, gt=gt, ut=ut, cn=cn, nt=nt, tl0=tl0, dcg=dcg: e.tensor_tensor(
                                out=v[:, tl0:tl0 + nt, dcg * 128:(dcg + 1) * 128],
                                in0=gt[:, :cn].rearrange("p (a t) -> p a t", a=nt),
                                in1=ut[:, :cn].rearrange("p (a t) -> p a t", a=nt), op=ALU.mult),
                                deps=[t_g1, t_u, tmm2])
                            u256.rel(ui, t_g2)
                            g256.rel(gi, t_g2)
                            last = t_g2
                ring_rel(slU, tmm)
                bbp.rel(bi, last)
                last_g[0] = last

            for g8 in range(8):
                UG(g8)
            for cbk in range(8):
                slO, tlO, ldO = ring_load(colblock(wout_d[i], cbk * 256))
                wo = cview(tlO)
                tmm = None
                for j in range(2):
                    dc = cbk * 2 + j
                    for g in groups:
                        b, bf = banks.get()
                        nt = g.n // 128
                        t0_ = g.tiles[0]
                        for kc in range(KC):
                            tmm = pl.op("pe", lambda e, kc=kc, g=g, b=b, j=j, wo=wo, nt=nt, t0_=t0_: e.matmul(
                                psf(b)[:, :g.n].rearrange("p (a t) -> p a t", a=nt),
                                lhsT=wo[:, kc, j * 128:(j + 1) * 128],
                                rhs=v[:, t0_:t0_ + nt, kc * 128:(kc + 1) * 128],
                                start=(kc == 0), stop=(kc == KC - 1)), deps=[ldO, last_g[0], bf], ms=(kc == KC - 1))
                        accum(b, g, dc, tmm)
                ring_rel(slO, tmm)

        KT = Bq[:, :].rearrange("p (m n) -> p m n", m=4)

        def phase_kv():
            groups = G01
            rmsnorm(C_KV, groups)
            fen = pl.fence()
            bv = tbuf[3][:, 0:256]
            t_bv = sp_dma(bv, bkv_d[256:512].partition_broadcast(128), deps=[fen])
            for half in range(2):
                dm = []
                for h2 in range(2):
                    src = wkv_d[:, (half * 2 + h2) * 64:(half * 2 + h2 + 1) * 64].rearrange("(k p) c -> p k c", p=128)
                    for dup in range(2):
                        dm.append((lambda t, dup=dup, h2=h2: t[:, :].rearrange(
                            "p (k h u c) -> p k h u c", k=KC, h=2, u=2)[:, :, h2, dup, :], src))
                sl, tl, ld = ring_load(dm)
                wk = cview(tl)
                tmm = None
                for hh in range(2):
                    m = half * 2 + hh
                    for g in groups:
                        b, bf = banks.get()
                        for kc in range(KC):
                            tmm = pl.op("pe", lambda e, kc=kc, g=g, b=b, hh=hh, wk=wk: e.matmul(
                                psf(b)[:, :g.n], lhsT=wk[:, kc, hh * 128:(hh + 1) * 128], rhs=xnT(kc, g),
                                start=(kc == 0), stop=(kc == KC - 1)), deps=[ld, xn_tok_g(g), bf], ms=(kc == KC - 1))
                        t_e = pl.op("act", lambda e, g=g, b=b, m=m: e.activation(
                            out=KT[:, m, g.a0:g.a0 + g.n], in_=psf(b)[:, :g.n], func=AF.Identity,
                            bias=cols[:, C_BK + m:C_BK + m + 1], scale=1.0), deps=[tmm, t_cols, fen])
                        banks.rel(b, t_e)
                ring_rel(sl, tmm)
            sl, tl, ld = ring_load(colblock(wkv_d, 256))
            wv = cview(tl)
            tmm = None
            for t in range(9):
                b, bf = banks.get()
                for kc in range(KC):
                    tmm = pl.op("pe", lambda e, kc=kc, t=t, b=b: e.matmul(
                        psf(b)[:, :256], lhsT=xn_tile(kc, t), rhs=wv[:, kc, :], start=(kc == 0), stop=(kc == KC - 1)),
                        deps=[ld, bf], ms=(kc == KC - 1))
                t_e = pl.op("dve", lambda e, t=t, b=b: e.tensor_tensor(out=Vt[:, t, :], in0=psf(b)[:, :256], in1=bv, op=ALU.add),
                            deps=[tmm, t_bv, fen])
                banks.rel(b, t_e)
            ring_rel(sl, tmm)

        def phase_attn(l):
            i = l - 2
            groups = G23
            rmsnorm(C_MIX + 16 * l, groups)
            fen = pl.fence()
            attnT = A[:, 0:16384].rearrange("p (k n) -> p k n", k=KC)
            qTz = [A[:, 16384:17408], A[:, 17408:18432]]
            t_qz = pl.op("dve", lambda e: e.memset(A[:, 16384:18432], 0.0), deps=[fen])
            hh_f = hT_h[:, :, :].rearrange("p k n -> p (k n)")
            braw = hh_f[:, 0:512].rearrange("p (e k) -> p e k", e=2)
            biasp = [hh_f[:, 512:1024].rearrange("p (e k) -> p e k", e=2), hh_f[:, 1024:1536].rearrange("p (e k) -> p e k", e=2)]
            bias0 = hh_f[:, 1536:2048].rearrange("p (e k) -> p e k", e=2)
            scp = Pool([tbuf[3][:, :].rearrange("p (e k) -> p e k", e=2), tbuf[4][:, :].rearrange("p (e k) -> p e k", e=2)])
            xh_b = xn_h[:, :, :].rearrange("p k n -> p (k n)")
            pp = Pool([xh_b[:, k * 512:(k + 1) * 512].rearrange("p (e k) -> p e k", e=2) for k in range(2)])
            pTp = Pool([xh_b[:, 1024 + k * 512:1024 + (k + 1) * 512].rearrange("p (a q) -> p a q", a=4) for k in range(2)])
            t5b = tbuf[5][:, :].bitcast(BF16)
            op_ = Pool([t5b[:, k * 128:(k + 1) * 128] for k in range(3)])
            sink_bc = small[:, 96:128]
            mask0 = tbuf[2][:, 0:256]
            t_sk = sp_dma(sink_bc, sink_d[i, :].partition_broadcast(128), deps=[fen])
            t_m0 = sp_dma(mask0, mask0_d[:, :], deps=[fen])
            sm = small[:, 128:256]
            smp = Pool([sm[:, k * 16:(k + 1) * 16] for k in range(8)])
            poolS, poolT, poolO, poolX = Banks([0, 1]), Banks([2, 3]), Banks([4, 5]), Banks([6, 7])
            bias_tok = [None, None]
            bias0_tok = [None]
            braw_free = [None]
            stt = {}
            wq_slot = [None]

            def SA(it):
                hp, qb = it
                c = stt.setdefault(it, {})
                if qb == 0:
                    if hp % 2 == 0:
                        if wq_slot[0] is not None:
                            ring_rel(wq_slot[0][0], wq_slot[0][3])
                        sl, tl, ld = ring_load(colblock(wq_d[i], (hp // 2) * 256))
                        wq_slot[0] = [sl, cview(tl), ld, None]
                    wq = wq_slot[0][1]
                    ld = wq_slot[0][2]
                    t_q = None
                    for g in groups:
                        b, bf = poolS.get()
                        tmm = None
                        for kc in range(KC):
                            tmm = pl.op("pe", lambda e, kc=kc, g=g, b=b: e.matmul(
                                psf(b)[:, :g.n], lhsT=wq[:, kc, (hp % 2) * 128:(hp % 2 + 1) * 128], rhs=xnT(kc, g),
                                start=(kc == 0), stop=(kc == KC - 1)), deps=[ld, xn_tok_g(g), bf], ms=(kc == KC - 1))
                        wq_slot[0][3] = tmm
                        for e2 in range(2):
                            t_q = pl.op("act", lambda e, g=g, b=b, e2=e2: e.activation(
                                out=qTz[e2][e2 * 64:(e2 + 1) * 64, g.t0:g.t0 + g.n], in_=psf(b)[e2 * 64:(e2 + 1) * 64, :g.n],
                                func=AF.Identity, bias=cols[e2 * 64:(e2 + 1) * 64, C_BQ + 16 * i + hp:C_BQ + 16 * i + hp + 1],
                                scale=1.0), deps=[tmm, t_cols, fen, t_qz])
                        Banks.rel(b, t_q)
                    stt["q_tok"] = t_q
                    t_ld = sp_dma(braw, biasg_d[:, 2 * hp:2 * hp + 2, :], deps=[braw_free[0], fen])
                    t_b0 = pl.op("dve", lambda e: e.tensor_tensor(
                        out=bias0, in0=braw, in1=mask0.unsqueeze(1).broadcast_to([128, 2, 256]), op=ALU.add),
                        deps=[t_ld, t_m0, bias0_tok[0]])
                    bp = biasp[hp % 2]
                    t_b1 = pl.op("dve", lambda e: e.tensor_tensor(
                        out=bp, in0=braw, in1=maskadd.unsqueeze(1).broadcast_to([128, 2, 256]), op=ALU.add),
                        deps=[t_ld, t_cst, bias_tok[hp % 2]])
                    braw_free[0] = t_b1
                    stt["b0"] = t_b0
                    stt["b1"] = t_b1
                c["b0"], c["b1"] = stt["b0"], stt["b1"]
                kvh = hp // 4
                b, bf = poolS.get()
                tmm = None
                for e2 in range(2):
                    tmm = pl.op("pe", lambda e, e2=e2, b=b: e.matmul(
                        psf(b)[:, e2 * 256:(e2 + 1) * 256], lhsT=qTz[e2][:, qb * 128:(qb + 1) * 128],
                        rhs=KT[:, kvh, qb * 128:qb * 128 + 256], start=True, stop=True),
                        deps=[stt["q_tok"], bf, fen], ms=(e2 == 1))
                c["bS"], c["tS"] = b, tmm

            def SB(it):
                hp, qb = it
                c = stt[it]
                b = c["bS"]
                si, sc, sf = scp.get()
                bsel = bias0 if qb == 0 else biasp[hp % 2]
                t_sc = pl.op("dve", lambda e: e.scalar_tensor_tensor(
                    out=sc, in0=psf(b)[:, :].rearrange("p (e k) -> p e k", e=2), scalar=0.125, in1=bsel,
                    op0=ALU.mult, op1=ALU.add), deps=[c["tS"], sf, c["b0"], c["b1"]])
                Banks.rel(b, t_sc)
                if qb == 0:
                    bias0_tok[0] = t_sc
                bias_tok[hp % 2] = t_sc
                mi, ms_, mf = smp.get()
                mx, negm, dd, rs, den, rden = (ms_[:, 0:2], ms_[:, 2:4], ms_[:, 4:6], ms_[:, 6:8], ms_[:, 8:10], ms_[:, 10:12])
                sk = sink_bc[:, 2 * hp:2 * hp + 2]
                t1 = pl.op("dve", lambda e: e.tensor_reduce(out=mx, in_=sc, axis=AX.X, op=ALU.max), deps=[t_sc, mf])
                t2 = pl.op("dve", lambda e: e.tensor_tensor(out=mx, in0=mx, in1=sk, op=ALU.max), deps=[t1, t_sk])
                t3 = pl.op("dve", lambda e: e.tensor_scalar(out=negm, in0=mx, scalar1=-1.0, scalar2=None, op0=ALU.mult), deps=[t2])
                t4 = pl.op("dve", lambda e: e.tensor_tensor(out=dd, in0=sk, in1=negm, op=ALU.add), deps=[t3])
                t_z = pl.op("dve", lambda e: e.memset(rs, 0.0), deps=[mf])
                c.update(si=si, sc=sc, mi=mi, negm=negm, dd=dd, rs=rs, den=den, rden=rden, tB=[t3, t4, t_z])

            def SC(it):
                c = stt[it]
                sc, negm, rs, dd = c["sc"], c["negm"], c["rs"], c["dd"]
                pi, p, pf = pp.get()
                t_e = None
                for e2 in range(2):
                    t_e = pl.op("act", lambda e, e2=e2: e.activation(
                        out=p[:, e2, :], in_=sc[:, e2, :], func=AF.Exp, bias=negm[:, e2:e2 + 1], scale=1.0,
                        accum_out=rs[:, e2:e2 + 1]), deps=c["tB"] + [pf])
                scp.rel(c["si"], t_e)
                t5 = pl.op("act", lambda e: e.activation(out=dd, in_=dd, func=AF.Exp), deps=c["tB"])
                c.update(pi=pi, p=p, tC=t5, tE=t_e)

            def SD(it):
                c = stt[it]
                p, den, rden, rs, dd = c["p"], c["den"], c["rden"], c["rs"], c["dd"]
                bT, bfT = poolT.get()
                tmm = None
                for e2 in range(2):
                    for kb in range(2):
                        a = e2 * 2 + kb
                        tmm = pl.op("pe", lambda e, a=a, e2=e2, kb=kb: e.transpose(
                            out=psb(bT)[:, a * 128:(a + 1) * 128], in_=p[:, e2, kb * 128:(kb + 1) * 128], identity=ident_b[:, :]),
                            deps=[c["tE"], bfT, t_idb], ms=(a == 3))
                pp.rel(c["pi"], tmm)
                c["bT"], c["tT"] = bT, tmm
                t6 = pl.op("dve", lambda e: e.tensor_tensor(out=den, in0=rs, in1=dd, op=ALU.add), deps=[c["tC"], c["tE"]])
                t7 = pl.op("dve", lambda e: e.reciprocal(out=rden, in_=den), deps=[t6])
                c["t_rden"] = t7

            def SE(it):
                c = stt[it]
                bT = c["bT"]
                ti, pT, tf = pTp.get()
                t_c = pl.op("act", lambda e: e.activation(
                    out=pT, in_=psb(bT)[:, 0:512].rearrange("p (a q) -> p a q", a=4), func=AF.Copy), deps=[c["tT"], tf])
                Banks.rel(bT, t_c)
                c.update(ti=ti, pT=pT, tPT=t_c)

            def SF(it):
                hp, qb = it
                c = stt[it]
                kvh = hp // 4
                pT = c["pT"]
                bO, bfO = poolO.get()
                tmm = None
                for e2 in range(2):
                    for kb in range(2):
                        tmm = pl.op("pe", lambda e, e2=e2, kb=kb: e.matmul(
                            psf(bO)[:, e2 * 64:(e2 + 1) * 64], lhsT=pT[:, e2 * 2 + kb, :],
                            rhs=Vt[:, qb + kb, kvh * 64:(kvh + 1) * 64], start=(kb == 0), stop=(kb == 1)),
                            deps=[c["tPT"], bfO, fen], ms=(e2 == 1 and kb == 1))
                pTp.rel(c["ti"], tmm)
                c["bO"], c["tO"] = bO, tmm

            def SG(it):
                c = stt[it]
                bO = c["bO"]
                oi, o, of = op_.get()
                t_o = pl.op("dve", lambda e: e.tensor_tensor(
                    out=o.rearrange("p (e d) -> p e d", e=2), in0=psf(bO)[:, 0:128].rearrange("p (e d) -> p e d", e=2),
                    in1=c["rden"].unsqueeze(2).broadcast_to([128, 2, 64]), op=ALU.mult), deps=[c["tO"], c["t_rden"], of])
                Banks.rel(bO, t_o)
                smp.rel(c["mi"], t_o)
                c.update(oi=oi, o=o, t_o=t_o)

            def SH(it):
                c = stt[it]
                o = c["o"]
                bX, bfX = poolX.get()
                tmm = pl.op("pe", lambda e: e.transpose(out=psb(bX)[:, 0:128], in_=o, identity=ident_b[:, :]),
                            deps=[c["t_o"], bfX], ms=True)
                op_.rel(c["oi"], tmm)
                c["bX"], c["tX"] = bX, tmm

            def SI(it):
                hp, qb = it
                c = stt[it]
                bX = c["bX"]
                t_a = pl.op("act", lambda e: e.activation(out=attnT[:, hp, qb * 128:(qb + 1) * 128], in_=psb(bX)[:, 0:128],
                                                        func=AF.Copy), deps=[c["tX"], fen])
                Banks.rel(bX, t_a)
                stt["attn_tok"] = t_a
                del stt[it]

            its = [(hp, qb) for hp in range(16) for qb in range(8)]
            stages = [SA, SB, SC, SD, SE, SF, SG, SH, SI]
            for k in range(1, 9):
                if f"st{k}" in dbg:
                    stages = stages[:k]
                    its = its[:16]
            for step in range(len(its) + len(stages) - 1):
                for sidx in reversed(range(len(stages))):
                    k = step - sidx
                    if 0 <= k < len(its):
                        stages[sidx](its[k])
            ring_rel(wq_slot[0][0], wq_slot[0][3])
            if len(stages) < 9:
                return
            for cbk in range(8):
                sl, tl, ld = ring_load(colblock(wo_d[i], cbk * 256))
                wo = cview(tl)
                tmm = None
                for j in range(2):
                    dc = cbk * 2 + j
                    for g in groups:
                        b, bf = banks.get()
                        for kc in range(KC):
                            tmm = pl.op("pe", lambda e, kc=kc, g=g, b=b, j=j, wo=wo: e.matmul(
                                psf(b)[:, :g.n], lhsT=wo[:, kc, j * 128:(j + 1) * 128], rhs=attnT[:, kc, g.t0:g.t0 + g.n],
                                start=(kc == 0), stop=(kc == KC - 1)), deps=[ld, stt["attn_tok"], bf], ms=(kc == KC - 1))
                        accum(b, g, dc, tmm, bias_col=cols[:, C_BO + 16 * i + dc:C_BO + 16 * i + dc + 1])
                ring_rel(sl, tmm)

        st_sems = [pl.new_sem("st0"), pl.new_sem("st1")]
        n_st = [0, 0]

        def phase_out(final):
            groups = G23 if final else G01
            fen = pl.fence()
            fT = [f32view(A, k * 1024, 1024) for k in range(4)]
            ostp = Pool([f32view(A, (4 + k) * 1024, 1024) for k in range(2)])
            fT_free = [None] * 4
            row0 = 0 if final else None
            for g in groups:
                n = g.n
                if final:
                    ri, rs, t_r = rstd_group(g)
                for kq in range(4):
                    srcs = []
                    if final:
                        for j in range(4):
                            kc = kq * 4 + j
                            t_f = pl.op("dve", lambda e, kc=kc, j=j, g=g, rs=rs: e.scalar_tensor_tensor(
                                out=fT[j][:, :g.n], in0=hT(kc, g), scalar=cols[:, C_FIN + kc:C_FIN + kc + 1], in1=rs[:, :g.n],
                                op0=ALU.mult, op1=ALU.mult), deps=[t_r, fT_free[j], fen, t_cols])
                            srcs.append((fT[j], 0, t_f))
                    else:
                        for j in range(4):
                            kc = kq * 4 + j
                            base = hT_h if g.kind == "h" else hT_m
                            srcs.append((base[:, kc, :], g.t0, state["h_tok"]))
                    for ti in range(n // 128):
                        b, bf = banks.get()
                        tmm = None
                        for j in range(4):
                            s_ap, off, s_tok = srcs[j]
                            tmm = pl.op("pe", lambda e, s_ap=s_ap, off=off, j=j, ti=ti, b=b: e.transpose(
                                out=psf(b)[:, j * 128:(j + 1) * 128], in_=s_ap[:, off + ti * 128:off + (ti + 1) * 128],
                                identity=ident_f), deps=[s_tok, bf, t_cst], ms=(j == 3))
                        oi, ost, of = ostp.get()
                        t_c = pl.op("act", lambda e, ost=ost, b=b: e.activation(out=ost[:, :], in_=psf(b), func=AF.Copy),
                                    deps=[tmm, of, fen])
                        banks.rel(b, t_c)
                        if final:
                            r = g.t0 + ti * 128
                        else:
                            r = g.a0 + ti * 128
                        pl.op("sp", lambda e, ost=ost, r=r, kq=kq: e.dma_start(
                            out=out[r:r + 128, kq * 512:(kq + 1) * 512], in_=ost[:, :]), deps=[t_c], inc=(st_sems[oi], 16))
                        n_st[oi] += 1
                        ostp.rel(oi, (st_sems[oi], 16 * n_st[oi]))
                    if final:
                        for j in range(4):
                            fT_free[j] = tmm
                if final:
                    rspool.rel(ri, pl.last("dve"))
            pl.wait("sp", (st_sems[0], 16 * n_st[0]))
            pl.wait("sp", (st_sems[1], 16 * n_st[1]))

        phase_load()
        for l in layers:
            if l < 2:
                phase_gmlp(l)
                phase_ffn(l, G01)
            else:
                if l == 2:
                    phase_kv()
                if "kvonly" not in dbg:
                    phase_attn(l)
                if "noffn" not in dbg:
                    phase_ffn(l, G23)
        phase_out(is_last)

        with nc.Block() as block:
            @block.tensor
            def _(e):
                pl.replay("pe", e)

            @block.scalar
            def _(e):
                pl.replay("act", e)

            @block.vector
            def _(e):
                pl.replay("dve", e)

            @block.gpsimd
            def _(e):
                pl.replay("pool", e)

            @block.sync
            def _(e):
                pl.replay("sp", e)
    return nc


def _colize(v):
    return np.ascontiguousarray(np.asarray(v, np.float32).reshape(-1, 128).T)


def _t5_bucket(dist):
    max_exact = 16
    d = np.maximum(dist, 1).astype(np.float32)
    large = max_exact + (np.log(d / np.float32(max_exact)) / np.float32(math.log(128 / max_exact))
                         * np.float32(32 - max_exact)).astype(np.int32)
    large = np.minimum(large, 31)
    return np.where(dist < max_exact, dist, large)


def _host_tables(inp):
    cols = np.zeros((128, NCOL), np.float32)
    for l in range(4):
        cols[:, C_MIX + 16 * l:C_MIX + 16 * (l + 1)] = _colize(inp["mix_norm"][l])
        cols[:, C_FFN + 16 * l:C_FFN + 16 * (l + 1)] = _colize(inp["ffn_norm"][l])
    cols[:, C_KV:C_KV + 16] = _colize(inp["kv_norm"])
    cols[:, C_FIN:C_FIN + 16] = _colize(inp["final_norm"])
    for i in range(2):
        cols[:, C_NV + 16 * i:C_NV + 16 * (i + 1)] = _colize(inp["a_norm_v"][i])
        cols[:, C_BQ + 16 * i:C_BQ + 16 * (i + 1)] = _colize(inp["b_b_q"][i])
        cols[:, C_BO + 16 * i:C_BO + 16 * (i + 1)] = _colize(inp["b_b_o"][i])
    bk = np.asarray(inp["b_kv"], np.float32)[:256].reshape(4, 64)
    cols[:, C_BK:C_BK + 4] = np.concatenate([bk, bk], axis=1).T
    cst = np.zeros((128, 512), np.float32)
    cst[:, 0:128] = np.eye(128, dtype=np.float32)
    s = np.arange(128)[:, None]
    t = np.arange(128)[None, :]
    cst[:, 128:256] = (s <= t).astype(np.float32)
    dist = np.arange(128)[:, None] + 128 - np.arange(256)[None, :]
    in_window = (dist >= 0) & (dist < 128)
    cst[:, 256:512] = np.where(in_window, 0.0, MASKV).astype(np.float32)
    mask0_first = np.where(in_window & (np.arange(256)[None, :] >= 128), 0.0, MASKV).astype(np.float32)
    mask0_second = cst[:, 256:512].copy()
    bucket = _t5_bucket(np.clip(dist, 0, None).astype(np.int32))
    rel = np.asarray(inp["rel_bias"], np.float32)
    biasg = np.ascontiguousarray(rel[bucket].transpose(0, 2, 1))
    return cols, cst, mask0_first, mask0_second, biasg


def _core_x(x, c):
    b, half = c // 2, c % 2
    xm = x[b, half * NMAIN:(half + 1) * NMAIN]
    if half == 0:
        halo = np.zeros((NHALO, D), np.float32)
    else:
        halo = x[b, NMAIN - NHALO:NMAIN]
    return np.ascontiguousarray(np.concatenate([halo, xm], axis=0))


_NC_CACHE = {}


def _get_nc(layers, is_first, is_last):
    key = (tuple(layers), is_first, is_last)
    if key not in _NC_CACHE:
        _NC_CACHE[key] = build(layers, is_first, is_last)
    return _NC_CACHE[key]


def _in_map(inp, tabs, xin, c, layers):
    cols, cst, m0f, m0s, biasg = tabs
    m = {"xin": xin, "cols": cols, "cst": cst,
         "ffn_w_gate": inp["ffn_w_gate"], "ffn_w_up": inp["ffn_w_up"], "ffn_w_down": inp["ffn_w_down"]}
    if any(l < 2 for l in layers):
        m["a_w_in"] = inp["a_w_in"]
        m["a_w_sT"] = inp["_a_w_sT"]
        m["a_b_s"] = inp["a_b_s"]
        m["a_w_out"] = inp["a_w_out"]
    if any(l >= 2 for l in layers):
        m["w_kv"] = inp["w_kv"]
        m["b_kv"] = inp["b_kv"]
        m["b_w_q"] = inp["b_w_q"]
        m["b_w_o"] = inp["b_w_o"]
        m["b_sinks"] = inp["b_sinks"]
        m["biasg"] = biasg
        m["mask0"] = m0f if c % 2 == 0 else m0s
    return m


LAUNCHES = [((0, 1, 2, 3), True, True)]


def kernel(**inputs):
    inp = {k: np.ascontiguousarray(np.asarray(v, np.float32)) for k, v in inputs.items()}
    inp["_a_w_sT"] = np.ascontiguousarray(inp["a_w_s"].transpose(0, 3, 1, 2))
    tabs = _host_tables(inp)
    ncores = 8
    cur = [_core_x(inp["x"], c) for c in range(ncores)]
    for layers, is_first, is_last in LAUNCHES:
        nc = _get_nc(layers, is_first, is_last)
        in_maps = [_in_map(inp, tabs, cur[c], c, layers) for c in range(ncores)]
        res = run_bass_kernel_spmd(nc, in_maps, core_ids=list(range(ncores)))
        cur = [np.asarray(res.results[c]["out"], np.float32) for c in range(ncores)]
    outp = np.zeros((4, 2 * NMAIN, D), np.float32)
    for c in range(ncores):
        outp[c // 2, (c % 2) * NMAIN:(c % 2 + 1) * NMAIN] = cur[c]
    return outp
```

```python
import math
from contextlib import ExitStack

import numpy as np
import concourse.bass as bass
import concourse.mybir as mybir
from concourse.bass_utils import run_bass_kernel_spmd

F32 = mybir.dt.float32
BF16 = mybir.dt.bfloat16
AF = mybir.ActivationFunctionType
ALU = mybir.AluOpType
AX = mybir.AxisListType

D = 2048
KC = 16
DFF = 5632
NSC = 11
EPS = 1e-5
NMAIN = 1024
NHALO = 128
NSLOT = 4
MASKV = -30000.0

C_MIX = 0
C_FFN = 64
C_KV = 128
C_FIN = 144
C_NV = 160
C_BQ = 192
C_BO = 224
C_BK = 256
NCOL = 260

GELU_C = math.sqrt(2.0 / math.pi)


class Grp:
    def __init__(self, kind, t0, n):
        self.kind, self.t0, self.n = kind, t0, n
        self.a0 = t0 if kind == "h" else NHALO + t0
        self.tiles = [self.a0 // 128 + i for i in range(n // 128)]


G01 = [Grp("h", 0, 128), Grp("m", 0, 512), Grp("m", 512, 512)]
G23 = [Grp("m", 0, 512), Grp("m", 512, 512)]


class Plan:
    ENG = ("pe", "act", "dve", "pool", "sp")

    def __init__(self, nc, stack):
        self.nc, self.stack = nc, stack
        self.sems = []
        self.lists = {e: [] for e in self.ENG}
        self.waited = {e: {} for e in self.ENG}
        self.cnt = {}
        self.esem = {}
        for e in self.ENG:
            self.esem[e] = self.new_sem("s_" + e)
            self.cnt[e] = 0

    def new_sem(self, name):
        s = self.stack.enter_context(self.nc.semaphore(name))
        self.sems.append(s)
        return len(self.sems) - 1

    def wait(self, eng, tok):
        if tok is None:
            return
        si, val = tok
        if eng == "pe" and si == self.esem["pe"]:
            return
        if self.waited[eng].get(si, 0) >= val:
            return
        self.waited[eng][si] = val
        self.lists[eng].append(("w", si, val))

    def op(self, eng, fn, deps=(), ms=True, inc=None):
        for d in deps:
            if isinstance(d, list):
                for dd in d:
                    self.wait(eng, dd)
            else:
                self.wait(eng, d)
        if inc is not None:
            self.lists[eng].append(("o", fn, inc[0], inc[1]))
            return None
        if ms:
            self.cnt[eng] += 1
            self.lists[eng].append(("o", fn, self.esem[eng], 1))
            return (self.esem[eng], self.cnt[eng])
        self.lists[eng].append(("o", fn, None, 0))
        return None

    def last(self, eng):
        return (self.esem[eng], self.cnt[eng]) if self.cnt[eng] else None

    def fence(self):
        return [self.last("pe"), self.last("act"), self.last("dve")]

    def replay(self, eng, e):
        for it in self.lists[eng]:
            if it[0] == "w":
                e.wait_ge(self.sems[it[1]], it[2])
            else:
                ins = it[1](e)
                if it[2] is not None:
                    ins.then_inc(self.sems[it[2]], it[3])


class Pool:
    def __init__(self, aps):
        self.aps = aps
        self.free = [None] * len(aps)
        self.nxt = 0

    def get(self):
        i = self.nxt
        self.nxt = (i + 1) % len(self.aps)
        return i, self.aps[i], self.free[i]

    def rel(self, i, tok):
        self.free[i] = tok


def build(layers, is_first, is_last, dbg=()):
    layers = tuple(layers)
    do_g = any(l < 2 for l in layers)
    do_a = any(l >= 2 for l in layers)
    nc = bass.Bass("TRN2", target_bir_lowering=False)

    def dram(name, shape, kind="ExternalInput"):
        return nc.dram_tensor(name, list(shape), F32, kind=kind).ap()

    xin = dram("xin", [NHALO + NMAIN, D])
    n_out = NMAIN if is_last else NHALO + NMAIN
    out = dram("out", [n_out, D], kind="ExternalOutput")
    cols_d = dram("cols", [128, NCOL])
    cst_d = dram("cst", [128, 512])
    wg_d = dram("ffn_w_gate", [4, D, DFF])
    wu_d = dram("ffn_w_up", [4, D, DFF])
    wd_d = dram("ffn_w_down", [4, DFF, D])
    if do_g:
        win_d = dram("a_w_in", [2, D, 2 * D])
        wst_d = dram("a_w_sT", [2, 128, 8, 128])
        bs_d = dram("a_b_s", [2, 8, 128])
        wout_d = dram("a_w_out", [2, D, D])
    if do_a:
        wkv_d = dram("w_kv", [D, 512])
        bkv_d = dram("b_kv", [512])
        wq_d = dram("b_w_q", [2, D, D])
        wo_d = dram("b_w_o", [2, D, D])
        sink_d = dram("b_sinks", [2, 32])
        biasg_d = dram("biasg", [128, 32, 256])
        mask0_d = dram("mask0", [128, 256])

    with ExitStack() as st:
        pl = Plan(nc, st)

        def sb(name, shape, dt):
            return st.enter_context(nc.sbuf_tensor(name, list(shape), dt))

        hT_m = sb("hT_m", [128, KC, NMAIN], F32)
        hT_h = sb("hT_h", [128, KC, NHALO], F32)
        xn_m = sb("xn_m", [128, KC, NMAIN], BF16)
        xn_h = sb("xn_h", [128, KC, NHALO], BF16)
        ring_t = [sb(f"ring{i}", [128, 4096], BF16) for i in range(NSLOT)]
        cols = sb("cols_sb", [128, NCOL], F32)
        cst = sb("cst_sb", [128, 512], F32)
        ident_b = sb("ident_b", [128, 128], BF16)
        ones_b = sb("ones_b", [128, 128], BF16)
        small = sb("small", [128, 256], F32)
        A = sb("arenaA", [128, 18432], BF16)
        Bq = sb("arenaB", [128, 4608], BF16)
        Vt = sb("Vt", [128, 9, 256], BF16)
        T = sb("arenaT", [128, 6144], BF16)
        ps = st.enter_context(nc.psum_tensor("ps", [128, 8, 512], F32))
        ident_f = cst[:, 0:128]
        triu = cst[:, 128:256]
        maskadd = cst[:, 256:512]

        def psf(b):
            return ps[:, b, :]

        def psb(b):
            return ps[:, b, :].bitcast(BF16)

        def f32view(t, off, n):
            return t[:, off:off + n].bitcast(F32)

        tbuf = [f32view(T, i * 1024, 1024) for i in range(6)]
        sqpool = Pool(tbuf[0:2])
        rspool = Pool(tbuf[2:3])
        tApool = Pool(tbuf[3:5] + tbuf[5:6])
        banks_free = [None] * 8
        banks_pending = [False] * 8

        class Banks:
            def __init__(self, ids):
                self.ids, self.nxt = ids, 0

            def get(self):
                for _ in range(len(self.ids)):
                    b = self.ids[self.nxt]
                    self.nxt = (self.nxt + 1) % len(self.ids)
                    if not banks_pending[b]:
                        banks_pending[b] = True
                        return b, banks_free[b]
                raise AssertionError("no free PSUM bank in pool")

            @staticmethod
            def rel(b, tok):
                banks_free[b] = tok
                banks_pending[b] = False

        banks = Banks(list(range(8)))

        sp_sems = [pl.new_sem(f"sp{i}") for i in range(8)]
        sp_n = [0]

        def sp_dma(out_ap, in_ap, deps=()):
            i = sp_n[0]
            sp_n[0] += 1
            si = sp_sems[i % 8]
            k = i // 8
            prev = (si, 16 * k) if k > 0 else None
            pl.op("sp", lambda e: e.dma_start(out=out_ap, in_=in_ap), deps=[prev] + list(deps), inc=(si, 16))
            return (si, 16 * (k + 1))

        ring_sem = [pl.new_sem(f"rg{i}") for i in range(NSLOT)]
        ring_fills = [0] * NSLOT
        ring_free = [None] * NSLOT
        ring_nxt = [0]

        def ring_load(dmas):
            s = ring_nxt[0]
            ring_nxt[0] = (s + 1) % NSLOT
            t = ring_t[s]
            for dst_fn, src in dmas:
                dst = dst_fn(t)
                pl.op("pool", lambda e, dst=dst, src=src: e.dma_start(out=dst, in_=src),
                      deps=[ring_free[s]], inc=(ring_sem[s], 16))
                ring_fills[s] += 1
            return s, t, (ring_sem[s], 16 * ring_fills[s])

        def ring_rel(s, tok):
            ring_free[s] = tok

        def colblock(w2d, c0):
            src = w2d.rearrange("(k p) n -> p k n", p=128)[:, :, c0:c0 + 256]
            return [(lambda t: t[:, :].rearrange("p (k c) -> p k c", k=KC), src)]

        def rowblock(w2d, r0):
            src = w2d[r0:r0 + 256, :].rearrange("(j p) n -> p j n", p=128)
            return [(lambda t: t[:, :].rearrange("p (j n) -> p j n", j=2), src)]

        def cview(t):
            return t[:, :].rearrange("p (k c) -> p k c", k=KC)

        def rview(t):
            return t[:, :].rearrange("p (j n) -> p j n", j=2)

        def hT(kc, g):
            return (hT_h if g.kind == "h" else hT_m)[:, kc, g.t0:g.t0 + g.n]

        def xnT(kc, g):
            return (xn_h if g.kind == "h" else xn_m)[:, kc, g.t0:g.t0 + g.n]

        def xn_tile(kc, t):
            return xn_h[:, kc, :] if t == 0 else xn_m[:, kc, (t - 1) * 128:t * 128]

        state = {"h_tok": None, "xn": {}}

        def xn_tok_g(g):
            return state["xn"].get((g.kind, g.t0))

        def xn_tok_t(t):
            return state["xn"].get(("h", 0)) if t == 0 else state["xn"].get(("m", 0 if t <= 4 else 512))

        t_cols = sp_dma(cols[:, :], cols_d[:, :])
        t_cst = sp_dma(cst[:, :], cst_d[:, :])
        t_idb = pl.op("dve", lambda e: e.tensor_copy(out=ident_b[:, :], in_=ident_f), deps=[t_cst])
        t_ones = pl.op("dve", lambda e: e.memset(ones_b[:, :], 1.0 / D))
        bq8 = sb("bq8", [128, 32], F32)
        t_bq8 = pl.op("dve", lambda e: e.tensor_scalar(out=bq8[:, :], in0=cols[:, C_BQ:C_BQ + 32], scalar1=0.125, scalar2=None,
                                                     op0=ALU.mult), deps=[t_cols])
        epsc = small[:, 90:91]
        t_eps = pl.op("dve", lambda e: e.memset(epsc, EPS))

        def phase_load():
            tok = None
            n = 0
            for t in range(9):
                for qd in range(4):
                    bi, buf, bfree = tApool.get()
                    t_ld = sp_dma(buf[:, :], xin[t * 128:(t + 1) * 128, qd * 512:(qd + 1) * 512], deps=[bfree])
                    b, bf = banks.get()
                    for j in range(4):
                        t_mm = pl.op("pe", lambda e, b=b, j=j, buf=buf: e.transpose(
                            out=psf(b)[:, j * 128:(j + 1) * 128], in_=buf[:, j * 128:(j + 1) * 128], identity=ident_f),
                            deps=[t_ld, bf, t_cst], ms=(j == 3))
                    tApool.rel(bi, t_mm)
                    if t == 0:
                        dst = hT_h[:, qd * 4:(qd + 1) * 4, :]
                    else:
                        dst = hT_m[:, qd * 4:(qd + 1) * 4, (t - 1) * 128:t * 128]
                    src = psf(b).rearrange("p (j n) -> p j n", j=4)
                    tok = pl.op("dve", lambda e, dst=dst, src=src: e.tensor_copy(out=dst, in_=src), deps=[t_mm])
                    banks.rel(b, tok)
                    n += 1
            state["h_tok"] = tok

        def rstd_group(g):
            n = g.n
            b, bf = banks.get()
            t_mm = None
            for kc in range(KC):
                qi, sq, sqf = sqpool.get()
                sqb = sq.bitcast(BF16)
                t_sq = pl.op("act", lambda e, sqb=sqb, kc=kc: e.activation(out=sqb[:, :n], in_=hT(kc, g), func=AF.Square),
                             deps=[sqf, state["h_tok"]])
                t_mm = pl.op("pe", lambda e, sqb=sqb, kc=kc, b=b: e.matmul(
                    psf(b)[:, :n], lhsT=ones_b[:, :], rhs=sqb[:, :n], start=(kc == 0), stop=(kc == KC - 1)),
                    deps=[t_sq, bf if kc == 0 else None, t_ones], ms=True)
                sqpool.rel(qi, t_mm)
            ri, rs, rsf = rspool.get()
            t_r0 = pl.op("act", lambda e, b=b, rs=rs: e.activation(
                out=rs[:, :n], in_=psf(b)[:, :n], func=AF.Sqrt, bias=epsc, scale=1.0), deps=[t_mm, rsf, t_eps])
            banks.rel(b, t_r0)
            t_r = pl.op("dve", lambda e, rs=rs: e.reciprocal(out=rs[:, :n], in_=rs[:, :n]), deps=[t_r0])
            return ri, rs, t_r

        def rmsnorm(cbase, groups):
            tok = None
            for g in groups:
                n = g.n
                ri, rs, t_r = rstd_group(g)
                for kc in range(KC):
                    tok = pl.op("dve", lambda e, kc=kc, g=g, rs=rs: e.scalar_tensor_tensor(
                        out=xnT(kc, g), in0=hT(kc, g), scalar=cols[:, cbase + kc:cbase + kc + 1], in1=rs[:, :g.n],
                        op0=ALU.mult, op1=ALU.mult), deps=[t_r, t_cols, state["h_tok"]])
                rspool.rel(ri, tok)
                state["xn"][(g.kind, g.t0)] = tok

        def accum(b, g, dc, t_mm, bias_col=None):
            n = g.n
            if bias_col is None:
                tok = pl.op("dve", lambda e: e.tensor_tensor(out=hT(dc, g), in0=psf(b)[:, :n], in1=hT(dc, g), op=ALU.add),
                            deps=[t_mm])
            else:
                tok = pl.op("dve", lambda e: e.scalar_tensor_tensor(
                    out=hT(dc, g), in0=psf(b)[:, :n], scalar=bias_col, in1=hT(dc, g), op0=ALU.add, op1=ALU.add),
                    deps=[t_mm])
            banks.rel(b, tok)
            state["h_tok"] = tok
            return tok

        def phase_ffn(l, groups):
            rmsnorm(C_FFN + 16 * l, groups)
            ntok = NHALO + NMAIN
            hid = [A[:, i * 4 * ntok:(i + 1) * 4 * ntok].rearrange("p (j n) -> p j n", j=4) for i in range(2)]
            sgp = Pool([f32view(A, 2 * 4 * ntok + i * 1024, 1024) for i in range(3)])
            hid_tok = [None, None]
            fen = pl.fence()

            def GU(s):
                hb = hid[s % 2]
                last = None
                for half in range(2):
                    c0 = s * 512 + half * 256
                    sg_, tg_, lg = ring_load(colblock(wg_d[l], c0))
                    su_, tu_, lu = ring_load(colblock(wu_d[l], c0))
                    wg, wu = cview(tg_), cview(tu_)
                    tU = None
                    for j in range(2):
                        jj = half * 2 + j
                        for g in groups:
                            n = g.n
                            bg, fg = banks.get()
                            bu, fu = banks.get()
                            for kc in range(KC):
                                tG = pl.op("pe", lambda e, kc=kc, g=g, bg=bg, j=j, wg=wg: e.matmul(
                                    psf(bg)[:, :g.n], lhsT=wg[:, kc, j * 128:(j + 1) * 128], rhs=xnT(kc, g),
                                    start=(kc == 0), stop=(kc == KC - 1)),
                                    deps=[lg, xn_tok_g(g), fg], ms=(kc == KC - 1))
                            for kc in range(KC):
                                tU = pl.op("pe", lambda e, kc=kc, g=g, bu=bu, j=j, wu=wu: e.matmul(
                                    psf(bu)[:, :g.n], lhsT=wu[:, kc, j * 128:(j + 1) * 128], rhs=xnT(kc, g),
                                    start=(kc == 0), stop=(kc == KC - 1)),
                                    deps=[lu, fu], ms=(kc == KC - 1))
                            si, sgt, sf = sgp.get()
                            t1 = pl.op("act", lambda e, sgt=sgt, bg=bg, n=n: e.activation(
                                out=sgt[:, :n], in_=psf(bg)[:, :n], func=AF.Silu), deps=[tG, sf, fen])
                            banks.rel(bg, t1)
                            t2 = pl.op("dve", lambda e, sgt=sgt, bu=bu, n=n, jj=jj, g=g: e.tensor_tensor(
                                out=hb[:, jj, g.a0:g.a0 + n], in0=psf(bu)[:, :n], in1=sgt[:, :n], op=ALU.mult),
                                deps=[tU, t1, fen])
                            banks.rel(bu, t2)
                            sgp.rel(si, t2)
                            last = t2
                    ring_rel(sg_, tU)
                    ring_rel(su_, tU)
                hid_tok[s % 2] = last

            def DOWN(s):
                hb = hid[s % 2]
                r0 = s * 512
                s0, t0_, l0 = ring_load(rowblock(wd_d[l], r0))
                s1, t1_, l1 = ring_load(rowblock(wd_d[l], r0 + 256))
                wd = [rview(t0_), rview(t1_)]
                tmm = None
                for dc in range(KC):
                    for g in groups:
                        b, bf = banks.get()
                        for jj in range(4):
                            tmm = pl.op("pe", lambda e, jj=jj, g=g, b=b, dc=dc: e.matmul(
                                psf(b)[:, :g.n], lhsT=wd[jj // 2][:, jj % 2, dc * 128:(dc + 1) * 128],
                                rhs=hb[:, jj, g.a0:g.a0 + g.n], start=(jj == 0), stop=(jj == 3)),
                                deps=[l0, l1, hid_tok[s % 2], bf], ms=(jj == 3))
                        accum(b, g, dc, tmm)
                ring_rel(s0, tmm)
                ring_rel(s1, tmm)

            GU(0)
            for s in range(NSC):
                if s + 1 < NSC:
                    GU(s + 1)
                DOWN(s)

        def gelu_from_psum(src, n, hxp, wp, out_fn, extra_deps, sq_on_dve=False):
            hi, hx, hf = hxp.get()
            wi, w, wf = wp.get()
            t_h = pl.op("act", lambda e: e.activation(out=hx[:, :n], in_=src, func=AF.Copy, scale=0.5),
                        deps=[hf] + list(extra_deps))
            if sq_on_dve:
                t_w = pl.op("dve", lambda e: e.scalar_tensor_tensor(
                    out=w[:, :n], in0=hx[:, :n], scalar=4.0 * 0.044715, in1=hx[:, :n], op0=ALU.mult, op1=ALU.mult),
                    deps=[t_h, wf])
            else:
                t_w = pl.op("act", lambda e: e.activation(out=w[:, :n], in_=src, func=AF.Square, scale=math.sqrt(0.044715)),
                            deps=[wf])
            t_t = pl.op("dve", lambda e: e.scalar_tensor_tensor(
                out=w[:, :n], in0=w[:, :n], scalar=1.0, in1=hx[:, :n], op0=ALU.add, op1=ALU.mult), deps=[t_h, t_w])
            t_th = pl.op("act", lambda e: e.activation(out=w[:, :n], in_=w[:, :n], func=AF.Tanh, scale=2.0 * GELU_C),
                         deps=[t_t])
            t_o = out_fn(hx, w, [t_th])
            hxp.rel(hi, t_o)
            wp.rel(wi, t_o)
            return (t_h if sq_on_dve else t_w), t_o

        def phase_gmlp(l):
            i = l
            groups = G01
            rmsnorm(C_MIX + 16 * l, groups)
            fen = pl.fence()
            v = A[:, :].rearrange("p (t d) -> p t d", t=9)
            ntok = NHALO + NMAIN
            gated = [Bq[:, k * 2 * ntok:(k + 1) * 2 * ntok].rearrange("p (j n) -> p j n", j=2) for k in range(2)]
            h256 = [tbuf[3][:, 0:256], tbuf[3][:, 256:512], tbuf[4][:, 0:256]]
            w256 = [tbuf[4][:, 256:512], tbuf[5][:, 0:256], tbuf[5][:, 256:512]]
            hxp, wp = Pool(h256), Pool(w256)
            halves = [tbuf[k][:, c:c + 256] for k in range(6) for c in (0, 256)]
            hxpA, wpA = Pool(halves[0::2]), Pool(halves[1::2])
            u256 = Pool([tbuf[0][:, 0:256], tbuf[0][:, 256:512], tbuf[1][:, 0:256]]
                        + [Bq[:, k * 512:(k + 1) * 512].bitcast(F32) for k in range(9)])
            g256 = Pool([tbuf[1][:, 256:512], tbuf[2][:, 0:256]])
            VtF = Vt[:, :, :].rearrange("p a b -> p (a b)")
            WsT = VtF[:, 0:1024].rearrange("p (g t) -> p g t", g=8)
            bbp = Pool([VtF[:, 1024:1280].bitcast(F32), VtF[:, 1280:1536].bitcast(F32)])
            junk = VtF[:, 1536:1792]
            ssv = small[:, 0:72].rearrange("p (t c) -> p t c", t=9)
            ss9 = small[:, 72:81]
            rsv = small[:, 81:90]

            t_z = pl.op("dve", lambda e: e.memset(ssv, 0.0), deps=[fen])
            t_ws = None
            for hh in range(2):
                ti, tb, tf = sqpool.get()
                t_ld = sp_dma(tb[:, :].rearrange("p (g t) -> p g t", g=4), wst_d[i, :, hh * 4:(hh + 1) * 4, :], deps=[tf, fen])
                for gg in range(4):
                    t_ws = pl.op("dve", lambda e, tb=tb, gg=gg, hh=hh: e.tensor_tensor(
                        out=WsT[:, hh * 4 + gg, :], in0=tb[:, gg * 128:(gg + 1) * 128], in1=triu, op=ALU.mult),
                        deps=[t_ld, t_cst, fen])
                sqpool.rel(ti, t_ws)

            unitsA = [(cb, t) for cb in range(8) for t in range(9)]
            stA = {}
            slotA = [None]
            junk_tok = [None]

            def A1(u):
                cb, t = u
                c = stA.setdefault(u, {})
                if t == 0:
                    sl, tl, ld = ring_load(colblock(win_d[i], D + cb * 256))
                    slotA[0] = (sl, cview(tl), ld)
                sl, wv, ld = slotA[0]
                b, bf = banks.get()
                tmm = None
                for kc in range(KC):
                    tmm = pl.op("pe", lambda e, kc=kc, b=b: e.matmul(
                        psf(b)[:, :256], lhsT=xn_tile(kc, t), rhs=wv[:, kc, :], start=(kc == 0), stop=(kc == KC - 1)),
                        deps=[ld, xn_tok_t(t), bf], ms=(kc == KC - 1))
                if t == 8:
                    ring_rel(sl, tmm)
                hi, hx, hf = hxpA.get()
                t_h = pl.op("act", lambda e: e.activation(out=hx[:, :256], in_=psf(b)[:, :256], func=AF.Copy, scale=0.5),
                            deps=[hf, tmm, t_ws])
                banks.rel(b, t_h)
                c.update(hi=hi, hx=hx, t_h=t_h)

            def A2(u):
                c = stA[u]
                hx = c["hx"]
                wi, w, wf = wpA.get()
                t_w = pl.op("dve", lambda e: e.scalar_tensor_tensor(
                    out=w[:, :256], in0=hx[:, :256], scalar=4.0 * 0.044715, in1=hx[:, :256], op0=ALU.mult, op1=ALU.mult),
                    deps=[c["t_h"], wf, t_ws])
                t_t = pl.op("dve", lambda e: e.scalar_tensor_tensor(
                    out=w[:, :256], in0=w[:, :256], scalar=1.0, in1=hx[:, :256], op0=ALU.add, op1=ALU.mult), deps=[t_w])
                c.update(wi=wi, w=w, t_t=t_t)

            def A3(u):
                c = stA[u]
                w = c["w"]
                c["t_th"] = pl.op("act", lambda e: e.activation(out=w[:, :256], in_=w[:, :256], func=AF.Tanh,
                                                              scale=2.0 * GELU_C), deps=[c["t_t"]])

            def A4(u):
                cb, t = u
                c = stA[u]
                hx, w = c["hx"], c["w"]
                t_v = pl.op("dve", lambda e: e.scalar_tensor_tensor(
                    out=v[:, t, cb * 256:(cb + 1) * 256], in0=w[:, :256], scalar=1.0, in1=hx[:, :256],
                    op0=ALU.add, op1=ALU.mult), deps=[c["t_th"], fen])
                hxpA.rel(c["hi"], t_v)
                wpA.rel(c["wi"], t_v)
                c["t_v"] = t_v

            def A5(u):
                cb, t = u
                c = stA.pop(u)
                junk_tok[0] = pl.op("act", lambda e: e.activation(
                    out=junk, in_=v[:, t, cb * 256:(cb + 1) * 256], func=AF.Square,
                    accum_out=ssv[:, t, cb:cb + 1]), deps=[c["t_v"], t_z, junk_tok[0]])

            stagesA = [A1, A2, A3, A4, A5]
            for step in range(len(unitsA) + len(stagesA) - 1):
                for sidx in reversed(range(len(stagesA))):
                    k = step - sidx
                    if 0 <= k < len(unitsA):
                        stagesA[sidx](unitsA[k])
            t_a = pl.op("dve", lambda e: e.tensor_reduce(out=ss9, in_=ssv, axis=AX.X, op=ALU.add), deps=[pl.last("act")])
            t_b = pl.op("dve", lambda e: e.tensor_scalar(out=ss9, in0=ss9, scalar1=1.0 / D, scalar2=EPS,
                                                       op0=ALU.mult, op1=ALU.add), deps=[t_a])
            t_c0 = pl.op("act", lambda e: e.activation(out=rsv, in_=ss9, func=AF.Sqrt), deps=[t_b])
            t_c = pl.op("dve", lambda e: e.reciprocal(out=rsv, in_=rsv), deps=[t_c0])
            t_vn = None
            for t in range(9):
                t_vn = pl.op("act", lambda e, t=t: e.activation(out=v[:, t, :], in_=v[:, t, :], func=AF.Copy,
                                                              scale=rsv[:, t:t + 1]), deps=[t_c])

            fenB = pl.fence()
            last_g = [None]

            def UG(g8):
                slU, tlU, ldU = ring_load(colblock(win_d[i], g8 * 256))
                wu = cview(tlU)
                bi, bb, bbf = bbp.get()
                t_bb = sp_dma(bb, bs_d[i, g8, :].partition_broadcast(128), deps=[bbf])
                last = None
                tmm = None
                upieces = {}
                for j in range(2):
                    for g in groups:
                        n = g.n
                        b, bf = banks.get()
                        for kc in range(KC):
                            tmm = pl.op("pe", lambda e, kc=kc, g=g, b=b, j=j: e.matmul(
                                psf(b)[:, :g.n], lhsT=wu[:, kc, j * 128:(j + 1) * 128], rhs=xnT(kc, g),
                                start=(kc == 0), stop=(kc == KC - 1)), deps=[ldU, bf], ms=(kc == KC - 1))
                        pieces = [(c, min(256, n - c)) for c in range(0, n, 256)]
                        for pi, (c0, cn) in enumerate(pieces):
                            ui, ut, uf = u256.get()

                            def out_fn(hx, w, deps, ut=ut, cn=cn, uf=uf):
                                return pl.op("dve", lambda e: e.scalar_tensor_tensor(
                                    out=ut[:, :cn], in0=w[:, :cn], scalar=1.0, in1=hx[:, :cn],
                                    op0=ALU.add, op1=ALU.mult), deps=deps + [uf])
                            t_w, t_u = gelu_from_psum(psf(b)[:, c0:c0 + cn], cn, hxp, wp, out_fn, [tmm, fenB])
                            if pi == len(pieces) - 1:
                                banks.rel(b, t_w)
                            upieces[(j, g.a0, pi)] = (ui, ut, t_u)
                for j in range(2):
                    dcg = 2 * g8 + j
                    for g in groups:
                        n = g.n
                        b2, bf2 = banks.get()
                        tmm2 = None
                        for ti, t in enumerate(g.tiles):
                            tmm2 = pl.op("pe", lambda e, ti=ti, t=t, b2=b2, dcg=dcg: e.matmul(
                                psf(b2)[:, ti * 128:(ti + 1) * 128], lhsT=v[:, t, dcg * 128:(dcg + 1) * 128],
                                rhs=WsT[:, g8, :], start=True, stop=True),
                                deps=[t_vn, t_ws, bf2], ms=(ti == len(g.tiles) - 1))
                        pieces = [(c, min(256, n - c)) for c in range(0, n, 256)]
                        for pi, (c0, cn) in enumerate(pieces):
                            ui, ut, t_u = upieces[(j, g.a0, pi)]
                            gi, gt, gf = g256.get()
                            nt = cn // 128
                            t_g1 = pl.op("dve", lambda e, gt=gt, c0=c0, cn=cn, nt=nt, b2=b2, dcg=dcg, bb=bb: e.scalar_tensor_tensor(
                                out=gt[:, :cn].rearrange("p (a t) -> p a t", a=nt),
                                in0=psf(b2)[:, c0:c0 + cn].rearrange("p (a t) -> p a t", a=nt),
                                scalar=cols[:, C_NV + 16 * i + dcg:C_NV + 16 * i + dcg + 1],
                                in1=bb.unsqueeze(1).broadcast_to([128, nt, 128]),
                                op0=ALU.mult, op1=ALU.add), deps=[tmm2, t_bb, gf, t_cols, fenB])
                            if pi == len(pieces) - 1:
                                banks.rel(b2, t_g1)
                            tl0 = g.tiles[c0 // 128]
                            t_g2 = pl.op("dve", lambda e, gt=gt, ut=ut, cn=cn, nt=nt, tl0=tl0, dcg=dcg: e.tensor_tensor(
                                out=v[:, tl0:tl0 + nt, dcg * 128:(dcg + 1) * 128],
                                in0=gt[:, :cn].rearrange("p (a t) -> p a t", a=nt),
                                in1=ut[:, :cn].rearrange("p (a t) -> p a t", a=nt), op=ALU.mult),
                                deps=[t_g1, t_u, tmm2])
                            u256.rel(ui, t_g2)
                            g256.rel(gi, t_g2)
                            last = t_g2
                ring_rel(slU, tmm)
                bbp.rel(bi, last)
                last_g[0] = last

            for g8 in range(8):
                UG(g8)
            for cbk in range(8):
                slO, tlO, ldO = ring_load(colblock(wout_d[i], cbk * 256))
                wo = cview(tlO)
                tmm = None
                for j in range(2):
                    dc = cbk * 2 + j
                    for g in groups:
                        b, bf = banks.get()
                        nt = g.n // 128
                        t0_ = g.tiles[0]
                        for kc in range(KC):
                            tmm = pl.op("pe", lambda e, kc=kc, g=g, b=b, j=j, wo=wo, nt=nt, t0_=t0_: e.matmul(
                                psf(b)[:, :g.n].rearrange("p (a t) -> p a t", a=nt),
                                lhsT=wo[:, kc, j * 128:(j + 1) * 128],
                                rhs=v[:, t0_:t0_ + nt, kc * 128:(kc + 1) * 128],
                                start=(kc == 0), stop=(kc == KC - 1)), deps=[ldO, last_g[0], bf], ms=(kc == KC - 1))
                        accum(b, g, dc, tmm)
                ring_rel(slO, tmm)

        KT = Bq[:, :].rearrange("p (m n) -> p m n", m=4)

        def phase_kv():
            groups = G01
            rmsnorm(C_KV, groups)
            fen = pl.fence()
            bv = tbuf[3][:, 0:256]
            t_bv = sp_dma(bv, bkv_d[256:512].partition_broadcast(128), deps=[fen])
            for half in range(2):
                dm = []
                for h2 in range(2):
                    src = wkv_d[:, (half * 2 + h2) * 64:(half * 2 + h2 + 1) * 64].rearrange("(k p) c -> p k c", p=128)
                    for dup in range(2):
                        dm.append((lambda t, dup=dup, h2=h2: t[:, :].rearrange(
                            "p (k h u c) -> p k h u c", k=KC, h=2, u=2)[:, :, h2, dup, :], src))
                sl, tl, ld = ring_load(dm)
                wk = cview(tl)
                tmm = None
                for hh in range(2):
                    m = half * 2 + hh
                    for g in groups:
                        b, bf = banks.get()
                        for kc in range(KC):
                            tmm = pl.op("pe", lambda e, kc=kc, g=g, b=b, hh=hh, wk=wk: e.matmul(
                                psf(b)[:, :g.n], lhsT=wk[:, kc, hh * 128:(hh + 1) * 128], rhs=xnT(kc, g),
                                start=(kc == 0), stop=(kc == KC - 1)), deps=[ld, xn_tok_g(g), bf], ms=(kc == KC - 1))
                        t_e = pl.op("act", lambda e, g=g, b=b, m=m: e.activation(
                            out=KT[:, m, g.a0:g.a0 + g.n], in_=psf(b)[:, :g.n], func=AF.Identity,
                            bias=cols[:, C_BK + m:C_BK + m + 1], scale=1.0), deps=[tmm, t_cols, fen])
                        banks.rel(b, t_e)
                ring_rel(sl, tmm)
            sl, tl, ld = ring_load(colblock(wkv_d, 256))
            wv = cview(tl)
            tmm = None
            for t in range(9):
                b, bf = banks.get()
                for kc in range(KC):
                    tmm = pl.op("pe", lambda e, kc=kc, t=t, b=b: e.matmul(
                        psf(b)[:, :256], lhsT=xn_tile(kc, t), rhs=wv[:, kc, :], start=(kc == 0), stop=(kc == KC - 1)),
                        deps=[ld, bf], ms=(kc == KC - 1))
                t_e = pl.op("dve", lambda e, t=t, b=b: e.tensor_tensor(out=Vt[:, t, :], in0=psf(b)[:, :256], in1=bv, op=ALU.add),
                            deps=[tmm, t_bv, fen])
                banks.rel(b, t_e)
            ring_rel(sl, tmm)

        def phase_attn(l):
            i = l - 2
            groups = G23
            rmsnorm(C_MIX + 16 * l, groups)
            fen = pl.fence()
            attnT = A[:, 0:16384].rearrange("p (k n) -> p k n", k=KC)
            qTz = [A[:, 16384:17408], A[:, 17408:18432]]
            t_qz = pl.op("dve", lambda e: e.memset(A[:, 16384:18432], 0.0), deps=[fen])
            hh_f = hT_h[:, :, :].rearrange("p k n -> p (k n)")
            braw = hh_f[:, 0:512].rearrange("p (e k) -> p e k", e=2)
            biasp = [hh_f[:, 512:768].bitcast(BF16).rearrange("p (e k) -> p e k", e=2),
                     hh_f[:, 1024:1280].bitcast(BF16).rearrange("p (e k) -> p e k", e=2)]
            bias0 = hh_f[:, 1536:1792].bitcast(BF16).rearrange("p (e k) -> p e k", e=2)
            scp = Pool([tbuf[3][:, :].rearrange("p (e k) -> p e k", e=2), tbuf[4][:, :].rearrange("p (e k) -> p e k", e=2)])
            xh_b = xn_h[:, :, :].rearrange("p k n -> p (k n)")
            pp = Pool([xh_b[:, k * 512:(k + 1) * 512].rearrange("p (e k) -> p e k", e=2) for k in range(2)])
            pTp = Pool([xh_b[:, 1024 + k * 512:1024 + (k + 1) * 512].rearrange("p (a q) -> p a q", a=4) for k in range(2)])
            t5b = tbuf[5][:, :].bitcast(BF16)
            op_ = Pool([t5b[:, k * 128:(k + 1) * 128] for k in range(3)])
            sink_bc = small[:, 96:128]
            mask0 = tbuf[2][:, 0:256]
            t_sk = sp_dma(sink_bc, sink_d[i, :].partition_broadcast(128), deps=[fen])
            t_m0 = sp_dma(mask0, mask0_d[:, :], deps=[fen])
            sm = small[:, 128:256]
            smp = Pool([sm[:, k * 16:(k + 1) * 16] for k in range(8)])
            poolS, poolT, poolO, poolX = Banks([0, 1, 2]), Banks([3, 4]), Banks([5, 6]), Banks([7])
            bias_tok = [None, None]
            bias0_tok = [None]
            braw_free = [None]
            stt = {}
            wq_slot = [None]

            def SA(it):
                hp, qb = it
                c = stt.setdefault(it, {})
                if qb == 0:
                    if hp % 2 == 0:
                        if wq_slot[0] is not None:
                            ring_rel(wq_slot[0][0], wq_slot[0][3])
                        sl, tl, ld = ring_load(colblock(wq_d[i], (hp // 2) * 256))
                        wq_slot[0] = [sl, cview(tl), ld, None]
                    wq = wq_slot[0][1]
                    ld = wq_slot[0][2]
                    t_q = None
                    for g in groups:
                        b, bf = poolS.get()
                        tmm = None
                        for kc in range(KC):
                            tmm = pl.op("pe", lambda e, kc=kc, g=g, b=b: e.matmul(
                                psf(b)[:, :g.n], lhsT=wq[:, kc, (hp % 2) * 128:(hp % 2 + 1) * 128], rhs=xnT(kc, g),
                                start=(kc == 0), stop=(kc == KC - 1)), deps=[ld, xn_tok_g(g), bf], ms=(kc == KC - 1))
                        wq_slot[0][3] = tmm
                        for e2 in range(2):
                            t_q = pl.op("act", lambda e, g=g, b=b, e2=e2: e.activation(
                                out=qTz[e2][e2 * 64:(e2 + 1) * 64, g.t0:g.t0 + g.n], in_=psf(b)[e2 * 64:(e2 + 1) * 64, :g.n],
                                func=AF.Identity, bias=bq8[e2 * 64:(e2 + 1) * 64, 16 * i + hp:16 * i + hp + 1],
                                scale=0.125), deps=[tmm, t_bq8, fen, t_qz])
                        Banks.rel(b, t_q)
                    stt["q_tok"] = t_q
                    t_ld = sp_dma(braw, biasg_d[:, 2 * hp:2 * hp + 2, :], deps=[braw_free[0], fen])
                    t_b0 = pl.op("dve", lambda e: e.tensor_tensor(
                        out=bias0, in0=braw, in1=mask0.unsqueeze(1).broadcast_to([128, 2, 256]), op=ALU.add),
                        deps=[t_ld, t_m0, bias0_tok[0]])
                    bp = biasp[hp % 2]
                    t_b1 = pl.op("dve", lambda e: e.tensor_tensor(
                        out=bp, in0=braw, in1=maskadd.unsqueeze(1).broadcast_to([128, 2, 256]), op=ALU.add),
                        deps=[t_ld, t_cst, bias_tok[hp % 2]])
                    braw_free[0] = t_b1
                    stt["b0"] = t_b0
                    stt["b1"] = t_b1
                c["b0"], c["b1"] = stt["b0"], stt["b1"]
                kvh = hp // 4
                b, bf = poolS.get()
                tmm = None
                for e2 in range(2):
                    pl.op("pe", lambda e, e2=e2, b=b: e.matmul(
                        psf(b)[:, e2 * 256:(e2 + 1) * 256], lhsT=qTz[e2][:, qb * 128:(qb + 1) * 128],
                        rhs=KT[:, kvh, qb * 128:qb * 128 + 256], start=(e2 == 0), stop=False),
                        deps=[stt["q_tok"], bf, fen], ms=False)
                bsel = bias0 if qb == 0 else biasp[hp % 2]
                tmm = pl.op("pe", lambda e, b=b: e.matmul(
                    psf(b)[:, :].rearrange("p (e k) -> p e k", e=2), lhsT=ident_b[:, :], rhs=bsel, start=False, stop=True),
                    deps=[c["b0"], c["b1"], t_idb], ms=True)
                if qb == 0:
                    bias0_tok[0] = tmm
                bias_tok[hp % 2] = tmm
                c["bS"], c["tS"] = b, tmm

            def SB(it):
                hp, qb = it
                c = stt[it]
                b = c["bS"]
                sc = psf(b)[:, :].rearrange("p (e k) -> p e k", e=2)
                t_sc = c["tS"]
                mi, ms_, mf = smp.get()
                mx, negm, dd, rs, den, rden = (ms_[:, 0:2], ms_[:, 2:4], ms_[:, 4:6], ms_[:, 6:8], ms_[:, 8:10], ms_[:, 10:12])
                sk = sink_bc[:, 2 * hp:2 * hp + 2]
                t1 = pl.op("dve", lambda e: e.tensor_reduce(out=mx, in_=sc, axis=AX.X, op=ALU.max), deps=[t_sc, mf])
                t2 = pl.op("dve", lambda e: e.tensor_tensor(out=mx, in0=mx, in1=sk, op=ALU.max), deps=[t1, t_sk])
                t3 = pl.op("dve", lambda e: e.tensor_scalar(out=negm, in0=mx, scalar1=-1.0, scalar2=None, op0=ALU.mult), deps=[t2])
                t4 = pl.op("dve", lambda e: e.tensor_tensor(out=dd, in0=sk, in1=negm, op=ALU.add), deps=[t3])
                t_z = pl.op("dve", lambda e: e.memset(rs, 0.0), deps=[mf])
                c.update(sc=sc, mi=mi, negm=negm, dd=dd, rs=rs, den=den, rden=rden, tB=[t3, t4, t_z])

            def SC(it):
                c = stt[it]
                sc, negm, rs, dd = c["sc"], c["negm"], c["rs"], c["dd"]
                pi, p, pf = pp.get()
                t_e = None
                for e2 in range(2):
                    t_e = pl.op("act", lambda e, e2=e2: e.activation(
                        out=p[:, e2, :], in_=sc[:, e2, :], func=AF.Exp, bias=negm[:, e2:e2 + 1], scale=1.0,
                        accum_out=rs[:, e2:e2 + 1]), deps=c["tB"] + [pf])
                Banks.rel(c["bS"], t_e)
                t5 = pl.op("act", lambda e: e.activation(out=dd, in_=dd, func=AF.Exp), deps=c["tB"])
                c.update(pi=pi, p=p, tC=t5, tE=t_e)

            def SD(it):
                c = stt[it]
                p, den, rden, rs, dd = c["p"], c["den"], c["rden"], c["rs"], c["dd"]
                bT, bfT = poolT.get()
                tmm = None
                for e2 in range(2):
                    for kb in range(2):
                        a = e2 * 2 + kb
                        tmm = pl.op("pe", lambda e, a=a, e2=e2, kb=kb: e.transpose(
                            out=psb(bT)[:, a * 128:(a + 1) * 128], in_=p[:, e2, kb * 128:(kb + 1) * 128], identity=ident_b[:, :]),
                            deps=[c["tE"], bfT, t_idb], ms=(a == 3))
                pp.rel(c["pi"], tmm)
                c["bT"], c["tT"] = bT, tmm
                t6 = pl.op("dve", lambda e: e.tensor_tensor(out=den, in0=rs, in1=dd, op=ALU.add), deps=[c["tC"], c["tE"]])
                t7 = pl.op("dve", lambda e: e.reciprocal(out=rden, in_=den), deps=[t6])
                c["t_rden"] = t7

            def SE(it):
                c = stt[it]
                bT = c["bT"]
                ti, pT, tf = pTp.get()
                if (it[0] * 8 + it[1]) % 2 == 0:
                    t_c = pl.op("act", lambda e: e.activation(
                        out=pT, in_=psb(bT)[:, 0:512].rearrange("p (a q) -> p a q", a=4), func=AF.Copy), deps=[c["tT"], tf])
                else:
                    t_c = pl.op("dve", lambda e: e.tensor_copy(
                        out=pT, in_=psb(bT)[:, 0:512].rearrange("p (a q) -> p a q", a=4)), deps=[c["tT"], tf])
                Banks.rel(bT, t_c)
                c.update(ti=ti, pT=pT, tPT=t_c)

            def SF(it):
                hp, qb = it
                c = stt[it]
                kvh = hp // 4
                pT = c["pT"]
                bO, bfO = poolO.get()
                tmm = None
                for e2 in range(2):
                    for kb in range(2):
                        tmm = pl.op("pe", lambda e, e2=e2, kb=kb: e.matmul(
                            psf(bO)[:, e2 * 64:(e2 + 1) * 64], lhsT=pT[:, e2 * 2 + kb, :],
                            rhs=Vt[:, qb + kb, kvh * 64:(kvh + 1) * 64], start=(kb == 0), stop=(kb == 1)),
                            deps=[c["tPT"], bfO, fen], ms=(e2 == 1 and kb == 1))
                pTp.rel(c["ti"], tmm)
                c["bO"], c["tO"] = bO, tmm

            def SG(it):
                c = stt[it]
                bO = c["bO"]
                oi, o, of = op_.get()
                t_o = pl.op("dve", lambda e: e.tensor_tensor(
                    out=o.rearrange("p (e d) -> p e d", e=2), in0=psf(bO)[:, 0:128].rearrange("p (e d) -> p e d", e=2),
                    in1=c["rden"].unsqueeze(2).broadcast_to([128, 2, 64]), op=ALU.mult), deps=[c["tO"], c["t_rden"], of])
                Banks.rel(bO, t_o)
                smp.rel(c["mi"], t_o)
                c.update(oi=oi, o=o, t_o=t_o)

            def SH(it):
                c = stt[it]
                o = c["o"]
                bX, bfX = poolX.get()
                tmm = pl.op("pe", lambda e: e.transpose(out=psb(bX)[:, 0:128], in_=o, identity=ident_b[:, :]),
                            deps=[c["t_o"], bfX], ms=True)
                op_.rel(c["oi"], tmm)
                c["bX"], c["tX"] = bX, tmm

            def SI(it):
                hp, qb = it
                c = stt[it]
                bX = c["bX"]
                t_a = pl.op("act", lambda e: e.activation(out=attnT[:, hp, qb * 128:(qb + 1) * 128], in_=psb(bX)[:, 0:128],
                                                        func=AF.Copy), deps=[c["tX"], fen])
                Banks.rel(bX, t_a)
                stt["attn_tok"] = t_a
                del stt[it]

            its = [(hp, qb) for hp in range(16) for qb in range(8)]
            stages = [SA, SB, SC, SD, SE, SF, SG, SH, SI]
            for k in range(1, 9):
                if f"st{k}" in dbg:
                    stages = stages[:k]
                    its = its[:16]
            for step in range(len(its) + len(stages) - 1):
                for sidx in reversed(range(len(stages))):
                    k = step - sidx
                    if 0 <= k < len(its):
                        stages[sidx](its[k])
            ring_rel(wq_slot[0][0], wq_slot[0][3])
            if len(stages) < 9:
                return
            for cbk in range(8):
                sl, tl, ld = ring_load(colblock(wo_d[i], cbk * 256))
                wo = cview(tl)
                tmm = None
                for j in range(2):
                    dc = cbk * 2 + j
                    for g in groups:
                        b, bf = banks.get()
                        for kc in range(KC):
                            tmm = pl.op("pe", lambda e, kc=kc, g=g, b=b, j=j, wo=wo: e.matmul(
                                psf(b)[:, :g.n], lhsT=wo[:, kc, j * 128:(j + 1) * 128], rhs=attnT[:, kc, g.t0:g.t0 + g.n],
                                start=(kc == 0), stop=(kc == KC - 1)), deps=[ld, stt["attn_tok"], bf], ms=(kc == KC - 1))
                        accum(b, g, dc, tmm, bias_col=cols[:, C_BO + 16 * i + dc:C_BO + 16 * i + dc + 1])
                ring_rel(sl, tmm)

        st_sems = [pl.new_sem("st0"), pl.new_sem("st1")]
        n_st = [0, 0]

        def phase_out(final):
            groups = G23 if final else G01
            fen = pl.fence()
            fT = [f32view(A, k * 1024, 1024) for k in range(4)]
            ostp = Pool([f32view(A, (4 + k) * 1024, 1024) for k in range(2)])
            fT_free = [None] * 4
            row0 = 0 if final else None
            for g in groups:
                n = g.n
                if final:
                    ri, rs, t_r = rstd_group(g)
                for kq in range(4):
                    srcs = []
                    if final:
                        for j in range(4):
                            kc = kq * 4 + j
                            t_f = pl.op("dve", lambda e, kc=kc, j=j, g=g, rs=rs: e.scalar_tensor_tensor(
                                out=fT[j][:, :g.n], in0=hT(kc, g), scalar=cols[:, C_FIN + kc:C_FIN + kc + 1], in1=rs[:, :g.n],
                                op0=ALU.mult, op1=ALU.mult), deps=[t_r, fT_free[j], fen, t_cols])
                            srcs.append((fT[j], 0, t_f))
                    else:
                        for j in range(4):
                            kc = kq * 4 + j
                            base = hT_h if g.kind == "h" else hT_m
                            srcs.append((base[:, kc, :], g.t0, state["h_tok"]))
                    for ti in range(n // 128):
                        b, bf = banks.get()
                        tmm = None
                        for j in range(4):
                            s_ap, off, s_tok = srcs[j]
                            tmm = pl.op("pe", lambda e, s_ap=s_ap, off=off, j=j, ti=ti, b=b: e.transpose(
                                out=psf(b)[:, j * 128:(j + 1) * 128], in_=s_ap[:, off + ti * 128:off + (ti + 1) * 128],
                                identity=ident_f), deps=[s_tok, bf, t_cst], ms=(j == 3))
                        oi, ost, of = ostp.get()
                        t_c = pl.op("act", lambda e, ost=ost, b=b: e.activation(out=ost[:, :], in_=psf(b), func=AF.Copy),
                                    deps=[tmm, of, fen])
                        banks.rel(b, t_c)
                        if final:
                            r = g.t0 + ti * 128
                        else:
                            r = g.a0 + ti * 128
                        pl.op("sp", lambda e, ost=ost, r=r, kq=kq: e.dma_start(
                            out=out[r:r + 128, kq * 512:(kq + 1) * 512], in_=ost[:, :]), deps=[t_c], inc=(st_sems[oi], 16))
                        n_st[oi] += 1
                        ostp.rel(oi, (st_sems[oi], 16 * n_st[oi]))
                    if final:
                        for j in range(4):
                            fT_free[j] = tmm
                if final:
                    rspool.rel(ri, pl.last("dve"))
            pl.wait("sp", (st_sems[0], 16 * n_st[0]))
            pl.wait("sp", (st_sems[1], 16 * n_st[1]))

        phase_load()
        for l in layers:
            if l < 2:
                phase_gmlp(l)
                phase_ffn(l, G01)
            else:
                if l == 2:
                    phase_kv()
                if "kvonly" not in dbg:
                    phase_attn(l)
                if "noffn" not in dbg:
                    phase_ffn(l, G23)
        phase_out(is_last)

        with nc.Block() as block:
            @block.tensor
            def _(e):
                pl.replay("pe", e)

            @block.scalar
            def _(e):
                pl.replay("act", e)

            @block.vector
            def _(e):
                pl.replay("dve", e)

            @block.gpsimd
            def _(e):
                pl.replay("pool", e)

            @block.sync
            def _(e):
                pl.replay("sp", e)
    return nc


def _colize(v):
    return np.ascontiguousarray(np.asarray(v, np.float32).reshape(-1, 128).T)


def _t5_bucket(dist):
    max_exact = 16
    d = np.maximum(dist, 1).astype(np.float32)
    large = max_exact + (np.log(d / np.float32(max_exact)) / np.float32(math.log(128 / max_exact))
                         * np.float32(32 - max_exact)).astype(np.int32)
    large = np.minimum(large, 31)
    return np.where(dist < max_exact, dist, large)


def _host_tables(inp):
    cols = np.zeros((128, NCOL), np.float32)
    for l in range(4):
        cols[:, C_MIX + 16 * l:C_MIX + 16 * (l + 1)] = _colize(inp["mix_norm"][l])
        cols[:, C_FFN + 16 * l:C_FFN + 16 * (l + 1)] = _colize(inp["ffn_norm"][l])
    cols[:, C_KV:C_KV + 16] = _colize(inp["kv_norm"])
    cols[:, C_FIN:C_FIN + 16] = _colize(inp["final_norm"])
    for i in range(2):
        cols[:, C_NV + 16 * i:C_NV + 16 * (i + 1)] = _colize(inp["a_norm_v"][i])
        cols[:, C_BQ + 16 * i:C_BQ + 16 * (i + 1)] = _colize(inp["b_b_q"][i])
        cols[:, C_BO + 16 * i:C_BO + 16 * (i + 1)] = _colize(inp["b_b_o"][i])
    bk = np.asarray(inp["b_kv"], np.float32)[:256].reshape(4, 64)
    cols[:, C_BK:C_BK + 4] = np.concatenate([bk, bk], axis=1).T
    cst = np.zeros((128, 512), np.float32)
    cst[:, 0:128] = np.eye(128, dtype=np.float32)
    s = np.arange(128)[:, None]
    t = np.arange(128)[None, :]
    cst[:, 128:256] = (s <= t).astype(np.float32)
    dist = np.arange(128)[:, None] + 128 - np.arange(256)[None, :]
    in_window = (dist >= 0) & (dist < 128)
    cst[:, 256:512] = np.where(in_window, 0.0, MASKV).astype(np.float32)
    mask0_first = np.where(in_window & (np.arange(256)[None, :] >= 128), 0.0, MASKV).astype(np.float32)
    mask0_second = cst[:, 256:512].copy()
    bucket = _t5_bucket(np.clip(dist, 0, None).astype(np.int32))
    rel = np.asarray(inp["rel_bias"], np.float32)
    biasg = np.ascontiguousarray(rel[bucket].transpose(0, 2, 1))
    return cols, cst, mask0_first, mask0_second, biasg


def _core_x(x, c):
    b, half = c // 2, c % 2
    xm = x[b, half * NMAIN:(half + 1) * NMAIN]
    if half == 0:
        halo = np.zeros((NHALO, D), np.float32)
    else:
        halo = x[b, NMAIN - NHALO:NMAIN]
    return np.ascontiguousarray(np.concatenate([halo, xm], axis=0))


_NC_CACHE = {}


def _get_nc(layers, is_first, is_last):
    key = (tuple(layers), is_first, is_last)
    if key not in _NC_CACHE:
        _NC_CACHE[key] = build(layers, is_first, is_last)
    return _NC_CACHE[key]


def _in_map(inp, tabs, xin, c, layers):
    cols, cst, m0f, m0s, biasg = tabs
    m = {"xin": xin, "cols": cols, "cst": cst,
         "ffn_w_gate": inp["ffn_w_gate"], "ffn_w_up": inp["ffn_w_up"], "ffn_w_down": inp["ffn_w_down"]}
    if any(l < 2 for l in layers):
        m["a_w_in"] = inp["a_w_in"]
        m["a_w_sT"] = inp["_a_w_sT"]
        m["a_b_s"] = inp["a_b_s"]
        m["a_w_out"] = inp["a_w_out"]
    if any(l >= 2 for l in layers):
        m["w_kv"] = inp["w_kv"]
        m["b_kv"] = inp["b_kv"]
        m["b_w_q"] = inp["b_w_q"]
        m["b_w_o"] = inp["b_w_o"]
        m["b_sinks"] = inp["b_sinks"]
        m["biasg"] = biasg
        m["mask0"] = m0f if c % 2 == 0 else m0s
    return m


LAUNCHES = [((0, 1, 2, 3), True, True)]


def kernel(**inputs):
    inp = {k: np.ascontiguousarray(np.asarray(v, np.float32)) for k, v in inputs.items()}
    inp["_a_w_sT"] = np.ascontiguousarray(inp["a_w_s"].transpose(0, 3, 1, 2))
    tabs = _host_tables(inp)
    ncores = 8
    cur = [_core_x(inp["x"], c) for c in range(ncores)]
    for layers, is_first, is_last in LAUNCHES:
        nc = _get_nc(layers, is_first, is_last)
        in_maps = [_in_map(inp, tabs, cur[c], c, layers) for c in range(ncores)]
        res = run_bass_kernel_spmd(nc, in_maps, core_ids=list(range(ncores)))
        cur = [np.asarray(res.results[c]["out"], np.float32) for c in range(ncores)]
    outp = np.zeros((4, 2 * NMAIN, D), np.float32)
    for c in range(ncores):
        outp[c // 2, (c % 2) * NMAIN:(c % 2 + 1) * NMAIN] = cur[c]
    return outp
```
